# Optimizing a Trainium2 kernel written in Bass

```python
import jax, jax.numpy as jnp
from jax import lax
import numpy as np

D_MODEL = 1024
BATCH = 8
SEQ = 8192
DEPTH = 1

CTX_LEN = 256
GRID_W = 64
D_FF = 2816
N_MOD = 9
EPS = 1e-6
GDN_HEADS = D_MODEL // 128
GDN_HEAD_DIM = 128
GDN_WIDTH = GDN_HEADS * GDN_HEAD_DIM
CONV_K = 5
CHUNK = 64
POOL_WINDOWS = (2, 4, 8, 16)
POOL_GROUPS = len(POOL_WINDOWS)
POOL_WIDTH = D_MODEL // 2
POOL_GROUP_DIM = POOL_WIDTH // POOL_GROUPS
N_BRANCH = 2
QKV_END = 3 * GDN_WIDTH
AB_END = QKV_END + 4 * GDN_HEADS
GATE_END = AB_END + GDN_WIDTH
POOL_END = GATE_END + POOL_WIDTH
MIX_IN = POOL_END + N_BRANCH * D_MODEL

kernel_name = "hybrid_pool_gdn_macaron_dit"


def rmsnorm(x, w):
    xf = x.astype(jnp.float32)
    y = xf * lax.rsqrt(jnp.mean(xf * xf, axis=-1, keepdims=True) + EPS)
    return (y * w.astype(jnp.float32)).astype(x.dtype)


def modulate(n, shift, scale):
    return n * (1 + scale) + shift


def adaln(cvec, w, b):
    return jnp.split(jax.nn.silu(cvec) @ w + b, N_MOD, axis=-1)


def swiglu(h, w_in, w_out):
    g, u = jnp.split(h @ w_in, 2, axis=-1)
    return (jax.nn.silu(g) * u) @ w_out


def short_conv(x, w):
    pad = CONV_K // 2
    L = x.shape[1]
    xp = jnp.pad(x, ((0, 0), (pad, pad), (0, 0)))
    y = sum(xp[:, k:k + L] * w[k] for k in range(CONV_K))
    return jax.nn.silu(y)


def l2norm(x):
    return x * lax.rsqrt(jnp.sum(x * x, axis=-1, keepdims=True) + EPS)


def gdn_inputs(p, conv_w, a_log, dt_bias):
    B, L, _ = p.shape
    qkv = short_conv(p[..., :QKV_END], conv_w).astype(jnp.float32)
    q, k, v = [t.reshape(B, L, GDN_HEADS, GDN_HEAD_DIM) for t in jnp.split(qkv, 3, axis=-1)]
    q = l2norm(q) * GDN_HEAD_DIM ** -0.5
    k = l2norm(k)
    ab = p[..., QKV_END:AB_END].astype(jnp.float32).reshape(B, L, 4, GDN_HEADS)
    beta = jax.nn.sigmoid(ab[:, :, 0:2])
    g = -jnp.exp(a_log.astype(jnp.float32)) * jax.nn.softplus(ab[:, :, 2:4] + dt_bias.astype(jnp.float32))
    return q, k, v, beta, g


def _to_chunks(t, n):
    b, _, h = t.shape[:3]
    t = t.reshape((b, n, CHUNK, h) + t.shape[3:])
    return jnp.transpose(t, (1, 0, 3, 2) + tuple(range(4, t.ndim)))


def gated_delta(q, k, v, beta, g, S0):
    B, L, H, _ = q.shape
    dv = v.shape[-1]
    n = L // CHUNK
    qc, kc, vc = _to_chunks(q, n), _to_chunks(k, n), _to_chunks(v, n)
    bc, gc = _to_chunks(beta, n), _to_chunks(g, n)
    cum = jnp.cumsum(gc, axis=-1)
    idx = jnp.arange(CHUNK)
    incl = idx[:, None] >= idx[None, :]
    strict = idx[:, None] > idx[None, :]
    diff = cum[..., :, None] - cum[..., None, :]
    decay = jnp.where(incl, jnp.exp(jnp.where(incl, diff, 0.0)), 0.0)
    kb = kc * bc[..., None]
    vb = vc * bc[..., None]
    lmat = jnp.where(strict, jnp.einsum('nbhid,nbhjd->nbhij', kb, kc) * decay, 0.0)
    a_mat = lmat + jnp.eye(CHUNK, dtype=jnp.float32)
    rhs = jnp.concatenate([vb, kb * jnp.exp(cum)[..., None]], axis=-1)
    sol = lax.linalg.triangular_solve(a_mat, rhs, left_side=True, lower=True, unit_diagonal=True)
    u, w = sol[..., :dv], sol[..., dv:]
    aqk = jnp.einsum('nbhid,nbhjd->nbhij', qc, kc) * decay
    qd = qc * jnp.exp(cum)[..., None]
    kd = kc * jnp.exp(cum[..., -1:] - cum)[..., None]
    blast = jnp.exp(cum[..., -1])

    def step(S, xs):
        u_n, w_n, qd_n, kd_n, aqk_n, bl_n = xs
        v_new = u_n - jnp.einsum('bhcd,bhde->bhce', w_n, S)
        o = jnp.einsum('bhcd,bhde->bhce', qd_n, S) + jnp.einsum('bhij,bhje->bhie', aqk_n, v_new)
        S = S * bl_n[..., None, None] + jnp.einsum('bhcd,bhce->bhde', kd_n, v_new)
        return S, o

    S_fin, o = lax.scan(step, S0, (u, w, qd, kd, aqk, blast))
    o = jnp.transpose(o, (1, 0, 3, 2, 4)).reshape(B, L, H, dv)
    return o, S_fin


def bidir_gdn(lat, ctx):
    qx, kx, vx, bx, gx = lat
    qc, kc, vc, bcx, gcx = ctx
    B = qx.shape[0]
    S0 = jnp.zeros((B, GDN_HEADS, GDN_HEAD_DIM, GDN_HEAD_DIM), jnp.float32)
    fl = lambda t: jnp.flip(t, axis=1)
    oc_f, Sc_f = gated_delta(qc, kc, vc, bcx[:, :, 0], gcx[:, :, 0], S0)
    ox_f, _ = gated_delta(qx, kx, vx, bx[:, :, 0], gx[:, :, 0], Sc_f)
    oc_b, Sc_b = gated_delta(fl(qc), fl(kc), fl(vc), fl(bcx[:, :, 1]), fl(gcx[:, :, 1]), S0)
    ox_b, _ = gated_delta(fl(qx), fl(kx), fl(vx), fl(bx[:, :, 1]), fl(gx[:, :, 1]), Sc_b)
    return ox_f + fl(ox_b), oc_f + fl(oc_b)


def _bounds(n):
    t = jnp.arange(n)
    lo = jnp.stack([jnp.clip(t - w // 2, 0, n) for w in POOL_WINDOWS], axis=-1)
    hi = jnp.stack([jnp.clip(t + w - w // 2, 0, n) for w in POOL_WINDOWS], axis=-1)
    return lo, hi


def pool_grid(u):
    B, L, _ = u.shape
    R = L // GRID_W
    xg = u.astype(jnp.float32).reshape(B, R, GRID_W, POOL_GROUPS, POOL_GROUP_DIM)
    S = jnp.pad(jnp.cumsum(jnp.cumsum(xg, axis=1), axis=2), ((0, 0), (1, 0), (1, 0), (0, 0), (0, 0)))
    rlo, rhi = _bounds(R)
    clo, chi = _bounds(GRID_W)
    gi = jnp.arange(POOL_GROUPS)

    def corner(ri, ci):
        return S[:, ri[:, None, :], ci[None, :, :], gi[None, None, :], :]

    total = corner(rhi, chi) - corner(rlo, chi) - corner(rhi, clo) + corner(rlo, clo)
    area = ((rhi - rlo)[:, None, :] * (chi - clo)[None, :, :]).astype(jnp.float32)[..., None]
    return (total / area - xg).reshape(B, L, POOL_GROUPS, POOL_GROUP_DIM)


def pool_seq(u):
    B, L, _ = u.shape
    xg = u.astype(jnp.float32).reshape(B, L, POOL_GROUPS, POOL_GROUP_DIM)
    S = jnp.pad(jnp.cumsum(xg, axis=1), ((0, 0), (1, 0), (0, 0), (0, 0)))
    lo, hi = _bounds(L)
    gi = jnp.arange(POOL_GROUPS)[None, :]
    total = S[:, hi, gi, :] - S[:, lo, gi, :]
    count = (hi - lo).astype(jnp.float32)[..., None]
    return total / count - xg


def merge_branches(p, pool_diff, o, pool_w, pool_scale, gdn_norm_w, w_gdn_proj, w_pool_proj, w_mix_out):
    B, L, _ = p.shape
    dt = p.dtype
    gate = p[..., AB_END:GATE_END].reshape(B, L, GDN_HEADS, GDN_HEAD_DIM)
    o = rmsnorm(o.astype(dt), gdn_norm_w) * jax.nn.silu(gate)
    y_gdn = o.reshape(B, L, GDN_WIDTH) @ w_gdn_proj
    y_pool = jnp.einsum('blgc,gce->blge', pool_diff, pool_w.astype(jnp.float32)).reshape(B, L, POOL_WIDTH)
    y_pool = (y_pool * pool_scale.astype(jnp.float32)).astype(dt) @ w_pool_proj
    g_pool, g_gdn = jnp.split(jax.nn.sigmoid(p[..., POOL_END:]), N_BRANCH, axis=-1)
    return (g_pool * y_pool + g_gdn * y_gdn) @ w_mix_out


def setup_inputs(seed: int = 0) -> dict:
    key = jax.random.key(seed)
    ks = jax.random.split(key, 24)
    f32 = jnp.float32

    def nrm(k, shape, fan_in):
        return jax.random.normal(k, shape, f32) * fan_in ** -0.5

    def gain(k, shape):
        return 1.0 + 0.02 * jax.random.normal(k, shape, f32)

    dt = jnp.exp(jax.random.uniform(ks[12], (DEPTH, 2, GDN_HEADS), f32, np.log(1e-3), np.log(1e-1)))
    return {
        "x": jax.random.normal(ks[0], (BATCH, SEQ, D_MODEL), f32),
        "c": jax.random.normal(ks[1], (BATCH, D_MODEL), f32),
        "ctx": jax.random.normal(ks[2], (BATCH, CTX_LEN, D_MODEL), f32),
        "c_ctx": jax.random.normal(ks[3], (D_MODEL,), f32),
        "w_ada": nrm(ks[4], (DEPTH, D_MODEL, N_MOD * D_MODEL), D_MODEL),
        "b_ada": 0.02 * jax.random.normal(ks[5], (DEPTH, N_MOD * D_MODEL), f32),
        "norm1_w": gain(ks[6], (DEPTH, D_MODEL)),
        "ffn1_w_in": nrm(ks[7], (DEPTH, D_MODEL, 2 * D_FF), D_MODEL),
        "ffn1_w_out": nrm(ks[8], (DEPTH, D_FF, D_MODEL), D_FF),
        "norm2_w": gain(ks[9], (DEPTH, D_MODEL)),
        "w_mix_in": nrm(ks[10], (DEPTH, D_MODEL, MIX_IN), D_MODEL),
        "conv_w": nrm(ks[11], (DEPTH, CONV_K, QKV_END), CONV_K),
        "a_log": jnp.log(jax.random.uniform(ks[13], (DEPTH, 2, GDN_HEADS), f32, 1.0, 16.0)),
        "dt_bias": dt + jnp.log(-jnp.expm1(-dt)),
        "gdn_norm_w": gain(ks[14], (DEPTH, GDN_HEAD_DIM)),
        "w_gdn_proj": nrm(ks[15], (DEPTH, GDN_WIDTH, D_MODEL), GDN_WIDTH),
        "pool_w": nrm(ks[16], (DEPTH, POOL_GROUPS, POOL_GROUP_DIM, POOL_GROUP_DIM), POOL_GROUP_DIM),
        "pool_scale": gain(ks[17], (DEPTH, POOL_WIDTH)),
        "w_pool_proj": nrm(ks[18], (DEPTH, POOL_WIDTH, D_MODEL), POOL_WIDTH),
        "w_mix_out": nrm(ks[19], (DEPTH, D_MODEL, D_MODEL), D_MODEL),
        "norm3_w": gain(ks[20], (DEPTH, D_MODEL)),
        "ffn2_w_in": nrm(ks[21], (DEPTH, D_MODEL, 2 * D_FF), D_MODEL),
        "ffn2_w_out": nrm(ks[22], (DEPTH, D_FF, D_MODEL), D_FF),
        "final_norm_w": gain(ks[23], (D_MODEL,)),
    }


def reference(x, c, ctx, c_ctx, w_ada, b_ada, norm1_w, ffn1_w_in, ffn1_w_out, norm2_w, w_mix_in,
              conv_w, a_log, dt_bias, gdn_norm_w, w_gdn_proj, pool_w, pool_scale, w_pool_proj,
              w_mix_out, norm3_w, ffn2_w_in, ffn2_w_out, final_norm_w):
    for i in range(DEPTH):
        last = i == DEPTH - 1
        mx = [m[:, None, :] for m in adaln(c, w_ada[i], b_ada[i])]
        mc = adaln(c_ctx, w_ada[i], b_ada[i])

        x = x + 0.5 * mx[2] * swiglu(modulate(rmsnorm(x, norm1_w[i]), mx[0], mx[1]), ffn1_w_in[i], ffn1_w_out[i])
        ctx = ctx + 0.5 * mc[2] * swiglu(modulate(rmsnorm(ctx, norm1_w[i]), mc[0], mc[1]), ffn1_w_in[i], ffn1_w_out[i])

        ux = modulate(rmsnorm(x, norm2_w[i]), mx[3], mx[4])
        uc = modulate(rmsnorm(ctx, norm2_w[i]), mc[3], mc[4])
        px = ux @ w_mix_in[i]
        pc = uc @ (w_mix_in[i][:, :AB_END] if last else w_mix_in[i])
        lat_in = gdn_inputs(px[..., :AB_END], conv_w[i], a_log[i], dt_bias[i])
        ctx_in = gdn_inputs(pc[..., :AB_END], conv_w[i], a_log[i], dt_bias[i])
        ox, oc = bidir_gdn(lat_in, ctx_in)
        mix_x = merge_branches(px, pool_grid(px[..., GATE_END:POOL_END]), ox, pool_w[i], pool_scale[i],
                               gdn_norm_w[i], w_gdn_proj[i], w_pool_proj[i], w_mix_out[i])
        x = x + mx[5] * mix_x

        x = x + 0.5 * mx[8] * swiglu(modulate(rmsnorm(x, norm3_w[i]), mx[6], mx[7]), ffn2_w_in[i], ffn2_w_out[i])

        if not last:
            mix_c = merge_branches(pc, pool_seq(pc[..., GATE_END:POOL_END]), oc, pool_w[i], pool_scale[i],
                                   gdn_norm_w[i], w_gdn_proj[i], w_pool_proj[i], w_mix_out[i])
            ctx = ctx + mc[5] * mix_c
            ctx = ctx + 0.5 * mc[8] * swiglu(modulate(rmsnorm(ctx, norm3_w[i]), mc[6], mc[7]), ffn2_w_in[i], ffn2_w_out[i])
    return rmsnorm(x, final_norm_w)
```

```python
import numpy as np
import ml_dtypes
from contextlib import ExitStack
import concourse.bass as bass
import concourse.mybir as mybir
from concourse.bass_utils import run_bass_kernel_spmd

F32 = mybir.dt.float32
BF16 = mybir.dt.bfloat16
AF = mybir.ActivationFunctionType
ALU = mybir.AluOpType

D = 1024
DFF = 2816
NFF = DFF // 128
NMOD = 9
CTX = 256
H = 8
HD = 128
CH = 64
QKV = 3072
AB_END = 3104
GATE_END = 4128
POOL_END = 4640
MIX_IN = 6688
NT = 256
EPS = 1e-6
GRID_W = 64
NEG = -30000.0


class Buf:
    __slots__ = ("w", "r", "name")

    def __init__(self, name=""):
        self.w = {}
        self.r = {}
        self.name = name


class Sched:
    NDS = 8

    def __init__(self, nc, es):
        self.nc = nc
        self.engs = {"pe": nc.tensor, "dve": nc.vector, "act": nc.scalar, "pool": nc.gpsimd, "sp": nc.sync}
        self.sems = {}
        self.cnt = {}
        for k in ["pe", "dve", "act", "pool"]:
            self.sems[k] = es.enter_context(nc.semaphore("s_" + k))
            self.cnt[k] = 0
        self.dq = {}
        for q in ["sp", "pool", "act"]:
            lst = []
            for i in range(self.NDS):
                key = "d_%s%d" % (q, i)
                self.sems[key] = es.enter_context(nc.semaphore(key))
                self.cnt[key] = 0
                lst.append(key)
            self.dq[q] = [lst, 0]
        self.seen = {e: {} for e in self.engs}
        self.nwaits = 0
        self.nins = 0

    def _wait(self, e, key, val):
        if self.seen[e].get(key, 0) >= val:
            return
        self.engs[e].wait_ge(self.sems[key], val)
        self.seen[e][key] = val
        self.nwaits += 1

    def _deps(self, e, reads, writes):
        for b in reads:
            for key, val in b.w.items():
                self._wait(e, key, val)
        for b in writes:
            for key, val in b.w.items():
                self._wait(e, key, val)
            for key, val in b.r.items():
                self._wait(e, key, val)

    def _done(self, key, val, reads, writes):
        for b in reads:
            b.r[key] = val
        for b in writes:
            b.w[key] = val
            b.r = {}

    def op(self, e, fn, reads=(), writes=()):
        self._deps(e, reads, writes)
        ins = fn(self.engs[e])
        self.cnt[e] += 1
        ins.then_inc(self.sems[e], 1)
        self._done(e, self.cnt[e], reads, writes)
        self.nins += 1

    def mm(self, mms, reads=(), writes=()):
        self._deps("pe", reads, writes)
        ins = None
        for (o, l, r, st, sp) in mms:
            ins = self.nc.tensor.matmul(o, l, r, start=st, stop=sp)
        self.cnt["pe"] += 1
        ins.then_inc(self.sems["pe"], 1)
        self._done("pe", self.cnt["pe"], reads, writes)
        self.nins += len(mms)

    def tr(self, trs, reads=(), writes=()):
        self._deps("pe", reads, writes)
        ins = None
        for (o, i, idn) in trs:
            ins = self.nc.tensor.transpose(o, i, idn)
        self.cnt["pe"] += 1
        ins.then_inc(self.sems["pe"], 1)
        self._done("pe", self.cnt["pe"], reads, writes)
        self.nins += len(trs)

    def dma(self, q, out, in_, reads=(), writes=(), **kw):
        self._deps(q, reads, writes)
        lst, i = self.dq[q]
        key = lst[i % self.NDS]
        self.dq[q][1] = i + 1
        if self.cnt[key] > 0:
            self._wait(q, key, self.cnt[key])
        ins = self.engs[q].dma_start(out=out, in_=in_, **kw)
        self.cnt[key] += 16
        ins.then_inc(self.sems[key], 16)
        self._done(key, self.cnt[key], reads, writes)
        self.nins += 1

    def barrier(self):
        for e in self.engs:
            for key in self.sems:
                if self.cnt[key] > 0:
                    self._wait(e, key, self.cnt[key])


class K:
    def __init__(self, nc, es, SEQ, debug=()):
        self.nc = nc
        self.es = es
        self.s = Sched(nc, es)
        self.SEQ = SEQ
        self.NTOK = CTX + SEQ
        self.debug = set(debug)
        self.uid = 0
        self.dram_in = {}

    def name(self, p):
        self.uid += 1
        return "%s_%d" % (p, self.uid)

    def sb(self, ctx, shape, dt, p="t"):
        return ctx.enter_context(self.nc.sbuf_tensor(self.name(p), list(shape), dt))

    def ps(self, ctx, shape, dt, p="ps"):
        return ctx.enter_context(self.nc.psum_tensor(self.name(p), list(shape), dt))

    def din(self, name, shape, dt=F32):
        t = self.nc.dram_tensor(name, list(shape), dt, kind="ExternalInput")
        self.dram_in[name] = t
        return t.ap()

    def dscr(self, name, shape, dt=F32):
        kind = "ExternalOutput" if name in self.debug else "Internal"
        return self.nc.dram_tensor(name, list(shape), dt, kind=kind).ap()


def bc_mid(ap2, n):
    P, Fd = ap2.shape
    return ap2.unsqueeze(1).broadcast_to([P, n, Fd])


def bc_last(ap2, n):
    P, Fd = ap2.shape
    return ap2.unsqueeze(2).broadcast_to([P, Fd, n])


def load_weight_bf16(k, dst3, w_dram, nk, pieces=1):
    cols = w_dram.shape[1]
    step = (cols + pieces - 1) // pieces
    wb = Buf()
    for kc in range(nk):
        for c0 in range(0, cols, step):
            c1 = min(cols, c0 + step)
            k.s.dma("pool", dst3[:, kc, c0:c1], w_dram[kc * 128:(kc + 1) * 128, c0:c1], writes=[wb])
    return wb


def pass0_mod(k, C):
    s = k.s
    nc = k.nc
    tab = C["tab"]
    btab = C["btab"]
    with ExitStack() as cx:
        sc = k.sb(cx, [128, 8, 2], F32)
        bT = k.sb(cx, [128, 72], F32)
        nw = k.sb(cx, [128, 4, 8], F32)
        mod = k.sb(cx, [128, 2, 72], F32)
        wa = [k.sb(cx, [128, 8, 1024], F32) for _ in range(2)]
        pm = k.ps(cx, [128, 72, 2], F32)
        bsc, bbT, bnw, bmod, bpm = Buf(), Buf(), Buf(), Buf(), Buf()
        bwa = [Buf(), Buf()]
        s.dma("sp", sc[:], k.cvec, writes=[bsc])
        s.dma("sp", bT[:], k.b_ada, writes=[bbT])
        s.dma("sp", nw[:], k.nw, writes=[bnw])
        s.op("act", lambda e: e.activation(sc[:], sc[:], AF.Silu), reads=[bsc], writes=[bsc])
        for j in range(NMOD):
            w = wa[j % 2]
            s.dma("sp", w[:], k.w_ada[:, j * 1024:(j + 1) * 1024].rearrange("(kc p) c -> p kc c", p=128),
                  writes=[bwa[j % 2]])
            mms = []
            for dc in range(8):
                for kc in range(8):
                    mms.append((pm[:, j * 8 + dc, :], w[:, kc, dc * 128:(dc + 1) * 128], sc[:, kc, :],
                                kc == 0, kc == 7))
            s.mm(mms, reads=[bwa[j % 2], bsc], writes=[bpm])
        for wch in range(2):
            s.op("dve", lambda e, wch=wch: e.tensor_tensor(mod[:, wch, :], pm[:, :, wch], bT[:], ALU.add),
                 reads=[bpm, bbT], writes=[bmod])
        for wch in range(2):
            for sub, (jsh, jsc, jg, half) in enumerate([(0, 1, 2, True), (3, 4, 5, False), (6, 7, 8, True)]):
                s.op("dve", lambda e, wch=wch, sub=sub, jsc=jsc: e.scalar_tensor_tensor(
                    tab[:, wch, sub * 3 + 0, :], mod[:, wch, jsc * 8:(jsc + 1) * 8], 1.0, nw[:, sub, :],
                    ALU.add, ALU.mult), reads=[bmod, bnw], writes=[btab])
                s.op("dve", lambda e, wch=wch, sub=sub, jsh=jsh: e.tensor_copy(
                    tab[:, wch, sub * 3 + 1, :], mod[:, wch, jsh * 8:(jsh + 1) * 8]), reads=[bmod], writes=[btab])
                s.op("dve", lambda e, wch=wch, sub=sub, jg=jg, half=half: e.tensor_scalar(
                    tab[:, wch, sub * 3 + 2, :], mod[:, wch, jg * 8:(jg + 1) * 8], 0.5 if half else 1.0, None,
                    ALU.mult), reads=[bmod], writes=[btab])
        s.op("dve", lambda e: e.tensor_copy(C["fnw"][:], nw[:, 3, :]), reads=[bnw], writes=[btab])
        s.barrier()


def ffn_pass(k, C, src, dst, w_in, w_out, sub, tiles, final_out=None):
    s = k.s
    tab = C["tab"]
    btab = C["btab"]
    with ExitStack() as cx:
        Win = k.sb(cx, [128, 8, 2 * DFF], BF16, "win")
        Wout = k.sb(cx, [128, NFF, D], BF16, "wout")
        bWin = load_weight_bf16(k, Win, w_in, 8, pieces=2)
        bWout = load_weight_bf16(k, Wout, w_out, NFF)
        xTs = [k.sb(cx, [128, 8, NT], F32, "xT") for _ in range(2)]
        bxT = [Buf(), Buf()]
        sq = k.sb(cx, [128, 8, NT], F32, "sq")
        bsq = Buf()
        hT = k.sb(cx, [128, 8, NT], BF16, "hT")
        bhT = Buf()
        aT = k.sb(cx, [128, NFF, NT], BF16, "aT")
        baT = Buf()
        rstd = k.sb(cx, [128, NT], F32, "rstd")
        brstd = Buf()
        sg = [k.sb(cx, [128, NT], F32, "sg") for _ in range(2)]
        bsg = [Buf(), Buf()]
        pss = k.ps(cx, [128, 512], F32)
        bpss = Buf()
        pgs = [k.ps(cx, [128, 512], F32) for _ in range(2)]
        pus = [k.ps(cx, [128, 512], F32) for _ in range(2)]
        pys = [k.ps(cx, [128, 512], F32) for _ in range(2)]
        bpg = [Buf(), Buf()]
        bpu = [Buf(), Buf()]
        bpy = [Buf(), Buf()]
        ones = C["ones"]
        epsb = C["eps"]

        def rms(xT, bx):
            s.op("act", lambda e: e.activation(sq[:], xT[:], AF.Square), reads=[bx], writes=[bsq])
            s.mm([(pss[:, 0:NT], ones[:], sq[:, kc, :], kc == 0, kc == 7) for kc in range(8)],
                 reads=[bsq, C["bconst"]], writes=[bpss])
            s.op("act", lambda e: e.activation(rstd[:], pss[:, 0:NT], AF.Sqrt, bias=epsb[:, 0:1], scale=1.0 / D),
                 reads=[bpss, C["bconst"]], writes=[brstd])
            s.op("dve", lambda e: e.reciprocal(rstd[:], rstd[:]), reads=[brstd], writes=[brstd])

        for ti, (t0s, t0d, wch) in enumerate(tiles):
            xT = xTs[ti % 2]
            bx = bxT[ti % 2]
            A = tab[:, wch, sub * 3 + 0, :]
            B = tab[:, wch, sub * 3 + 1, :]
            HG = tab[:, wch, sub * 3 + 2, :]
            s.dma("sp", xT[:], src[:, t0s:t0s + NT].rearrange("(kc p) t -> p kc t", p=128), writes=[bx])
            rms(xT, bx)
            s.op("dve", lambda e: e.tensor_tensor(sq[:], xT[:], bc_mid(rstd[:], 8), ALU.mult),
                 reads=[bx, brstd], writes=[bsq])
            s.op("pool", lambda e: e.tensor_tensor(sq[:], sq[:], bc_last(A, NT), ALU.mult),
                 reads=[bsq, btab], writes=[bsq])
            s.op("pool", lambda e: e.tensor_tensor(hT[:], sq[:], bc_last(B, NT), ALU.add),
                 reads=[bsq, btab], writes=[bhT])
            for m in range(NFF):
                pg, pu = pgs[m % 2], pus[m % 2]
                s.mm([(pg[:, 0:NT], Win[:, kc, m * 128:(m + 1) * 128], hT[:, kc, :], kc == 0, kc == 7)
                      for kc in range(8)], reads=[bWin, bhT], writes=[bpg[m % 2]])
                s.mm([(pu[:, 0:NT], Win[:, kc, DFF + m * 128:DFF + (m + 1) * 128], hT[:, kc, :], kc == 0, kc == 7)
                      for kc in range(8)], reads=[bWin, bhT], writes=[bpu[m % 2]])
                s.op("act", lambda e, m=m, pg=pg: e.activation(sg[m % 2][:], pg[:, 0:NT], AF.Silu),
                     reads=[bpg[m % 2]], writes=[bsg[m % 2]])
                s.op("dve", lambda e, m=m, pu=pu: e.tensor_tensor(aT[:, m, :], sg[m % 2][:], pu[:, 0:NT], ALU.mult),
                     reads=[bsg[m % 2], bpu[m % 2]], writes=[baT])
            for dc in range(8):
                py = pys[dc % 2]
                s.mm([(py[:, 0:NT], Wout[:, m, dc * 128:(dc + 1) * 128], aT[:, m, :], m == 0, m == NFF - 1)
                      for m in range(NFF)], reads=[bWout, baT], writes=[bpy[dc % 2]])
                s.op("dve", lambda e, dc=dc, py=py: e.scalar_tensor_tensor(
                    xT[:, dc, :], py[:, 0:NT], HG[:, dc:dc + 1], xT[:, dc, :], ALU.mult, ALU.add),
                    reads=[bpy[dc % 2], bx, btab], writes=[bx])
            if dst is not None:
                s.dma("sp", dst[:, t0d:t0d + NT].rearrange("(kc p) t -> p kc t", p=128), xT[:], reads=[bx])
            if final_out is not None:
                rms(xT, bx)
                s.op("dve", lambda e: e.tensor_tensor(sq[:], xT[:], bc_mid(rstd[:], 8), ALU.mult),
                     reads=[bx, brstd], writes=[bsq])
                s.op("pool", lambda e: e.tensor_tensor(sq[:], sq[:], bc_last(C["fnw"][:], NT), ALU.mult),
                     reads=[bsq, btab], writes=[bsq])
                s.dma("sp", final_out[:, t0d:t0d + NT].rearrange("(kc p) t -> p kc t", p=128), sq[:], reads=[bsq])
        s.barrier()


def rms_mod(k, C, cx_bufs, xT, bx, A, B, outT, bout):
    s = k.s
    sq, bsq, rstd, brstd, pss, bpss = cx_bufs
    s.op("act", lambda e: e.activation(sq[:], xT[:], AF.Square), reads=[bx], writes=[bsq])
    s.mm([(pss[:, 0:NT], C["ones"][:], sq[:, kc, :], kc == 0, kc == 7) for kc in range(8)],
         reads=[bsq, C["bconst"]], writes=[bpss])
    s.op("act", lambda e: e.activation(rstd[:], pss[:, 0:NT], AF.Sqrt, bias=C["eps"][:, 0:1], scale=1.0 / D),
         reads=[bpss, C["bconst"]], writes=[brstd])
    s.op("dve", lambda e: e.reciprocal(rstd[:], rstd[:]), reads=[brstd], writes=[brstd])
    s.op("dve", lambda e: e.tensor_tensor(sq[:], xT[:], bc_mid(rstd[:], 8), ALU.mult),
         reads=[bx, brstd], writes=[bsq])
    s.op("pool", lambda e: e.tensor_tensor(sq[:], sq[:], bc_last(A, NT), ALU.mult),
         reads=[bsq, C["btab"]], writes=[bsq])
    s.op("pool", lambda e: e.tensor_tensor(outT[:], sq[:], bc_last(B, NT), ALU.add),
         reads=[bsq, C["btab"]], writes=[bout])


def pass2_mixin(k, C, tiles):
    s = k.s
    tab = C["tab"]
    with ExitStack() as cx:
        Wm = k.sb(cx, [128, 8, MIX_IN], BF16, "wmix")
        bWm = load_weight_bf16(k, Wm, k.w_mix_in, 8, pieces=2)
        xTs = [k.sb(cx, [128, 8, NT], F32, "xT") for _ in range(2)]
        bxT = [Buf(), Buf()]
        sq = k.sb(cx, [128, 8, NT], F32, "sq")
        rstd = k.sb(cx, [128, NT], F32, "rstd")
        uT = k.sb(cx, [128, 8, NT], BF16, "uT")
        buT = Buf()
        pss = k.ps(cx, [128, 512], F32)
        cxb = (sq, Buf(), rstd, Buf(), pss, Buf())
        pq = [k.ps(cx, [128, 512], F32) for _ in range(2)]
        bpq = [Buf(), Buf()]
        pab = k.ps(cx, [128, 512], F32)
        bpab = Buf()
        stg = [k.sb(cx, [128, 4, NT], F32, "stg") for _ in range(3)]
        bstg = [Buf(), Buf(), Buf()]
        abs_ = k.sb(cx, [128, 2, 32], F32, "abs")
        babs = Buf()
        pps = k.sb(cx, [128, 2, 512], BF16, "pps")
        bpps = Buf()
        nst = 0
        nq = 0
        for ti, (t0, _, wch) in enumerate(tiles):
            xT = xTs[ti % 2]
            bx = bxT[ti % 2]
            s.dma("sp", xT[:], k.X1T[:, t0:t0 + NT].rearrange("(kc p) t -> p kc t", p=128), writes=[bx])
            rms_mod(k, C, cxb, xT, bx, tab[:, wch, 3, :], tab[:, wch, 4, :], uT, buT)
            if True:
              s.mm([(pab[:, blk * 32:(blk + 1) * 32], uT[:, kc, blk * 128:(blk + 1) * 128], Wm[:, kc, QKV:AB_END],
                   kc == 0, kc == 7) for blk in range(2) for kc in range(8)], reads=[buT, bWm], writes=[bpab])
              s.op("dve", lambda e: e.tensor_copy(abs_[:], pab[:, 0:64].rearrange("p (b c) -> p b c", b=2)),
                 reads=[bpab], writes=[babs])
              s.dma("sp", k.ABT[t0:t0 + NT, :].rearrange("(b p) c -> p b c", p=128), abs_[:], reads=[babs])
            groups = [(0, 24, AF.Copy, k.PQKV, 0, t0)]
            if wch == 0:
                tx = t0 - CTX
                gsel = "123"
                allg = [(AB_END, 8, AF.Silu, k.SGATE, 0, tx), None,
                           (POOL_END, 16, AF.Sigmoid, k.GATES, 0, tx)]
                groups += [allg[int(ch) - 1] for ch in gsel if ch != "2"]
                for blk in range(2):
                    pp = pq[nq % 2]
                    bp = bpq[nq % 2]
                    nq += 1
                    s.mm([(pp[:, :], uT[:, kc, blk * 128:(blk + 1) * 128], Wm[:, kc, GATE_END:POOL_END],
                           kc == 0, kc == 7) for kc in range(8)], reads=[buT, bWm], writes=[bp])
                    s.op("act", lambda e, pp=pp, blk=blk: e.activation(pps[:, blk, :], pp[:, :], AF.Copy),
                         reads=[bp], writes=[bpps])
                s.dma("sp", k.PPT[tx:tx + NT, :].rearrange("(b p) c -> p b c", p=128), pps[:], reads=[bpps])
            for (col0, nch, func, dram, row0, tq) in groups:
                for c0 in range(0, nch, 4):
                    st = stg[nst % 3]
                    bst = bstg[nst % 3]
                    nst += 1
                    for c in range(c0, c0 + 4):
                        pp = pq[nq % 2]
                        bp = bpq[nq % 2]
                        nq += 1
                        cc = col0 + c * 128
                        s.mm([(pp[:, 0:NT], Wm[:, kc, cc:cc + 128], uT[:, kc, :], kc == 0, kc == 7)
                              for kc in range(8)], reads=[buT, bWm], writes=[bp])
                        s.op("act", lambda e, st=st, c=c, c0=c0, pp=pp, func=func: e.activation(
                            st[:, c - c0, :], pp[:, 0:NT], func), reads=[bp], writes=[bst])
                    r0 = row0 + c0 * 128
                    s.dma("sp", dram[r0:r0 + 512, tq:tq + NT].rearrange("(c p) t -> p c t", p=128), st[:],
                          reads=[bst])
        s.barrier()


def pass3a_gdnprep(k, C, tiles):
    s = k.s
    NTOK = k.NTOK
    with ExitStack() as cx:
        cw = k.sb(cx, [128, 24, 5], F32, "cw")
        bcw = Buf()
        s.dma("sp", cw[:], k.convw, writes=[bcw])
        c2 = k.sb(cx, [128, 3, 128], F32, "c2")
        s.dma("sp", c2[:], k.cst2, writes=[bcw])
        identb = k.sb(cx, [128, 128], BF16, "identb")
        s.op("dve", lambda e: e.tensor_copy(identb[:], C["ident"]), reads=[C["bconst"]], writes=[bcw])
        adt = k.sb(cx, [128, 32], F32, "adt")
        s.dma("sp", adt[:], k.adt.partition_broadcast(128), writes=[bcw])
        nega = k.sb(cx, [128, 16], F32, "nega")
        s.op("act", lambda e: e.activation(nega[:], adt[:, 0:16], AF.Exp), reads=[bcw], writes=[bcw])
        s.op("dve", lambda e: e.tensor_scalar(nega[:], nega[:], -1.0, None, ALU.mult), reads=[bcw], writes=[bcw])
        cbias = k.sb(cx, [128, 4], F32, "cbias")
        s.op("dve", lambda e: e.memset(cbias[:, 0:1], EPS), writes=[bcw])
        s.op("dve", lambda e: e.memset(cbias[:, 1:2], 128.0 * EPS), writes=[bcw])
        s.op("dve", lambda e: e.memset(cbias[:, 2:3], 1.0), writes=[bcw])

        W = NT + 4
        pin4 = [k.sb(cx, [128, 4, W], F32, "pin4") for _ in range(2)]
        bpin = [Buf(), Buf()]
        acc4 = [k.sb(cx, [128, 4, NT], F32, "acc4") for _ in range(2)]
        bacc = [Buf(), Buf()]
        sact4 = [k.sb(cx, [128, 4, NT], F32, "sact4") for _ in range(2)]
        bsact = [Buf(), Buf()]
        sq4 = k.sb(cx, [128, 4, NT], F32, "sq4")
        bsq4 = Buf()
        rn4 = k.sb(cx, [128, 4, NT], F32, "rn4")
        brn4 = Buf()
        knf4 = [k.sb(cx, [128, 4, NT], BF16, "knf4") for _ in range(2)]
        bknf = [Buf(), Buf()]
        QTs = [k.sb(cx, [128, 4, 8, 64], BF16, "QTs") for _ in range(2)]
        KTs = [k.sb(cx, [128, 4, 8, 64], BF16, "KTs") for _ in range(2)]
        KTOKs = [k.sb(cx, [128, 2, 8, 128], BF16, "KTOKs") for _ in range(2)]
        VTOKs = [k.sb(cx, [128, 2, 8, 128], BF16, "VTOKs") for _ in range(2)]
        bQTs, bKTs, bKTOKs, bVTOKs = [Buf(), Buf()], [Buf(), Buf()], [Buf(), Buf()], [Buf(), Buf()]
        pn = k.ps(cx, [128, 4, NT], F32)
        bpn = Buf()
        ptr = [k.ps(cx, [128, 2, 4, 128], BF16) for _ in range(2)]
        bptr = [Buf(), Buf()]
        pa = k.ps(cx, [128, 512], F32)
        bpa = Buf()
        pr = k.ps(cx, [128, 512], F32)
        bpr = Buf()
        abt = [k.sb(cx, [128, 32], F32, "abt") for _ in range(2)]
        babt = [Buf(), Buf()]
        aux = [k.sb(cx, [128, 5, 16], F32, "aux") for _ in range(2)]
        baux = [Buf(), Buf()]
        lb = k.sb(cx, [128, 16], F32, "lb")
        gt = k.sb(cx, [128, 16], F32, "gt")
        cl = k.sb(cx, [128, 16], F32, "cl")
        bl = [k.sb(cx, [128, 16], F32, "bl") for _ in range(2)]
        bbl = [Buf(), Buf()]
        rowt = [k.sb(cx, [8, 3, 2, 128], F32, "rowt") for _ in range(2)]
        browt = [Buf(), Buf()]
        bsm = Buf()
        ng = 0
        nblk = 0
        ntr = 0
        for ti, (t0, _, wch) in enumerate(tiles):
            seq_lo = 0 if wch == 1 else CTX
            seq_hi = CTX if wch == 1 else NTOK
            qts, kts, ktoks, vtoks = QTs[ti % 2], KTs[ti % 2], KTOKs[ti % 2], VTOKs[ti % 2]
            for g4 in range(6):
                kind = g4 // 2
                h0 = (g4 % 2) * 4
                pin = pin4[ng % 2]
                bp = bpin[ng % 2]
                acc = acc4[ng % 2]
                ba = bacc[ng % 2]
                sact = sact4[ng % 2]
                bs = bsact[ng % 2]
                ng += 1
                lo = max(t0 - 2, seq_lo)
                hi = min(t0 + NT + 2, seq_hi)
                if lo > t0 - 2:
                    s.op("pool", lambda e, pin=pin: e.memset(pin[:, :, 0:2], 0.0), writes=[bp])
                if hi < t0 + NT + 2:
                    s.op("pool", lambda e, pin=pin: e.memset(pin[:, :, W - 2:W], 0.0), writes=[bp])
                s.dma("sp", pin[:, :, lo - (t0 - 2):hi - (t0 - 2)],
                      k.PQKV[g4 * 512:(g4 + 1) * 512, lo:hi].rearrange("(c p) t -> p c t", p=128), writes=[bp])
                for i in range(4):
                    cc = g4 * 4 + i
                    s.op("dve", lambda e, i=i, cc=cc, pin=pin, acc=acc: e.tensor_scalar(
                        acc[:, i, :], pin[:, i, 0:NT], cw[:, cc, 0:1], None, ALU.mult), reads=[bp, bcw], writes=[ba])
                    for kk in range(1, 5):
                        s.op("dve", lambda e, i=i, cc=cc, kk=kk, pin=pin, acc=acc: e.scalar_tensor_tensor(
                            acc[:, i, :], pin[:, i, kk:kk + NT], cw[:, cc, kk:kk + 1], acc[:, i, :],
                            ALU.mult, ALU.add), reads=[bp, bcw, ba], writes=[ba])
                if kind == 2:
                    vb = knf4[ng % 2]
                    bvb = bknf[ng % 2]
                    s.op("act", lambda e, acc=acc, vb=vb: e.activation(vb[:], acc[:], AF.Silu), reads=[ba], writes=[bvb])
                    src_t, bsrc, dstk, bdst = vb, bvb, vtoks, bVTOKs[ti % 2]
                else:
                    s.op("act", lambda e, acc=acc, sact=sact: e.activation(sact[:], acc[:], AF.Silu),
                         reads=[ba], writes=[bs])
                    s.op("act", lambda e, sact=sact: e.activation(sq4[:], sact[:], AF.Square), reads=[bs], writes=[bsq4])
                    s.mm([(pn[:, i, :], C["ones"], sq4[:, i, :], True, True) for i in range(4)],
                         reads=[bsq4, C["bconst"]], writes=[bpn])
                    if kind == 0:
                        s.op("act", lambda e: e.activation(rn4[:], pn[:], AF.Ln, bias=cbias[:, 1:2], scale=128.0),
                             reads=[bpn, bcw], writes=[brn4])
                    else:
                        s.op("act", lambda e: e.activation(rn4[:], pn[:], AF.Ln, bias=cbias[:, 0:1], scale=1.0),
                             reads=[bpn, bcw], writes=[brn4])
                    s.op("act", lambda e: e.activation(rn4[:], rn4[:], AF.Exp, scale=-0.5), reads=[brn4], writes=[brn4])
                    if kind == 0:
                        for i in range(4):
                            s.op("dve", lambda e, i=i, sact=sact: e.tensor_tensor(
                                qts[:, :, h0 + i, :], sact[:, i, :].rearrange("p (c t) -> p c t", c=4),
                                rn4[:, i, :].rearrange("p (c t) -> p c t", c=4), ALU.mult),
                                reads=[bs, brn4], writes=[bQTs[ti % 2]])
                        continue
                    kn = knf4[ng % 2]
                    bkn = bknf[ng % 2]
                    s.op("dve", lambda e, sact=sact, kn=kn: e.tensor_tensor(kn[:], sact[:], rn4[:], ALU.mult),
                         reads=[bs, brn4], writes=[bkn])
                    for i in range(4):
                        s.op("pool", lambda e, i=i, kn=kn: e.tensor_copy(
                            kts[:, :, h0 + i, :], kn[:, i, :].rearrange("p (c t) -> p c t", c=4)),
                            reads=[bkn], writes=[bKTs[ti % 2]])
                    src_t, bsrc, dstk, bdst = kn, bkn, ktoks, bKTOKs[ti % 2]
                pt = ptr[ntr % 2]
                bpt = bptr[ntr % 2]
                ntr += 1
                s.tr([(pt[:, blk, i, :], src_t[:, i, blk * 128:(blk + 1) * 128], identb[:])
                      for blk in range(2) for i in range(4)], reads=[bsrc, bcw], writes=[bpt])
                s.op("act", lambda e, pt=pt, dstk=dstk: e.activation(dstk[:, :, h0:h0 + 4, :], pt[:], AF.Copy),
                     reads=[bpt], writes=[bdst])
            c0 = t0 // CH
            s.dma("sp", k.QT[c0:c0 + 4].rearrange("c d h t -> d c (h t)"), qts[:].rearrange("d c h t -> d c (h t)"),
                  reads=[bQTs[ti % 2]])
            s.dma("sp", k.KT[c0:c0 + 4].rearrange("c d h t -> d c (h t)"), kts[:].rearrange("d c h t -> d c (h t)"),
                  reads=[bKTs[ti % 2]])
            s.dma("sp", k.KTOK[t0:t0 + NT].rearrange("(b p) h d -> p b (h d)", p=128),
                  ktoks[:].rearrange("p b h d -> p b (h d)"), reads=[bKTOKs[ti % 2]])
            s.dma("sp", k.VTOK[t0:t0 + NT].rearrange("(b p) h d -> p b (h d)", p=128),
                  vtoks[:].rearrange("p b h d -> p b (h d)"), reads=[bVTOKs[ti % 2]])
            for blk in range(2):
                tb = t0 + blk * 128
                ab = abt[nblk % 2]
                bab = babt[nblk % 2]
                ax = aux[nblk % 2]
                bax = baux[nblk % 2]
                blt = bl[nblk % 2]
                bblt = bbl[nblk % 2]
                rw = rowt[nblk % 2]
                brw = browt[nblk % 2]
                nblk += 1
                s.dma("sp", ab[:], k.ABT[tb:tb + 128, :], writes=[bab])
                s.op("act", lambda e, ab=ab, ax=ax: e.activation(ax[:, 2, :], ab[:, 0:16], AF.Sigmoid),
                     reads=[bab], writes=[bax])
                s.op("act", lambda e, ax=ax: e.activation(lb[:], ax[:, 2, :], AF.Ln), reads=[bax], writes=[bsm])
                s.op("dve", lambda e, ab=ab: e.tensor_tensor(gt[:], ab[:, 16:32], adt[:, 16:32], ALU.add),
                     reads=[bab, bcw, bsm], writes=[bsm])
                s.op("act", lambda e: e.activation(gt[:], gt[:], AF.Exp), reads=[bsm], writes=[bsm])
                s.op("act", lambda e: e.activation(gt[:], gt[:], AF.Ln, bias=cbias[:, 2:3], scale=1.0),
                     reads=[bsm, bcw], writes=[bsm])
                s.op("dve", lambda e: e.tensor_tensor(gt[:], gt[:], nega[:], ALU.mult), reads=[bsm, bcw], writes=[bsm])
                s.mm([(pa[:, 0:8], c2[:, 0, :], gt[:, 0:8], True, True),
                      (pa[:, 8:16], c2[:, 1, :], gt[:, 8:16], True, True),
                      (pa[:, 16:32], c2[:, 2, :], gt[:, 0:16], True, True)], reads=[bsm, bcw], writes=[bpa])
                s.mm([(pr[0:8, 0:128], gt[:, 0:8], c2[:, 0, :], True, True),
                      (pr[0:8, 128:256], gt[:, 8:16], c2[:, 1, :], True, True),
                      (pr[0:8, 256:384], gt[:, 0:8], c2[:, 0, :], True, False),
                      (pr[0:8, 256:384], lb[:, 0:8], C["ident"], False, True),
                      (pr[0:8, 384:512], gt[:, 8:16], c2[:, 1, :], True, False),
                      (pr[0:8, 384:512], lb[:, 8:16], C["ident"], False, True)],
                     reads=[bsm, bcw, C["bconst"]], writes=[bpr])
                s.op("dve", lambda e, ax=ax: e.tensor_copy(ax[:, 0, :], pa[:, 0:16]), reads=[bpa], writes=[bax])
                s.op("dve", lambda e: e.tensor_copy(cl[:], pa[:, 16:32]), reads=[bpa, bsm], writes=[bsm])
                s.op("dve", lambda e, ax=ax: e.tensor_tensor(ax[:, 1, :], ax[:, 0, :], lb[:], ALU.add),
                     reads=[bax, bsm], writes=[bax])
                s.op("act", lambda e, ax=ax: e.activation(ax[:, 3, :], ax[:, 1, :], AF.Exp), reads=[bax], writes=[bax])
                s.op("dve", lambda e, ax=ax: e.tensor_tensor(ax[:, 4, :], cl[:], ax[:, 0, :], ALU.subtract),
                     reads=[bax, bsm], writes=[bax])
                s.op("act", lambda e, ax=ax: e.activation(ax[:, 4, :], ax[:, 4, :], AF.Exp), reads=[bax], writes=[bax])
                s.op("act", lambda e, blt=blt: e.activation(blt[:], cl[:], AF.Exp), reads=[bsm], writes=[bblt])
                s.op("dve", lambda e, rw=rw: e.tensor_copy(rw[:, 0:2, :, :].rearrange("p a b t -> p (a b t)"),
                                                          pr[0:8, :]), reads=[bpr], writes=[brw])
                s.op("act", lambda e, rw=rw: e.activation(rw[:, 2, :, :].rearrange("p b t -> p (b t)"),
                                                         pr[0:8, 0:256], AF.Exp), reads=[bpr], writes=[brw])
                s.dma("sp", k.AUXT[tb:tb + 128], ax[:], reads=[bax])
                s.dma("sp", k.BLT[tb:tb + 128], blt[:], reads=[bblt])
                for a in range(3):
                    s.dma("sp", k.AUXR[a, :, :, tb:tb + 128], rw[:, a, :, :], reads=[brw])
        s.barrier()


class TB:
    def __init__(self, t):
        self.t = t
        self.b = Buf()


def pass3b_scan(k, C):
    s = k.s
    NTOK = k.NTOK
    NCH = NTOK // CH
    NCC = CTX // CH
    with ExitStack() as cx:
        def sbt(shape, dt, p="t"):
            return TB(k.sb(cx, shape, dt, p))

        msk = sbt([64, 4, 8, 64], F32, "msk")
        s.dma("sp", msk.t[:], k.masks, writes=[msk.b])
        idr = sbt([64, 8, 64], F32, "idr")
        s.dma("sp", idr.t[:], k.identrep, writes=[idr.b])
        S32 = [sbt([128, 8, 128], F32, "S32") for _ in range(2)]
        Sb = [sbt([128, 8, 128], BF16, "Sb") for _ in range(2)]
        for d in range(2):
            s.op("dve", lambda e, d=d: e.memset(S32[d].t[:], 0.0), writes=[S32[d].b])
            s.op("pool", lambda e, d=d: e.memset(Sb[d].t[:], 0.0), writes=[Sb[d].b])

        NSET = 2

        def mk():
            return dict(
                qT=sbt([128, 8, 64], BF16, "qT"), kT=sbt([128, 8, 64], BF16, "kT"),
                ktok=sbt([64, 8, 128], BF16, "ktok"), vtok=sbt([64, 8, 128], BF16, "vtok"),
                axt=sbt([64, 5, 16], F32, "axt"), R1=sbt([64, 8, 64], F32, "R1"), R2=sbt([64, 8, 64], F32, "R2"),
                EC=sbt([128, 8, 64], F32, "EC"), BL=sbt([128, 8], F32, "BL"),
                EA=sbt([64, 8, 64], F32, "EA"), EM=sbt([64, 8, 64], F32, "EM"), EL=sbt([64, 8, 64], F32, "EL"),
                aqk=sbt([64, 8, 64], BF16, "aqk"),
                Mp=[sbt([64, 8, 64], F32, "Mp") for _ in range(2)],
                Lp=[sbt([64, 8, 64], F32, "Lp") for _ in range(2)],
                P=[sbt([64, 8, 64], F32, "P") for _ in range(2)],
                TT=sbt([64, 8, 64], BF16, "TT"),
                bv=sbt([64, 8, 128], BF16, "bv"), kb=sbt([64, 8, 128], BF16, "kb"), kd=sbt([64, 8, 128], BF16, "kd"),
                u=sbt([64, 8, 128], F32, "u"), wT=sbt([128, 8, 64], BF16, "wT"), qd=sbt([128, 8, 64], BF16, "qd"),
                vn=sbt([64, 8, 128], BF16, "vn"), o=sbt([128, 8, 64], F32, "o"),
            )

        sets = [mk() for _ in range(NSET)]
        pA = TB(k.ps(cx, [128, 512], F32))
        pB = TB(k.ps(cx, [128, 512], F32))
        pC = TB(k.ps(cx, [128, 512], F32))
        pDE = TB(k.ps(cx, [128, 1024], F32))
        pF = TB(k.ps(cx, [128, 512], F32))
        pGH = TB(k.ps(cx, [128, 1024], F32))

        def v3(ap, n):
            return ap.rearrange("p (h n) -> p h n", h=8)

        def instance(n, ch, d, is_x):
            T = sets[n % NSET]
            tok0 = ch * CH
            d8 = slice(d * 8, (d + 1) * 8)
            s.dma("sp", T["qT"].t[:], k.QT[ch], writes=[T["qT"].b])
            s.dma("sp", T["kT"].t[:], k.KT[ch], writes=[T["kT"].b])
            s.dma("sp", T["ktok"].t[:], k.KTOK[tok0:tok0 + CH], writes=[T["ktok"].b])
            s.dma("sp", T["vtok"].t[:], k.VTOK[tok0:tok0 + CH], writes=[T["vtok"].b])
            s.dma("sp", T["axt"].t[:], k.AUXT[tok0:tok0 + CH], writes=[T["axt"].b])
            s.dma("sp", T["R1"].t[:], k.AUXR[0, :, d, tok0:tok0 + CH].partition_broadcast(64), writes=[T["R1"].b])
            s.dma("sp", T["R2"].t[:], k.AUXR[1, :, d, tok0:tok0 + CH].partition_broadcast(64), writes=[T["R2"].b])
            s.dma("sp", T["EC"].t[:], k.AUXR[2, :, d, tok0:tok0 + CH].partition_broadcast(128), writes=[T["EC"].b])
            s.dma("sp", T["BL"].t[:], k.BLT[tok0:tok0 + 1, d8].partition_broadcast(128), writes=[T["BL"].b])
            axt = T["axt"].t
            c_b = bc_last(axt[:, 0, d8], 64)
            cb_b = bc_last(axt[:, 1, d8], 64)
            m_incl = msk.t[:, 0 if d == 0 else 2, :, :]
            m_strT = msk.t[:, 1 if d == 0 else 3, :, :]
            m_strL = msk.t[:, 3 if d == 0 else 1, :, :]
            kT, qT = T["kT"], T["qT"]
            s.mm([(v3(pA.t[0:64, :], 64)[:, h, :], kT.t[:, h, :], kT.t[:, h, :], True, True) for h in range(8)],
                 reads=[kT.b], writes=[pA.b])
            s.mm([(v3(pB.t[0:64, :], 64)[:, h, :], kT.t[:, h, :], qT.t[:, h, :], True, True) for h in range(8)],
                 reads=[kT.b, qT.b], writes=[pB.b])
            G = v3(pA.t[0:64, :], 64)
            QK = v3(pB.t[0:64, :], 64)
            EA, EM, EL = T["EA"], T["EM"], T["EL"]
            s.op("dve", lambda e: e.tensor_tensor(EA.t[:], T["R1"].t[:], c_b, ALU.subtract),
                 reads=[T["R1"].b, T["axt"].b], writes=[EA.b])
            s.op("dve", lambda e: e.tensor_tensor(EM.t[:], T["R2"].t[:], c_b, ALU.subtract),
                 reads=[T["R2"].b, T["axt"].b], writes=[EM.b])
            s.op("dve", lambda e: e.scalar_tensor_tensor(EL.t[:], T["R1"].t[:], -1.0, cb_b, ALU.mult, ALU.add),
                 reads=[T["R1"].b, T["axt"].b], writes=[EL.b])
            for (E, mk_) in ((EA, m_incl), (EM, m_strT), (EL, m_strL)):
                s.op("pool", lambda e, E=E: e.tensor_scalar(E.t[:], E.t[:], 0.0, None, ALU.min),
                     reads=[E.b], writes=[E.b])
                s.op("pool", lambda e, E=E, mk_=mk_: e.tensor_tensor(E.t[:], E.t[:], mk_, ALU.add),
                     reads=[E.b, msk.b], writes=[E.b])
                s.op("act", lambda e, E=E: e.activation(E.t[:], E.t[:], AF.Exp), reads=[E.b], writes=[E.b])
            aqk = T["aqk"]
            s.op("dve", lambda e: e.tensor_tensor(aqk.t[:], QK, EA.t[:], ALU.mult), reads=[pB.b, EA.b], writes=[aqk.b])
            Mp, Lp, P = T["Mp"], T["Lp"], T["P"]
            s.op("dve", lambda e: e.tensor_tensor(Mp[0].t[:], G, EM.t[:], ALU.mult), reads=[pA.b, EM.b], writes=[Mp[0].b])
            s.op("dve", lambda e: e.tensor_tensor(Lp[0].t[:], G, EL.t[:], ALU.mult), reads=[pA.b, EL.b], writes=[Lp[0].b])
            s.op("dve", lambda e: e.tensor_tensor(P[0].t[:], idr.t[:], Mp[0].t[:], ALU.subtract),
                 reads=[idr.b, Mp[0].b], writes=[P[0].b])
            cur = 0
            pc = 0
            for lvl in range(1, 6):
                nxt = 1 - cur
                Lc, Mc, Ln, Mn = Lp[cur], Mp[cur], Lp[nxt], Mp[nxt]
                pa3, pb3, pc3 = v3(pA.t[0:64, :], 64), v3(pB.t[0:64, :], 64), v3(pC.t[0:64, :], 64)
                s.mm([(pb3[:, h, :], Mc.t[:, h, :], Lc.t[:, h, :], True, True) for h in range(8)],
                     reads=[Mc.b, Lc.b], writes=[pB.b])
                if lvl < 5:
                    s.mm([(pa3[:, h, :], Lc.t[:, h, :], Mc.t[:, h, :], True, True) for h in range(8)],
                         reads=[Mc.b, Lc.b], writes=[pA.b])
                s.op("act", lambda e, Ln=Ln: e.activation(Ln.t[:], pb3, AF.Copy), reads=[pB.b], writes=[Ln.b])
                if lvl < 5:
                    s.op("act", lambda e, Mn=Mn: e.activation(Mn.t[:], pa3, AF.Copy), reads=[pA.b], writes=[Mn.b])
                Pc, Pn = P[pc], P[1 - pc]
                s.mm([(pc3[:, h, :], Ln.t[:, h, :], Pc.t[:, h, :], True, True) for h in range(8)],
                     reads=[Ln.b, Pc.b], writes=[pC.b])
                if lvl < 5:
                    s.op("dve", lambda e, Pc=Pc, Pn=Pn: e.tensor_tensor(Pn.t[:], Pc.t[:], pc3, ALU.add),
                         reads=[Pc.b, pC.b], writes=[Pn.b])
                else:
                    s.op("dve", lambda e, Pc=Pc: e.tensor_tensor(T["TT"].t[:], Pc.t[:], pc3, ALU.add),
                         reads=[Pc.b, pC.b], writes=[T["TT"].b])
                cur = nxt
                pc = 1 - pc
            TT = T["TT"]
            bv, kb, kd, qd = T["bv"], T["kb"], T["kd"], T["qd"]
            s.op("pool", lambda e: e.tensor_tensor(bv.t[:], T["vtok"].t[:], bc_last(axt[:, 2, d8], 128), ALU.mult),
                 reads=[T["vtok"].b, T["axt"].b], writes=[bv.b])
            s.op("pool", lambda e: e.tensor_tensor(kb.t[:], T["ktok"].t[:], bc_last(axt[:, 3, d8], 128), ALU.mult),
                 reads=[T["ktok"].b, T["axt"].b], writes=[kb.b])
            s.op("pool", lambda e: e.tensor_tensor(kd.t[:], T["ktok"].t[:], bc_last(axt[:, 4, d8], 128), ALU.mult),
                 reads=[T["ktok"].b, T["axt"].b], writes=[kd.b])
            s.op("pool", lambda e: e.tensor_tensor(qd.t[:], qT.t[:], T["EC"].t[:], ALU.mult),
                 reads=[qT.b, T["EC"].b], writes=[qd.b])
            pu = pDE.t[0:64, :].rearrange("p (h n) -> p h n", h=8)
            pw = v3(pF.t[:, :], 64)
            s.mm([(pu[:, h, :], TT.t[:, h, :], bv.t[:, h, :], True, True) for h in range(8)],
                 reads=[TT.b, bv.b], writes=[pDE.b])
            s.mm([(pw[:, h, :], kb.t[:, h, :], TT.t[:, h, :], True, True) for h in range(8)],
                 reads=[TT.b, kb.b], writes=[pF.b])
            u, wT = T["u"], T["wT"]
            s.op("act", lambda e: e.activation(u.t[:], pu, AF.Copy), reads=[pDE.b], writes=[u.b])
            s.op("act", lambda e: e.activation(wT.t[:], pw, AF.Copy), reads=[pF.b], writes=[wT.b])
            Sd, S3 = Sb[d], S32[d]
            s.mm([(pu[:, h, :], wT.t[:, h, :], Sd.t[:, h, :], True, True) for h in range(8)],
                 reads=[wT.b, Sd.b], writes=[pDE.b])
            vn = T["vn"]
            s.op("dve", lambda e: e.tensor_tensor(vn.t[:], u.t[:], pu, ALU.subtract), reads=[u.b, pDE.b], writes=[vn.b])
            if is_x:
                mms = []
                for h in range(8):
                    mms.append((pw[:, h, :], Sd.t[:, h, :], qd.t[:, h, :], True, False))
                    mms.append((pw[:, h, :], vn.t[:, h, :], aqk.t[:, h, :], False, True))
                s.mm(mms, reads=[Sd.b, qd.b, vn.b, aqk.b], writes=[pF.b])
                o = T["o"]
                s.op("act", lambda e: e.activation(o.t[:], pw, AF.Copy), reads=[pF.b], writes=[o.b])
                s.dma("sp", k.OT[d, ch - NCC], o.t[:], reads=[o.b])
            pS = pGH.t[:, :].rearrange("p (h n) -> p h n", h=8)
            s.mm([(pS[:, h, :], kd.t[:, h, :], vn.t[:, h, :], True, True) for h in range(8)],
                 reads=[kd.b, vn.b], writes=[pGH.b])
            s.op("pool", lambda e: e.tensor_tensor(S3.t[:], S3.t[:], bc_last(T["BL"].t[:, :], 128), ALU.mult),
                 reads=[S3.b, T["BL"].b], writes=[S3.b])
            s.op("dve", lambda e: e.tensor_tensor(S3.t[:], S3.t[:], pS, ALU.add), reads=[S3.b, pGH.b], writes=[S3.b])
            s.op("act", lambda e: e.activation(Sd.t[:], S3.t[:], AF.Copy), reads=[S3.b], writes=[Sd.b])

        order_f = list(range(NCH))
        order_b = list(range(NCC - 1, -1, -1)) + list(range(NCH - 1, NCC - 1, -1))
        n = 0
        for st in range(NCH):
            instance(n, order_f[st], 0, order_f[st] >= NCC)
            n += 1
            instance(n, order_b[st], 1, order_b[st] >= NCC)
            n += 1
        s.barrier()


def pool_operators(SEQ):
    R = SEQ // GRID_W
    NB = SEQ // 128
    wins = (2, 4, 8, 16)
    mats = []
    index = {}
    cache = {}
    cc = np.arange(GRID_W)
    for g, w in enumerate(wins):
        clo = np.clip(cc - w // 2, 0, GRID_W)
        chi = np.clip(cc + w - w // 2, 0, GRID_W)
        cm = ((cc[:, None] >= clo[None, :]) & (cc[:, None] < chi[None, :])).astype(np.float64)
        car = (chi - clo).astype(np.float64)
        maxoff = (w // 2 + 1) // 2 + 1
        for b in range(NB):
            for off in range(-maxoff, maxoff + 1):
                bp = b + off
                if bp < 0 or bp >= NB:
                    continue
                M = np.zeros((128, 128), np.float64)
                for ro in range(2):
                    r = 2 * b + ro
                    rlo = max(r - w // 2, 0)
                    rhi = min(r + w - w // 2, R)
                    for ri in range(2):
                        rp = 2 * bp + ri
                        if rlo <= rp < rhi:
                            M[ri * 64:(ri + 1) * 64, ro * 64:(ro + 1) * 64] = cm / (car[None, :] * (rhi - rlo))
                if off == 0:
                    M -= np.eye(128)
                if not M.any():
                    continue
                key = (g, off, M.tobytes())
                if key not in cache:
                    cache[key] = len(mats)
                    mats.append(M.astype(np.float32))
                index[(g, b, off)] = cache[key]
    return np.stack(mats, 0), index


def pass4_merge(k, C):
    s = k.s
    SEQ = k.SEQ
    NB = SEQ // 128
    tab = C["tab"]
    with ExitStack() as cx:
        Wg = k.sb(cx, [128, 8, D], BF16, "wg")
        Wp = k.sb(cx, [128, 4, D], BF16, "wp")
        Wmo = k.sb(cx, [128, 8, D], BF16, "wmo")
        bW = load_weight_bf16(k, Wg, k.w_gdn_proj, 8)
        bW2 = load_weight_bf16(k, Wp, k.w_pool_proj, 4)
        bW3 = load_weight_bf16(k, Wmo, k.w_mix_out, 8)
        nmat = k.opm_n
        opm = k.sb(cx, [128, nmat, 128], BF16, "opm")
        poolw = k.sb(cx, [128, 4, 128], BF16, "poolw")
        bW4 = Buf()
        s.dma("pool", opm[:], k.opm.rearrange("n p t -> p n t"), writes=[bW4])
        s.dma("pool", poolw[:], k.poolw, writes=[bW4])
        small = k.sb(cx, [128, 8], F32, "small")
        s.dma("sp", small[:, 0:1], k.gnw, writes=[bW4])
        s.dma("sp", small[:, 1:5], k.pscale, writes=[bW4])
        cb2 = k.sb(cx, [128, 2], F32, "cb2")
        s.op("dve", lambda e: e.memset(cb2[:, 0:1], EPS), writes=[bW4])
        wts = [bW, bW2, bW3, bW4]

        def dbl(shape, dt, p):
            return [TB(k.sb(cx, shape, dt, p)) for _ in range(2)]

        x1T = dbl([128, 8, NT], F32, "x1T")
        of = [TB(k.sb(cx, [128, 4, 8, 64], F32, "of"))] * 2
        ob = [TB(k.sb(cx, [128, 4, 8, 64], F32, "ob"))] * 2
        sgT = dbl([128, 8, NT], F32, "sgT")
        gts = [TB(k.sb(cx, [128, 16, NT], F32, "gts"))] * 2
        ppt = dbl([128, 10, 512], BF16, "ppt")
        o = TB(k.sb(cx, [128, 8, NT], F32, "o"))
        sq = TB(k.sb(cx, [128, 8, NT], F32, "sq"))
        rs = sq
        og = TB(k.sb(cx, [128, 8, NT], BF16, "og"))
        pd = TB(k.sb(cx, [128, 4, NT], BF16, "pd"))
        yp1 = TB(k.sb(cx, [128, 4, NT], BF16, "yp1"))
        t1 = dbl([128, NT], F32, "t1")
        t2 = dbl([128, NT], F32, "t2")
        msT = TB(k.sb(cx, [128, 8, NT], BF16, "msT"))
        pn = TB(k.ps(cx, [128, 4, NT], F32))
        pgp = [TB(k.ps(cx, [128, 2, NT], F32)) for _ in range(2)]
        ppd = TB(k.ps(cx, [128, 4, NT], F32))
        pmx = [TB(k.ps(cx, [128, 512], F32)) for _ in range(2)]
        for ti in range(SEQ // NT):
            tx0 = ti * NT
            cx0 = tx0 // CH
            b0 = tx0 // 128
            X, OF, OB, SG, GT, PP = x1T[ti % 2], of[ti % 2], ob[ti % 2], sgT[ti % 2], gts[ti % 2], ppt[ti % 2]
            s.dma("sp", X.t[:], k.X1T[:, CTX + tx0:CTX + tx0 + NT].rearrange("(kc p) t -> p kc t", p=128), writes=[X.b])
            s.dma("sp", OF.t[:].rearrange("p c h t -> p c (h t)"),
                  k.OT[0, cx0:cx0 + 4].rearrange("c d h t -> d c (h t)"), writes=[OF.b])
            s.dma("sp", OB.t[:].rearrange("p c h t -> p c (h t)"),
                  k.OT[1, cx0:cx0 + 4].rearrange("c d h t -> d c (h t)"), writes=[OB.b])
            s.dma("sp", SG.t[:], k.SGATE[:, tx0:tx0 + NT].rearrange("(kc p) t -> p kc t", p=128), writes=[SG.b])
            s.dma("sp", GT.t[:], k.GATES[:, tx0:tx0 + NT].rearrange("(kc p) t -> p kc t", p=128), writes=[GT.b])
            blo = max(b0 - 4, 0)
            bhi = min(b0 + 6, NB)
            s.dma("sp", PP.t[:, 0:bhi - blo, :], k.PPT[blo * 128:bhi * 128, :].rearrange("(b p) c -> p b c", p=128),
                  writes=[PP.b])
            o4 = o.t[:].rearrange("p h (c t) -> p h c t", c=4)
            s.op("pool", lambda e, OF=OF, OB=OB: e.tensor_tensor(
                o4, OF.t[:].rearrange("p c h t -> p h c t"), OB.t[:].rearrange("p c h t -> p h c t"), ALU.add),
                reads=[OF.b, OB.b], writes=[o.b])
            s.op("act", lambda e: e.activation(sq.t[:], o.t[:], AF.Square), reads=[o.b], writes=[sq.b])
            for hh in range(2):
                s.mm([(pn.t[:, i, :], C["ones"], sq.t[:, hh * 4 + i, :], True, True) for i in range(4)],
                     reads=[sq.b, C["bconst"]], writes=[pn.b])
                s.op("act", lambda e, hh=hh: e.activation(rs.t[:, hh * 4:hh * 4 + 4, :], pn.t[:], AF.Ln,
                                                          bias=cb2[:, 0:1], scale=1.0 / HD),
                     reads=[pn.b, bW4], writes=[rs.b])
            s.op("act", lambda e: e.activation(rs.t[:], rs.t[:], AF.Exp, scale=-0.5), reads=[rs.b], writes=[rs.b])
            s.op("dve", lambda e: e.tensor_tensor(o.t[:], o.t[:], rs.t[:], ALU.mult), reads=[o.b, rs.b], writes=[o.b])
            s.op("dve", lambda e, SG=SG: e.scalar_tensor_tensor(
                og.t[:].rearrange("p h t -> p (h t)"), o.t[:].rearrange("p h t -> p (h t)"), small[:, 0:1],
                SG.t[:].rearrange("p h t -> p (h t)"), ALU.mult, ALU.mult), reads=[o.b, SG.b, bW4], writes=[og.b])
            mms = []
            for g in range(4):
                for obk in range(2):
                    b = b0 + obk
                    offs = [off for off in range(-5, 6) if (g, b, off) in k.opm_index]
                    for j, off in enumerate(offs):
                        mms.append((ppd.t[:, g, obk * 128:(obk + 1) * 128],
                                    PP.t[:, b + off - blo, g * 128:(g + 1) * 128],
                                    opm[:, k.opm_index[(g, b, off)], :], j == 0, j == len(offs) - 1))
            s.mm(mms, reads=[PP.b, bW4], writes=[ppd.b])
            s.op("act", lambda e: e.activation(pd.t[:], ppd.t[:], AF.Copy), reads=[ppd.b], writes=[pd.b])
            s.mm([(ppd.t[:, g, :], poolw[:, g, :], pd.t[:, g, :], True, True) for g in range(4)],
                 reads=[pd.b, bW4], writes=[ppd.b])
            s.op("dve", lambda e: e.tensor_tensor(yp1.t[:], ppd.t[:], bc_last(small[:, 1:5], NT), ALU.mult),
                 reads=[ppd.b, bW4], writes=[yp1.b])
            for dc in range(8):
                pg = pgp[dc % 2]
                mms = [(pg.t[:, 0, :], Wg[:, h, dc * 128:(dc + 1) * 128], og.t[:, h, :], h == 0, h == 7)
                       for h in range(8)]
                mms += [(pg.t[:, 1, :], Wp[:, g, dc * 128:(dc + 1) * 128], yp1.t[:, g, :], g == 0, g == 3)
                        for g in range(4)]
                s.mm(mms, reads=[og.b, yp1.b] + wts, writes=[pg.b])
                a1, a2 = t1[dc % 2], t2[dc % 2]
                s.op("dve", lambda e, pg=pg, a1=a1, GT=GT, dc=dc: e.tensor_tensor(
                    a1.t[:], pg.t[:, 0, :], GT.t[:, 8 + dc, :], ALU.mult), reads=[pg.b, GT.b], writes=[a1.b])
                s.op("dve", lambda e, pg=pg, a2=a2, GT=GT, dc=dc: e.tensor_tensor(
                    a2.t[:], pg.t[:, 1, :], GT.t[:, dc, :], ALU.mult), reads=[pg.b, GT.b], writes=[a2.b])
                s.op("pool", lambda e, a1=a1, a2=a2, dc=dc: e.tensor_tensor(msT.t[:, dc, :], a1.t[:], a2.t[:], ALU.add),
                     reads=[a1.b, a2.b], writes=[msT.b])
            for dc in range(8):
                pm = pmx[dc % 2]
                s.mm([(pm.t[:, 0:NT], Wmo[:, kc, dc * 128:(dc + 1) * 128], msT.t[:, kc, :], kc == 0, kc == 7)
                      for kc in range(8)], reads=[msT.b] + wts, writes=[pm.b])
                s.op("dve", lambda e, pm=pm, X=X, dc=dc: e.scalar_tensor_tensor(
                    X.t[:, dc, :], pm.t[:, 0:NT], tab[:, 0, 5, dc:dc + 1], X.t[:, dc, :], ALU.mult, ALU.add),
                    reads=[pm.b, X.b, C["btab"]], writes=[X.b])
            s.dma("sp", k.X2T[:, tx0:tx0 + NT].rearrange("(kc p) t -> p kc t", p=128), X.t[:], reads=[X.b])
        s.barrier()


def build(SEQ=8192, debug=(), upto=99):
    nc = bass.Bass("TRN2", target_bir_lowering=False)
    es = ExitStack()
    k = K(nc, es, SEQ, debug)
    s = k.s
    NTOK = k.NTOK
    k.xt = k.din("xt", [D, NTOK])
    k.cvec = k.din("cvec", [128, 8, 2])
    k.w_ada = k.din("w_ada", [D, NMOD * D])
    k.b_ada = k.din("b_ada", [128, 72])
    k.nw = k.din("nw", [128, 4, 8])
    k.ffn1_w_in = k.din("ffn1_w_in", [D, 2 * DFF])
    k.ffn1_w_out = k.din("ffn1_w_out", [DFF, D])
    k.ffn2_w_in = k.din("ffn2_w_in", [D, 2 * DFF])
    k.ffn2_w_out = k.din("ffn2_w_out", [DFF, D])
    k.cst = k.din("cst", [128, 2, 128])
    k.w_mix_in = k.din("w_mix_in", [D, MIX_IN])
    k.convw = k.din("convw", [128, 24, 5])
    k.cst2 = k.din("cst2", [128, 3, 128])
    k.adt = k.din("adt", [1, 32])
    k.masks = k.din("masks", [64, 4, 8, 64])
    k.identrep = k.din("identrep", [64, 8, 64])
    mats, k.opm_index = pool_operators(SEQ)
    k.opm_n = mats.shape[0]
    k.opm = k.din("opm", [k.opm_n, 128, 128])
    k.poolw = k.din("poolw", [128, 4, 128])
    k.gnw = k.din("gnw", [128, 1])
    k.pscale = k.din("pscale", [128, 4])
    k.w_gdn_proj = k.din("w_gdn_proj", [D, D])
    k.w_pool_proj = k.din("w_pool_proj", [512, D])
    k.w_mix_out = k.din("w_mix_out", [D, D])
    k.X1T = k.dscr("X1T", [D, NTOK])
    k.PQKV = k.dscr("PQKV", [QKV, NTOK])
    k.ABT = k.dscr("ABT", [NTOK, 32])
    k.SGATE = k.dscr("SGATE", [D, SEQ])
    k.PPT = k.dscr("PPT", [SEQ, 512], BF16)
    k.X2T = k.dscr("X2T", [D, SEQ])
    k.GATES = k.dscr("GATES", [2 * D, SEQ])
    NCH = NTOK // CH
    k.QT = k.dscr("QT", [NCH, 128, 8, CH], BF16)
    k.KT = k.dscr("KT", [NCH, 128, 8, CH], BF16)
    k.KTOK = k.dscr("KTOK", [NTOK, 8, 128], BF16)
    k.VTOK = k.dscr("VTOK", [NTOK, 8, 128], BF16)
    k.AUXT = k.dscr("AUXT", [NTOK, 5, 16])
    k.BLT = k.dscr("BLT", [NTOK, 16])
    k.AUXR = k.dscr("AUXR", [3, 8, 2, NTOK])
    k.OT = k.dscr("OT", [2, SEQ // CH, 128, 8, CH])
    k.outT = nc.dram_tensor("outT", [D, SEQ], F32, kind="ExternalOutput").ap()
    C = {}
    cst = k.sb(es, [128, 2, 128], F32, "cst")
    C["bconst"] = Buf()
    s.dma("sp", cst[:], k.cst, writes=[C["bconst"]])
    C["ones"] = cst[:, 0, :]
    C["ident"] = cst[:, 1, :]
    C["eps"] = k.sb(es, [128, 2], F32, "eps")
    s.op("dve", lambda e: e.memset(C["eps"][:], EPS), writes=[C["bconst"]])
    C["tab"] = k.sb(es, [128, 2, 9, 8], F32, "tab")
    C["fnw"] = k.sb(es, [128, 8], F32, "fnw")
    C["btab"] = Buf()

    x_tiles = [(0, 0, 1)] + [(CTX + i * NT, CTX + i * NT, 0) for i in range(SEQ // NT)]
    pass0_mod(k, C)
    if upto >= 1:
        ffn_pass(k, C, k.xt, k.X1T, k.ffn1_w_in, k.ffn1_w_out, 0, x_tiles)
    if upto >= 2:
        pass2_mixin(k, C, x_tiles)
    if upto >= 3:
        pass3a_gdnprep(k, C, x_tiles)
    if upto >= 4:
        pass3b_scan(k, C)
    if upto >= 5:
        pass4_merge(k, C)
    if upto >= 6:
        tiles5 = [(i * NT, i * NT, 0) for i in range(SEQ // NT)]
        ffn_pass(k, C, k.X2T, None, k.ffn2_w_in, k.ffn2_w_out, 2, tiles5, final_out=k.outT)
    s.barrier()
    es.close()
    return nc, k


def host_inputs(inp, b, SEQ):
    f = np.float32
    x = np.asarray(inp["x"], f)[b, :SEQ]
    ctx = np.asarray(inp["ctx"], f)[b]
    m = {}
    m["xt"] = np.ascontiguousarray(np.concatenate([ctx.T, x.T], axis=1))
    cv = np.stack([np.asarray(inp["c"], f)[b], np.asarray(inp["c_ctx"], f)], axis=-1)
    m["cvec"] = np.ascontiguousarray(cv.reshape(8, 128, 2).transpose(1, 0, 2))
    m["w_ada"] = np.ascontiguousarray(np.asarray(inp["w_ada"], f)[0])
    m["b_ada"] = np.ascontiguousarray(np.asarray(inp["b_ada"], f)[0].reshape(72, 128).T)
    nws = np.stack([np.asarray(inp["norm1_w"], f)[0], np.asarray(inp["norm2_w"], f)[0],
                    np.asarray(inp["norm3_w"], f)[0], np.asarray(inp["final_norm_w"], f)], axis=0)
    m["nw"] = np.ascontiguousarray(nws.reshape(4, 8, 128).transpose(2, 0, 1))
    m["opm"] = pool_operators(SEQ)[0]
    m["poolw"] = np.ascontiguousarray(np.asarray(inp["pool_w"], f)[0].transpose(1, 0, 2))
    m["gnw"] = np.asarray(inp["gdn_norm_w"], f)[0].reshape(128, 1).copy()
    m["pscale"] = np.ascontiguousarray(np.asarray(inp["pool_scale"], f)[0].reshape(4, 128).T)
    for nm in ["ffn1_w_in", "ffn1_w_out", "ffn2_w_in", "ffn2_w_out", "w_mix_in", "w_gdn_proj", "w_pool_proj",
               "w_mix_out"]:
        m[nm] = np.ascontiguousarray(np.asarray(inp[nm], f)[0])
    cw = np.asarray(inp["conv_w"], f)[0]
    m["convw"] = np.ascontiguousarray(cw.reshape(5, 24, 128).transpose(2, 1, 0))
    m["adt"] = np.concatenate([np.asarray(inp["a_log"], f)[0].reshape(-1),
                               np.asarray(inp["dt_bias"], f)[0].reshape(-1)])[None, :].copy()
    jj = np.arange(128)
    same = (jj[:, None] // 64) == (jj[None, :] // 64)
    c2 = np.zeros((128, 3, 128), f)
    c2[:, 0, :] = same & (jj[:, None] <= jj[None, :])
    c2[:, 1, :] = same & (jj[:, None] >= jj[None, :])
    c2[:, 2, :] = same
    m["cst2"] = c2
    pp = np.arange(64)[:, None]
    ff = np.arange(64)[None, :]
    mk = np.stack([ff >= pp, ff > pp, ff <= pp, ff < pp], 0)
    mk = np.where(mk, 0.0, NEG).astype(f)
    m["masks"] = np.ascontiguousarray(np.broadcast_to(mk.transpose(1, 0, 2)[:, :, None, :], (64, 4, 8, 64)))
    m["identrep"] = np.ascontiguousarray(np.broadcast_to(np.eye(64, dtype=f)[:, None, :], (64, 8, 64)))
    cst = np.zeros((128, 2, 128), f)
    cst[:, 0, :] = 1.0
    cst[:, 1, :] = np.eye(128, dtype=f)
    m["cst"] = cst
    return m


def kernel(**inputs):
    SEQ = int(np.asarray(inputs["x"]).shape[1])
    B = int(np.asarray(inputs["x"]).shape[0])
    nc, k = build(SEQ)
    shared = None
    in_maps = []
    for b in range(B):
        m = host_inputs(inputs, b, SEQ)
        if shared is None:
            shared = {n: v for n, v in m.items() if n not in ("xt", "cvec")}
        else:
            for n in shared:
                m[n] = shared[n]
        in_maps.append({n: v for n, v in m.items() if n in k.dram_in})
    res = run_bass_kernel_spmd(nc, in_maps, core_ids=list(range(B)))
    out = np.stack([np.asarray(res.results[b]["outT"]).T for b in range(B)], axis=0)
    return np.ascontiguousarray(out.astype(np.float32))
```

```python
import numpy as np
import ml_dtypes
from contextlib import ExitStack
import concourse.bass as bass
import concourse.mybir as mybir
from concourse.bass_utils import run_bass_kernel_spmd

F32 = mybir.dt.float32
BF16 = mybir.dt.bfloat16
AF = mybir.ActivationFunctionType
ALU = mybir.AluOpType

D = 1024
DFF = 2816
NFF = DFF // 128
NMOD = 9
CTX = 256
H = 8
HD = 128
CH = 64
QKV = 3072
AB_END = 3104
GATE_END = 4128
POOL_END = 4640
MIX_IN = 6688
NT = 256
EPS = 1e-6
GRID_W = 64
NEG = -30000.0


class Buf:
    __slots__ = ("w", "r", "name")

    def __init__(self, name=""):
        self.w = {}
        self.r = {}
        self.name = name


class Sched:
    NDS = 8

    def __init__(self, nc, es):
        self.nc = nc
        self.engs = {"pe": nc.tensor, "dve": nc.vector, "act": nc.scalar, "pool": nc.gpsimd, "sp": nc.sync}
        self.sems = {}
        self.cnt = {}
        for k in ["pe", "dve", "act", "pool"]:
            self.sems[k] = es.enter_context(nc.semaphore("s_" + k))
            self.cnt[k] = 0
        self.dq = {}
        for q in ["sp", "pool", "act"]:
            lst = []
            for i in range(self.NDS):
                key = "d_%s%d" % (q, i)
                self.sems[key] = es.enter_context(nc.semaphore(key))
                self.cnt[key] = 0
                lst.append(key)
            self.dq[q] = [lst, 0]
        self.seen = {e: {} for e in self.engs}
        self.nwaits = 0
        self.nins = 0

    def _wait(self, e, key, val, raw=True):
        if key == e:
            if e == "pe" or not raw or self.cnt[e] - val >= 2:
                return
        if self.seen[e].get(key, 0) >= val:
            return
        self.engs[e].wait_ge(self.sems[key], val)
        self.seen[e][key] = val
        self.nwaits += 1

    def _deps(self, e, reads, writes):
        for b in reads:
            for key, val in b.w.items():
                self._wait(e, key, val)
        for b in writes:
            for key, val in b.w.items():
                self._wait(e, key, val, raw=False)
            for key, val in b.r.items():
                self._wait(e, key, val, raw=False)

    def _done(self, key, val, reads, writes):
        for b in reads:
            b.r[key] = val
        for b in writes:
            b.w[key] = val
            b.r = {}

    def op(self, e, fn, reads=(), writes=()):
        self._deps(e, reads, writes)
        ins = fn(self.engs[e])
        self.cnt[e] += 1
        ins.then_inc(self.sems[e], 1)
        self._done(e, self.cnt[e], reads, writes)
        self.nins += 1

    def mm(self, mms, reads=(), writes=()):
        self._deps("pe", reads, writes)
        ins = None
        for (o, l, r, st, sp) in mms:
            ins = self.nc.tensor.matmul(o, l, r, start=st, stop=sp)
        self.cnt["pe"] += 1
        ins.then_inc(self.sems["pe"], 1)
        self._done("pe", self.cnt["pe"], reads, writes)
        self.nins += len(mms)

    def tr(self, trs, reads=(), writes=()):
        self._deps("pe", reads, writes)
        ins = None
        for (o, i, idn) in trs:
            ins = self.nc.tensor.transpose(o, i, idn)
        self.cnt["pe"] += 1
        ins.then_inc(self.sems["pe"], 1)
        self._done("pe", self.cnt["pe"], reads, writes)
        self.nins += len(trs)

    def dma(self, q, out, in_, reads=(), writes=(), **kw):
        self._deps(q, reads, writes)
        lst, i = self.dq[q]
        key = lst[i % self.NDS]
        self.dq[q][1] = i + 1
        if self.cnt[key] > 0:
            self._wait(q, key, self.cnt[key])
        ins = self.engs[q].dma_start(out=out, in_=in_, **kw)
        self.cnt[key] += 16
        ins.then_inc(self.sems[key], 16)
        self._done(key, self.cnt[key], reads, writes)
        self.nins += 1

    def barrier(self):
        for e in self.engs:
            for key in self.sems:
                if self.cnt[key] > 0:
                    self._wait(e, key, self.cnt[key])


class K:
    def __init__(self, nc, es, SEQ, debug=()):
        self.nc = nc
        self.es = es
        self.s = Sched(nc, es)
        self.SEQ = SEQ
        self.NTOK = CTX + SEQ
        self.debug = set(debug)
        self.uid = 0
        self.dram_in = {}

    def name(self, p):
        self.uid += 1
        return "%s_%d" % (p, self.uid)

    def sb(self, ctx, shape, dt, p="t"):
        return ctx.enter_context(self.nc.sbuf_tensor(self.name(p), list(shape), dt))

    def ps(self, ctx, shape, dt, p="ps"):
        return ctx.enter_context(self.nc.psum_tensor(self.name(p), list(shape), dt))

    def din(self, name, shape, dt=F32):
        t = self.nc.dram_tensor(name, list(shape), dt, kind="ExternalInput")
        self.dram_in[name] = t
        return t.ap()

    def dscr(self, name, shape, dt=F32):
        kind = "ExternalOutput" if name in self.debug else "Internal"
        return self.nc.dram_tensor(name, list(shape), dt, kind=kind).ap()


def bc_mid(ap2, n):
    P, Fd = ap2.shape
    return ap2.unsqueeze(1).broadcast_to([P, n, Fd])


def bc_last(ap2, n):
    P, Fd = ap2.shape
    return ap2.unsqueeze(2).broadcast_to([P, Fd, n])


def load_weight_bf16(k, dst3, w_dram, nk, pieces=1):
    cols = w_dram.shape[1]
    step = (cols + pieces - 1) // pieces
    wb = Buf()
    for kc in range(nk):
        for c0 in range(0, cols, step):
            c1 = min(cols, c0 + step)
            k.s.dma("pool", dst3[:, kc, c0:c1], w_dram[kc * 128:(kc + 1) * 128, c0:c1], writes=[wb])
    return wb


def pass0_mod(k, C):
    s = k.s
    nc = k.nc
    tab = C["tab"]
    btab = C["btab"]
    with ExitStack() as cx:
        sc = k.sb(cx, [128, 8, 2], F32)
        bT = k.sb(cx, [128, 72], F32)
        nw = k.sb(cx, [128, 4, 8], F32)
        mod = k.sb(cx, [128, 2, 72], F32)
        wa = [k.sb(cx, [128, 8, 1024], F32) for _ in range(2)]
        pm = k.ps(cx, [128, 72, 2], F32)
        bsc, bbT, bnw, bmod, bpm = Buf(), Buf(), Buf(), Buf(), Buf()
        bwa = [Buf(), Buf()]
        s.dma("sp", sc[:], k.cvec, writes=[bsc])
        s.dma("sp", bT[:], k.b_ada, writes=[bbT])
        s.dma("sp", nw[:], k.nw, writes=[bnw])
        s.op("act", lambda e: e.activation(sc[:], sc[:], AF.Silu), reads=[bsc], writes=[bsc])
        for j in range(NMOD):
            w = wa[j % 2]
            s.dma("sp", w[:], k.w_ada[:, j * 1024:(j + 1) * 1024].rearrange("(kc p) c -> p kc c", p=128),
                  writes=[bwa[j % 2]])
            mms = []
            for dc in range(8):
                for kc in range(8):
                    mms.append((pm[:, j * 8 + dc, :], w[:, kc, dc * 128:(dc + 1) * 128], sc[:, kc, :],
                                kc == 0, kc == 7))
            s.mm(mms, reads=[bwa[j % 2], bsc], writes=[bpm])
        for wch in range(2):
            s.op("dve", lambda e, wch=wch: e.tensor_tensor(mod[:, wch, :], pm[:, :, wch], bT[:], ALU.add),
                 reads=[bpm, bbT], writes=[bmod])
        for wch in range(2):
            for sub, (jsh, jsc, jg, half) in enumerate([(0, 1, 2, True), (3, 4, 5, False), (6, 7, 8, True)]):
                s.op("dve", lambda e, wch=wch, sub=sub, jsc=jsc: e.scalar_tensor_tensor(
                    tab[:, wch, sub * 3 + 0, :], mod[:, wch, jsc * 8:(jsc + 1) * 8], 1.0, nw[:, sub, :],
                    ALU.add, ALU.mult), reads=[bmod, bnw], writes=[btab])
                s.op("dve", lambda e, wch=wch, sub=sub, jsh=jsh: e.tensor_copy(
                    tab[:, wch, sub * 3 + 1, :], mod[:, wch, jsh * 8:(jsh + 1) * 8]), reads=[bmod], writes=[btab])
                s.op("dve", lambda e, wch=wch, sub=sub, jg=jg, half=half: e.tensor_scalar(
                    tab[:, wch, sub * 3 + 2, :], mod[:, wch, jg * 8:(jg + 1) * 8], 0.5 if half else 1.0, None,
                    ALU.mult), reads=[bmod], writes=[btab])
        s.op("dve", lambda e: e.tensor_copy(C["fnw"][:], nw[:, 3, :]), reads=[bnw], writes=[btab])
        s.barrier()


def ffn_pass(k, C, src, dst, w_in, w_out, sub, tiles, final_out=None):
    s = k.s
    tab = C["tab"]
    btab = C["btab"]
    with ExitStack() as cx:
        Win = k.sb(cx, [128, 8, 2 * DFF], BF16, "win")
        Wout = k.sb(cx, [128, NFF, D], BF16, "wout")
        bWin = load_weight_bf16(k, Win, w_in, 8, pieces=2)
        bWout = load_weight_bf16(k, Wout, w_out, NFF)
        xTs = [k.sb(cx, [128, 8, NT], F32, "xT") for _ in range(2)]
        bxT = [Buf(), Buf()]
        sq = k.sb(cx, [128, 8, NT], F32, "sq")
        bsq = Buf()
        hT = k.sb(cx, [128, 8, NT], BF16, "hT")
        bhT = Buf()
        aT = k.sb(cx, [128, NFF, NT], BF16, "aT")
        baT = Buf()
        rstd = k.sb(cx, [128, NT], F32, "rstd")
        brstd = Buf()
        pss = k.ps(cx, [128, 512], F32)
        bpss = Buf()
        NPB = 4
        pgu = [k.ps(cx, [128, 512], F32) for _ in range(NPB)]
        pys = [k.ps(cx, [128, 512], F32) for _ in range(2)]
        bpgu = [Buf() for _ in range(NPB)]
        bpy = [Buf(), Buf()]
        sg = [k.sb(cx, [128, NT], F32, "sg") for _ in range(NPB)]
        bsg = [Buf() for _ in range(NPB)]
        ones = C["ones"]
        epsb = C["eps"]

        def rms(xT, bx):
            s.op("act", lambda e: e.activation(sq[:], xT[:], AF.Square), reads=[bx], writes=[bsq])
            s.mm([(pss[:, 0:NT], ones[:], sq[:, kc, :], kc == 0, kc == 7) for kc in range(8)],
                 reads=[bsq, C["bconst"]], writes=[bpss])
            s.op("act", lambda e: e.activation(rstd[:], pss[:, 0:NT], AF.Sqrt, bias=epsb[:, 0:1], scale=1.0 / D),
                 reads=[bpss, C["bconst"]], writes=[brstd])
            s.op("dve", lambda e: e.reciprocal(rstd[:], rstd[:]), reads=[brstd], writes=[brstd])

        for ti, (t0s, t0d, wch) in enumerate(tiles):
            xT = xTs[ti % 2]
            bx = bxT[ti % 2]
            A = tab[:, wch, sub * 3 + 0, :]
            B = tab[:, wch, sub * 3 + 1, :]
            HG = tab[:, wch, sub * 3 + 2, :]
            s.dma("sp", xT[:], src[:, t0s:t0s + NT].rearrange("(kc p) t -> p kc t", p=128), writes=[bx])
            rms(xT, bx)
            s.op("dve", lambda e: e.tensor_tensor(sq[:], xT[:], bc_mid(rstd[:], 8), ALU.mult),
                 reads=[bx, brstd], writes=[bsq])
            s.op("pool", lambda e: e.tensor_tensor(sq[:], sq[:], bc_last(A, NT), ALU.mult),
                 reads=[bsq, btab], writes=[bsq])
            s.op("pool", lambda e: e.tensor_tensor(hT[:], sq[:], bc_last(B, NT), ALU.add),
                 reads=[bsq, btab], writes=[bhT])
            for m in range(NFF):
                pp = pgu[m % NPB]
                bpp = bpgu[m % NPB]
                s.mm([(pp[:, 0:NT], Win[:, kc, m * 128:(m + 1) * 128], hT[:, kc, :], kc == 0, kc == 7)
                      for kc in range(8)] +
                     [(pp[:, NT:2 * NT], Win[:, kc, DFF + m * 128:DFF + (m + 1) * 128], hT[:, kc, :], kc == 0, kc == 7)
                      for kc in range(8)], reads=[bWin, bhT], writes=[bpp])
                s.op("act", lambda e, m=m, pp=pp: e.activation(sg[m % NPB][:], pp[:, 0:NT], AF.Silu),
                     reads=[bpp], writes=[bsg[m % NPB]])
                s.op("dve", lambda e, m=m, pp=pp: e.tensor_tensor(aT[:, m, :], sg[m % NPB][:], pp[:, NT:2 * NT], ALU.mult),
                     reads=[bsg[m % NPB], bpp], writes=[baT])
            for dc in range(8):
                py = pys[dc % 2]
                s.mm([(py[:, 0:NT], Wout[:, m, dc * 128:(dc + 1) * 128], aT[:, m, :], m == 0, m == NFF - 1)
                      for m in range(NFF)], reads=[bWout, baT], writes=[bpy[dc % 2]])
                s.op("dve", lambda e, dc=dc, py=py: e.scalar_tensor_tensor(
                    xT[:, dc, :], py[:, 0:NT], HG[:, dc:dc + 1], xT[:, dc, :], ALU.mult, ALU.add),
                    reads=[bpy[dc % 2], bx, btab], writes=[bx])
            if dst is not None:
                s.dma("sp", dst[:, t0d:t0d + NT].rearrange("(kc p) t -> p kc t", p=128), xT[:], reads=[bx])
            if final_out is not None:
                rms(xT, bx)
                s.op("dve", lambda e: e.tensor_tensor(sq[:], xT[:], bc_mid(rstd[:], 8), ALU.mult),
                     reads=[bx, brstd], writes=[bsq])
                s.op("pool", lambda e: e.tensor_tensor(sq[:], sq[:], bc_last(C["fnw"][:], NT), ALU.mult),
                     reads=[bsq, btab], writes=[bsq])
                s.dma("sp", final_out[:, t0d:t0d + NT].rearrange("(kc p) t -> p kc t", p=128), sq[:], reads=[bsq])
        s.barrier()


def rms_mod(k, C, cx_bufs, xT, bx, A, B, outT, bout):
    s = k.s
    sq, bsq, rstd, brstd, pss, bpss = cx_bufs
    s.op("act", lambda e: e.activation(sq[:], xT[:], AF.Square), reads=[bx], writes=[bsq])
    s.mm([(pss[:, 0:NT], C["ones"][:], sq[:, kc, :], kc == 0, kc == 7) for kc in range(8)],
         reads=[bsq, C["bconst"]], writes=[bpss])
    s.op("act", lambda e: e.activation(rstd[:], pss[:, 0:NT], AF.Sqrt, bias=C["eps"][:, 0:1], scale=1.0 / D),
         reads=[bpss, C["bconst"]], writes=[brstd])
    s.op("dve", lambda e: e.reciprocal(rstd[:], rstd[:]), reads=[brstd], writes=[brstd])
    s.op("dve", lambda e: e.tensor_tensor(sq[:], xT[:], bc_mid(rstd[:], 8), ALU.mult),
         reads=[bx, brstd], writes=[bsq])
    s.op("pool", lambda e: e.tensor_tensor(sq[:], sq[:], bc_last(A, NT), ALU.mult),
         reads=[bsq, C["btab"]], writes=[bsq])
    s.op("pool", lambda e: e.tensor_tensor(outT[:], sq[:], bc_last(B, NT), ALU.add),
         reads=[bsq, C["btab"]], writes=[bout])


def pass2_mixin(k, C, tiles):
    s = k.s
    tab = C["tab"]
    with ExitStack() as cx:
        Wm = k.sb(cx, [128, 8, MIX_IN], BF16, "wmix")
        bWm = load_weight_bf16(k, Wm, k.w_mix_in, 8, pieces=2)
        xTs = [k.sb(cx, [128, 8, NT], F32, "xT") for _ in range(2)]
        bxT = [Buf(), Buf()]
        sq = k.sb(cx, [128, 8, NT], F32, "sq")
        rstd = k.sb(cx, [128, NT], F32, "rstd")
        uT = k.sb(cx, [128, 8, NT], BF16, "uT")
        buT = Buf()
        pss = k.ps(cx, [128, 512], F32)
        cxb = (sq, Buf(), rstd, Buf(), pss, Buf())
        NPQ = 4
        pqb = [k.ps(cx, [128, 512], F32) for _ in range(NPQ)]
        pq = [pqb[i][:, 0:NT] for i in range(NPQ)]
        bpq = [Buf() for _ in range(NPQ)]
        ppl = [k.ps(cx, [128, 512], F32) for _ in range(2)]
        bppl = [Buf(), Buf()]
        pab = k.ps(cx, [128, 512], F32)
        bpab = Buf()
        stg = [k.sb(cx, [128, 4, NT], F32, "stg") for _ in range(3)]
        bstg = [Buf(), Buf(), Buf()]
        abs_ = k.sb(cx, [128, 2, 32], F32, "abs")
        babs = Buf()
        pps = k.sb(cx, [128, 2, 512], BF16, "pps")
        bpps = Buf()
        nst = 0
        nq = 0
        for ti, (t0, _, wch) in enumerate(tiles):
            xT = xTs[ti % 2]
            bx = bxT[ti % 2]
            s.dma("sp", xT[:], k.X1T[:, t0:t0 + NT].rearrange("(kc p) t -> p kc t", p=128), writes=[bx])
            rms_mod(k, C, cxb, xT, bx, tab[:, wch, 3, :], tab[:, wch, 4, :], uT, buT)
            if True:
              s.mm([(pab[:, blk * 32:(blk + 1) * 32], uT[:, kc, blk * 128:(blk + 1) * 128], Wm[:, kc, QKV:AB_END],
                   kc == 0, kc == 7) for blk in range(2) for kc in range(8)], reads=[buT, bWm], writes=[bpab])
              s.op("dve", lambda e: e.tensor_copy(abs_[:], pab[:, 0:64].rearrange("p (b c) -> p b c", b=2)),
                 reads=[bpab], writes=[babs])
              s.dma("sp", k.ABT[t0:t0 + NT, :].rearrange("(b p) c -> p b c", p=128), abs_[:], reads=[babs])
            groups = [(0, 24, AF.Copy, k.PQKV, 0, t0)]
            if wch == 0:
                tx = t0 - CTX
                gsel = "123"
                allg = [(AB_END, 8, AF.Silu, k.SGATE, 0, tx), None,
                           (POOL_END, 16, AF.Sigmoid, k.GATES, 0, tx)]
                groups += [allg[int(ch) - 1] for ch in gsel if ch != "2"]
                for blk in range(2):
                    pp = ppl[blk]
                    bp = bppl[blk]
                    s.mm([(pp[:, :], uT[:, kc, blk * 128:(blk + 1) * 128], Wm[:, kc, GATE_END:POOL_END],
                           kc == 0, kc == 7) for kc in range(8)], reads=[buT, bWm], writes=[bp])
                    s.op("act", lambda e, pp=pp, blk=blk: e.activation(pps[:, blk, :], pp[:, :], AF.Copy),
                         reads=[bp], writes=[bpps])
                s.dma("sp", k.PPT[tx:tx + NT, :].rearrange("(b p) c -> p b c", p=128), pps[:], reads=[bpps])
            for (col0, nch, func, dram, row0, tq) in groups:
                for c0 in range(0, nch, 4):
                    st = stg[nst % 3]
                    bst = bstg[nst % 3]
                    nst += 1
                    for c in range(c0, c0 + 4):
                        pp = pq[nq % NPQ]
                        bp = bpq[nq % NPQ]
                        nq += 1
                        cc = col0 + c * 128
                        s.mm([(pp, Wm[:, kc, cc:cc + 128], uT[:, kc, :], kc == 0, kc == 7)
                              for kc in range(8)], reads=[buT, bWm], writes=[bp])
                        s.op("act", lambda e, st=st, c=c, c0=c0, pp=pp, func=func: e.activation(
                            st[:, c - c0, :], pp, func), reads=[bp], writes=[bst])
                    r0 = row0 + c0 * 128
                    s.dma("sp", dram[r0:r0 + 512, tq:tq + NT].rearrange("(c p) t -> p c t", p=128), st[:],
                          reads=[bst])
        s.barrier()


def pass3a_gdnprep(k, C, tiles):
    s = k.s
    NTOK = k.NTOK
    with ExitStack() as cx:
        cw = k.sb(cx, [128, 24, 5], F32, "cw")
        bcw = Buf()
        s.dma("sp", cw[:], k.convw, writes=[bcw])
        c2 = k.sb(cx, [128, 3, 128], F32, "c2")
        s.dma("sp", c2[:], k.cst2, writes=[bcw])
        identb = k.sb(cx, [128, 128], BF16, "identb")
        s.op("dve", lambda e: e.tensor_copy(identb[:], C["ident"]), reads=[C["bconst"]], writes=[bcw])
        adt = k.sb(cx, [128, 32], F32, "adt")
        s.dma("sp", adt[:], k.adt.partition_broadcast(128), writes=[bcw])
        nega = k.sb(cx, [128, 16], F32, "nega")
        s.op("act", lambda e: e.activation(nega[:], adt[:, 0:16], AF.Exp), reads=[bcw], writes=[bcw])
        s.op("dve", lambda e: e.tensor_scalar(nega[:], nega[:], -1.0, None, ALU.mult), reads=[bcw], writes=[bcw])
        cbias = k.sb(cx, [128, 4], F32, "cbias")
        s.op("dve", lambda e: e.memset(cbias[:, 0:1], EPS), writes=[bcw])
        s.op("dve", lambda e: e.memset(cbias[:, 1:2], 128.0 * EPS), writes=[bcw])
        s.op("dve", lambda e: e.memset(cbias[:, 2:3], 1.0), writes=[bcw])

        W = NT + 4
        NBF = 3
        pin4 = [k.sb(cx, [128, 4, W], F32, "pin4") for _ in range(NBF)]
        bpin = [Buf() for _ in range(NBF)]
        acc4 = [k.sb(cx, [128, 4, NT], F32, "acc4") for _ in range(NBF)]
        bacc = [Buf() for _ in range(NBF)]
        sact4 = [k.sb(cx, [128, 4, NT], F32, "sact4") for _ in range(NBF)]
        bsact = [Buf() for _ in range(NBF)]
        sq4s = [k.sb(cx, [128, 4, NT], F32, "sq4") for _ in range(2)]
        bsq4s = [Buf(), Buf()]
        rn4s = [k.sb(cx, [128, 4, NT], F32, "rn4") for _ in range(2)]
        brn4s = [Buf(), Buf()]
        knf4 = [k.sb(cx, [128, 4, NT], BF16, "knf4") for _ in range(NBF)]
        bknf = [Buf() for _ in range(NBF)]
        QTs = [k.sb(cx, [128, 4, 8, 64], BF16, "QTs") for _ in range(2)]
        KTs = [k.sb(cx, [128, 4, 8, 64], BF16, "KTs") for _ in range(2)]
        KTOKs = [k.sb(cx, [128, 2, 8, 128], BF16, "KTOKs") for _ in range(2)]
        VTOKs = [k.sb(cx, [128, 2, 8, 128], BF16, "VTOKs") for _ in range(2)]
        bQTs, bKTs, bKTOKs, bVTOKs = [Buf(), Buf()], [Buf(), Buf()], [Buf(), Buf()], [Buf(), Buf()]
        pns = [k.ps(cx, [128, 4, NT], F32) for _ in range(2)]
        bpns = [Buf(), Buf()]
        ptr = [k.ps(cx, [128, 2, 4, 128], BF16) for _ in range(2)]
        bptr = [Buf(), Buf()]
        pa = k.ps(cx, [128, 512], F32)
        bpa = Buf()
        pr = k.ps(cx, [128, 512], F32)
        bpr = Buf()
        abt = [k.sb(cx, [128, 32], F32, "abt") for _ in range(2)]
        babt = [Buf(), Buf()]
        aux = [k.sb(cx, [128, 5, 16], F32, "aux") for _ in range(2)]
        baux = [Buf(), Buf()]
        lb = k.sb(cx, [128, 16], F32, "lb")
        gt = k.sb(cx, [128, 16], F32, "gt")
        cl = k.sb(cx, [128, 16], F32, "cl")
        bl = [k.sb(cx, [128, 16], F32, "bl") for _ in range(2)]
        bbl = [Buf(), Buf()]
        rowt = [k.sb(cx, [8, 3, 2, 128], F32, "rowt") for _ in range(2)]
        browt = [Buf(), Buf()]
        bsm = Buf()
        state = {"ng": 0, "nblk": 0, "ntr": 0}
        done_groups = {}

        def grp(ti, t0, wch, g4):
                seq_lo = 0 if wch == 1 else CTX
                seq_hi = CTX if wch == 1 else NTOK
                qts, kts, ktoks, vtoks = QTs[ti % 2], KTs[ti % 2], KTOKs[ti % 2], VTOKs[ti % 2]
                kind = g4 // 2
                h0 = (g4 % 2) * 4
                ng = state["ng"]
                state["ng"] += 1
                pin = pin4[ng % NBF]
                bp = bpin[ng % NBF]
                acc = acc4[ng % NBF]
                ba = bacc[ng % NBF]
                sact = sact4[ng % NBF]
                bs = bsact[ng % NBF]
                sq4, bsq4, rn4, brn4, pn, bpn = sq4s[ng % 2], bsq4s[ng % 2], rn4s[ng % 2], brn4s[ng % 2], pns[ng % 2], bpns[ng % 2]
                ng += 1
                lo = max(t0 - 2, seq_lo)
                hi = min(t0 + NT + 2, seq_hi)
                if lo > t0 - 2:
                    s.op("pool", lambda e, pin=pin: e.memset(pin[:, :, 0:2], 0.0), writes=[bp])
                if hi < t0 + NT + 2:
                    s.op("pool", lambda e, pin=pin: e.memset(pin[:, :, W - 2:W], 0.0), writes=[bp])
                s.dma("sp", pin[:, :, lo - (t0 - 2):hi - (t0 - 2)],
                      k.PQKV[g4 * 512:(g4 + 1) * 512, lo:hi].rearrange("(c p) t -> p c t", p=128), writes=[bp])
                bai = [Buf() for _ in range(4)]
                for i in range(4):
                    cc = g4 * 4 + i
                    s.op("act", lambda e, i=i, cc=cc, pin=pin, acc=acc: e.activation(
                        acc[:, i, :], pin[:, i, 0:NT], AF.Copy, scale=cw[:, cc, 0:1]), reads=[bp, bcw], writes=[ba, bai[i]])
                yield
                for kk in range(1, 5):
                    for i in range(4):
                        cc = g4 * 4 + i
                        s.op("dve", lambda e, i=i, cc=cc, kk=kk, pin=pin, acc=acc: e.scalar_tensor_tensor(
                            acc[:, i, :], pin[:, i, kk:kk + NT], cw[:, cc, kk:kk + 1], acc[:, i, :],
                            ALU.mult, ALU.add), reads=[bp, bcw, bai[i]], writes=[bai[i]] + ([ba] if kk == 4 else []))
                    yield
                if kind == 2:
                    vb = knf4[ng % NBF]
                    bvb = bknf[ng % NBF]
                    s.op("act", lambda e, acc=acc, vb=vb: e.activation(vb[:], acc[:], AF.Silu), reads=[ba], writes=[bvb])
                    yield
                    src_t, bsrc, dstk, bdst = vb, bvb, vtoks, bVTOKs[ti % 2]
                else:
                    s.op("act", lambda e, acc=acc, sact=sact: e.activation(sact[:], acc[:], AF.Silu),
                         reads=[ba], writes=[bs])
                    yield
                    s.op("act", lambda e, sact=sact: e.activation(sq4[:], sact[:], AF.Square), reads=[bs], writes=[bsq4])
                    yield
                    s.mm([(pn[:, i, :], C["ones"], sq4[:, i, :], True, True) for i in range(4)],
                         reads=[bsq4, C["bconst"]], writes=[bpn])
                    yield
                    if kind == 0:
                        s.op("act", lambda e: e.activation(rn4[:], pn[:], AF.Ln, bias=cbias[:, 1:2], scale=128.0),
                             reads=[bpn, bcw], writes=[brn4])
                    else:
                        s.op("act", lambda e: e.activation(rn4[:], pn[:], AF.Ln, bias=cbias[:, 0:1], scale=1.0),
                             reads=[bpn, bcw], writes=[brn4])
                    yield
                    s.op("act", lambda e: e.activation(rn4[:], rn4[:], AF.Exp, scale=-0.5), reads=[brn4], writes=[brn4])
                    yield
                    if kind == 0:
                        for i in range(4):
                            s.op("dve", lambda e, i=i, sact=sact: e.tensor_tensor(
                                qts[:, :, h0 + i, :], sact[:, i, :].rearrange("p (c t) -> p c t", c=4),
                                rn4[:, i, :].rearrange("p (c t) -> p c t", c=4), ALU.mult),
                                reads=[bs, brn4], writes=[bQTs[ti % 2]])
                        done_groups[ti] = done_groups.get(ti, 0) + 1
                        return
                    kn = knf4[ng % NBF]
                    bkn = bknf[ng % NBF]
                    s.op("dve", lambda e, sact=sact, kn=kn: e.tensor_tensor(kn[:], sact[:], rn4[:], ALU.mult),
                         reads=[bs, brn4], writes=[bkn])
                    for i in range(4):
                        s.op("pool", lambda e, i=i, kn=kn: e.tensor_copy(
                            kts[:, :, h0 + i, :], kn[:, i, :].rearrange("p (c t) -> p c t", c=4)),
                            reads=[bkn], writes=[bKTs[ti % 2]])
                    yield
                    src_t, bsrc, dstk, bdst = kn, bkn, ktoks, bKTOKs[ti % 2]
                pt = ptr[state["ntr"] % 2]
                bpt = bptr[state["ntr"] % 2]
                state["ntr"] += 1
                s.tr([(pt[:, blk, i, :], src_t[:, i, blk * 128:(blk + 1) * 128], identb[:])
                      for blk in range(2) for i in range(4)], reads=[bsrc, bcw], writes=[bpt])
                yield
                s.op("act", lambda e, pt=pt, dstk=dstk: e.activation(dstk[:, :, h0:h0 + 4, :], pt[:], AF.Copy),
                     reads=[bpt], writes=[bdst])
                done_groups[ti] = done_groups.get(ti, 0) + 1

        def tail(ti, t0, wch):
            qts, kts, ktoks, vtoks = QTs[ti % 2], KTs[ti % 2], KTOKs[ti % 2], VTOKs[ti % 2]
            while done_groups.get(ti, 0) < 6:
                yield
            c0 = t0 // CH
            s.dma("sp", k.QT[c0:c0 + 4].rearrange("c d h t -> d c (h t)"), qts[:].rearrange("d c h t -> d c (h t)"),
                  reads=[bQTs[ti % 2]])
            s.dma("sp", k.KT[c0:c0 + 4].rearrange("c d h t -> d c (h t)"), kts[:].rearrange("d c h t -> d c (h t)"),
                  reads=[bKTs[ti % 2]])
            s.dma("sp", k.KTOK[t0:t0 + NT].rearrange("(b p) h d -> p b (h d)", p=128),
                  ktoks[:].rearrange("p b h d -> p b (h d)"), reads=[bKTOKs[ti % 2]])
            s.dma("sp", k.VTOK[t0:t0 + NT].rearrange("(b p) h d -> p b (h d)", p=128),
                  vtoks[:].rearrange("p b h d -> p b (h d)"), reads=[bVTOKs[ti % 2]])
            for blk in range(2):
                tb = t0 + blk * 128
                nblk = state["nblk"]
                state["nblk"] += 1
                ab = abt[nblk % 2]
                bab = babt[nblk % 2]
                ax = aux[nblk % 2]
                bax = baux[nblk % 2]
                blt = bl[nblk % 2]
                bblt = bbl[nblk % 2]
                rw = rowt[nblk % 2]
                brw = browt[nblk % 2]
                yield
                s.dma("sp", ab[:], k.ABT[tb:tb + 128, :], writes=[bab])
                s.op("act", lambda e, ab=ab, ax=ax: e.activation(ax[:, 2, :], ab[:, 0:16], AF.Sigmoid),
                     reads=[bab], writes=[bax])
                s.op("act", lambda e, ax=ax: e.activation(lb[:], ax[:, 2, :], AF.Ln), reads=[bax], writes=[bsm])
                s.op("dve", lambda e, ab=ab: e.tensor_tensor(gt[:], ab[:, 16:32], adt[:, 16:32], ALU.add),
                     reads=[bab, bcw, bsm], writes=[bsm])
                s.op("act", lambda e: e.activation(gt[:], gt[:], AF.Exp), reads=[bsm], writes=[bsm])
                s.op("act", lambda e: e.activation(gt[:], gt[:], AF.Ln, bias=cbias[:, 2:3], scale=1.0),
                     reads=[bsm, bcw], writes=[bsm])
                s.op("dve", lambda e: e.tensor_tensor(gt[:], gt[:], nega[:], ALU.mult), reads=[bsm, bcw], writes=[bsm])
                s.mm([(pa[:, 0:8], c2[:, 0, :], gt[:, 0:8], True, True),
                      (pa[:, 8:16], c2[:, 1, :], gt[:, 8:16], True, True),
                      (pa[:, 16:32], c2[:, 2, :], gt[:, 0:16], True, True)], reads=[bsm, bcw], writes=[bpa])
                s.mm([(pr[0:8, 0:128], gt[:, 0:8], c2[:, 0, :], True, True),
                      (pr[0:8, 128:256], gt[:, 8:16], c2[:, 1, :], True, True),
                      (pr[0:8, 256:384], gt[:, 0:8], c2[:, 0, :], True, False),
                      (pr[0:8, 256:384], lb[:, 0:8], C["ident"], False, True),
                      (pr[0:8, 384:512], gt[:, 8:16], c2[:, 1, :], True, False),
                      (pr[0:8, 384:512], lb[:, 8:16], C["ident"], False, True)],
                     reads=[bsm, bcw, C["bconst"]], writes=[bpr])
                s.op("dve", lambda e, ax=ax: e.tensor_copy(ax[:, 0, :], pa[:, 0:16]), reads=[bpa], writes=[bax])
                s.op("dve", lambda e: e.tensor_copy(cl[:], pa[:, 16:32]), reads=[bpa, bsm], writes=[bsm])
                s.op("dve", lambda e, ax=ax: e.tensor_tensor(ax[:, 1, :], ax[:, 0, :], lb[:], ALU.add),
                     reads=[bax, bsm], writes=[bax])
                s.op("act", lambda e, ax=ax: e.activation(ax[:, 3, :], ax[:, 1, :], AF.Exp), reads=[bax], writes=[bax])
                s.op("dve", lambda e, ax=ax: e.tensor_tensor(ax[:, 4, :], cl[:], ax[:, 0, :], ALU.subtract),
                     reads=[bax, bsm], writes=[bax])
                s.op("act", lambda e, ax=ax: e.activation(ax[:, 4, :], ax[:, 4, :], AF.Exp), reads=[bax], writes=[bax])
                s.op("act", lambda e, blt=blt: e.activation(blt[:], cl[:], AF.Exp), reads=[bsm], writes=[bblt])
                s.op("dve", lambda e, rw=rw: e.tensor_copy(rw[:, 0:2, :, :].rearrange("p a b t -> p (a b t)"),
                                                          pr[0:8, :]), reads=[bpr], writes=[brw])
                s.op("act", lambda e, rw=rw: e.activation(rw[:, 2, :, :].rearrange("p b t -> p (b t)"),
                                                         pr[0:8, 0:256], AF.Exp), reads=[bpr], writes=[brw])
                s.dma("sp", k.AUXT[tb:tb + 128], ax[:], reads=[bax])
                s.dma("sp", k.BLT[tb:tb + 128], blt[:], reads=[bblt])
                for a in range(3):
                    s.dma("sp", k.AUXR[a, :, :, tb:tb + 128], rw[:, a, :, :], reads=[brw])
                yield

        items = []
        for ti, (t0, _, wch) in enumerate(tiles):
            for g4 in range(6):
                items.append(grp(ti, t0, wch, g4))
            items.append(tail(ti, t0, wch))
        active = []
        pos = 0
        WIN = 2
        while pos < len(items) or active:
            while len(active) < WIN and pos < len(items):
                active.append(items[pos])
                pos += 1
            for g in list(active):
                try:
                    next(g)
                except StopIteration:
                    active.remove(g)
        s.barrier()


class TB:
    def __init__(self, t):
        self.t = t
        self.b = Buf()


def pass3b_scan(k, C):
    s = k.s
    NTOK = k.NTOK
    NCH = NTOK // CH
    NCC = CTX // CH
    with ExitStack() as cx:
        def sbt(shape, dt, p="t"):
            return TB(k.sb(cx, shape, dt, p))

        msk = sbt([64, 4, 8, 64], F32, "msk")
        s.dma("sp", msk.t[:], k.masks, writes=[msk.b])
        idr = sbt([64, 8, 64], F32, "idr")
        s.dma("sp", idr.t[:], k.identrep, writes=[idr.b])
        S32 = [sbt([128, 8, 128], F32, "S32") for _ in range(2)]
        Sb = [sbt([128, 8, 128], BF16, "Sb") for _ in range(2)]
        for d in range(2):
            s.op("dve", lambda e, d=d: e.memset(S32[d].t[:], 0.0), writes=[S32[d].b])
            s.op("pool", lambda e, d=d: e.memset(Sb[d].t[:], 0.0), writes=[Sb[d].b])

        NSET = 2

        def mk():
            return dict(
                qT=sbt([128, 8, 64], BF16, "qT"), kT=sbt([128, 8, 64], BF16, "kT"),
                ktok=sbt([64, 8, 128], BF16, "ktok"), vtok=sbt([64, 8, 128], BF16, "vtok"),
                axt=sbt([64, 5, 16], F32, "axt"), R1=sbt([64, 8, 64], F32, "R1"), R2=sbt([64, 8, 64], F32, "R2"),
                EC=sbt([128, 8, 64], F32, "EC"), BL=sbt([128, 8], F32, "BL"),
                EA=sbt([64, 8, 64], F32, "EA"), EM=sbt([64, 8, 64], F32, "EM"), EL=sbt([64, 8, 64], F32, "EL"),
                aqk=sbt([64, 8, 64], BF16, "aqk"),
                Mp=[sbt([64, 8, 64], F32, "Mp") for _ in range(2)],
                Lp=[sbt([64, 8, 64], F32, "Lp") for _ in range(2)],
                P=[sbt([64, 8, 64], F32, "P") for _ in range(2)],
                TT=sbt([64, 8, 64], BF16, "TT"),
                bv=sbt([64, 8, 128], BF16, "bv"), kb=sbt([64, 8, 128], BF16, "kb"), kd=sbt([64, 8, 128], BF16, "kd"),
                u=sbt([64, 8, 128], F32, "u"), wT=sbt([128, 8, 64], BF16, "wT"), qd=sbt([128, 8, 64], BF16, "qd"),
                vn=sbt([64, 8, 128], BF16, "vn"), o=sbt([128, 8, 64], F32, "o"),
            )

        sets = [mk() for _ in range(NSET)]
        pA = TB(k.ps(cx, [128, 512], F32))
        pB = TB(k.ps(cx, [128, 512], F32))
        pC = TB(k.ps(cx, [128, 512], F32))
        pDE = TB(k.ps(cx, [128, 1024], F32))
        pF = TB(k.ps(cx, [128, 512], F32))
        pGH = TB(k.ps(cx, [128, 1024], F32))

        def v3(ap, n):
            return ap.rearrange("p (h n) -> p h n", h=8)

        def instance(n, ch, d, is_x):
            T = sets[n % NSET]
            tok0 = ch * CH
            d8 = slice(d * 8, (d + 1) * 8)
            s.dma("sp", T["qT"].t[:], k.QT[ch], writes=[T["qT"].b])
            s.dma("sp", T["kT"].t[:], k.KT[ch], writes=[T["kT"].b])
            s.dma("sp", T["ktok"].t[:], k.KTOK[tok0:tok0 + CH], writes=[T["ktok"].b])
            s.dma("sp", T["vtok"].t[:], k.VTOK[tok0:tok0 + CH], writes=[T["vtok"].b])
            s.dma("sp", T["axt"].t[:], k.AUXT[tok0:tok0 + CH], writes=[T["axt"].b])
            s.dma("sp", T["R1"].t[:], k.AUXR[0, :, d, tok0:tok0 + CH].partition_broadcast(64), writes=[T["R1"].b])
            s.dma("sp", T["R2"].t[:], k.AUXR[1, :, d, tok0:tok0 + CH].partition_broadcast(64), writes=[T["R2"].b])
            s.dma("sp", T["EC"].t[:], k.AUXR[2, :, d, tok0:tok0 + CH].partition_broadcast(128), writes=[T["EC"].b])
            s.dma("sp", T["BL"].t[:], k.BLT[tok0:tok0 + 1, d8].partition_broadcast(128), writes=[T["BL"].b])
            axt = T["axt"].t
            c_b = bc_last(axt[:, 0, d8], 64)
            cb_b = bc_last(axt[:, 1, d8], 64)
            m_incl = msk.t[:, 0 if d == 0 else 2, :, :]
            m_strT = msk.t[:, 1 if d == 0 else 3, :, :]
            m_strL = msk.t[:, 3 if d == 0 else 1, :, :]
            kT, qT = T["kT"], T["qT"]
            s.mm([(v3(pA.t[0:64, :], 64)[:, h, :], kT.t[:, h, :], kT.t[:, h, :], True, True) for h in range(8)],
                 reads=[kT.b], writes=[pA.b])
            s.mm([(v3(pB.t[0:64, :], 64)[:, h, :], kT.t[:, h, :], qT.t[:, h, :], True, True) for h in range(8)],
                 reads=[kT.b, qT.b], writes=[pB.b])
            G = v3(pA.t[0:64, :], 64)
            QK = v3(pB.t[0:64, :], 64)
            EA, EM, EL = T["EA"], T["EM"], T["EL"]
            s.op("dve", lambda e: e.tensor_tensor(EA.t[:], T["R1"].t[:], c_b, ALU.subtract),
                 reads=[T["R1"].b, T["axt"].b], writes=[EA.b])
            s.op("dve", lambda e: e.tensor_tensor(EM.t[:], T["R2"].t[:], c_b, ALU.subtract),
                 reads=[T["R2"].b, T["axt"].b], writes=[EM.b])
            s.op("dve", lambda e: e.scalar_tensor_tensor(EL.t[:], T["R1"].t[:], -1.0, cb_b, ALU.mult, ALU.add),
                 reads=[T["R1"].b, T["axt"].b], writes=[EL.b])
            for (E, mk_) in ((EA, m_incl), (EM, m_strT), (EL, m_strL)):
                s.op("pool", lambda e, E=E: e.tensor_scalar(E.t[:], E.t[:], 0.0, None, ALU.min),
                     reads=[E.b], writes=[E.b])
                s.op("pool", lambda e, E=E, mk_=mk_: e.tensor_tensor(E.t[:], E.t[:], mk_, ALU.add),
                     reads=[E.b, msk.b], writes=[E.b])
                s.op("act", lambda e, E=E: e.activation(E.t[:], E.t[:], AF.Exp), reads=[E.b], writes=[E.b])
            aqk = T["aqk"]
            s.op("dve", lambda e: e.tensor_tensor(aqk.t[:], QK, EA.t[:], ALU.mult), reads=[pB.b, EA.b], writes=[aqk.b])
            Mp, Lp, P = T["Mp"], T["Lp"], T["P"]
            s.op("dve", lambda e: e.tensor_tensor(Mp[0].t[:], G, EM.t[:], ALU.mult), reads=[pA.b, EM.b], writes=[Mp[0].b])
            s.op("dve", lambda e: e.tensor_tensor(Lp[0].t[:], G, EL.t[:], ALU.mult), reads=[pA.b, EL.b], writes=[Lp[0].b])
            s.op("dve", lambda e: e.tensor_tensor(P[0].t[:], idr.t[:], Mp[0].t[:], ALU.subtract),
                 reads=[idr.b, Mp[0].b], writes=[P[0].b])
            cur = 0
            pc = 0
            for lvl in range(1, 6):
                nxt = 1 - cur
                Lc, Mc, Ln, Mn = Lp[cur], Mp[cur], Lp[nxt], Mp[nxt]
                pa3, pb3, pc3 = v3(pA.t[0:64, :], 64), v3(pB.t[0:64, :], 64), v3(pC.t[0:64, :], 64)
                s.mm([(pb3[:, h, :], Mc.t[:, h, :], Lc.t[:, h, :], True, True) for h in range(8)],
                     reads=[Mc.b, Lc.b], writes=[pB.b])
                if lvl < 5:
                    s.mm([(pa3[:, h, :], Lc.t[:, h, :], Mc.t[:, h, :], True, True) for h in range(8)],
                         reads=[Mc.b, Lc.b], writes=[pA.b])
                s.op("act", lambda e, Ln=Ln: e.activation(Ln.t[:], pb3, AF.Copy), reads=[pB.b], writes=[Ln.b])
                if lvl < 5:
                    s.op("act", lambda e, Mn=Mn: e.activation(Mn.t[:], pa3, AF.Copy), reads=[pA.b], writes=[Mn.b])
                Pc, Pn = P[pc], P[1 - pc]
                s.mm([(pc3[:, h, :], Ln.t[:, h, :], Pc.t[:, h, :], True, True) for h in range(8)],
                     reads=[Ln.b, Pc.b], writes=[pC.b])
                if lvl < 5:
                    s.op("dve", lambda e, Pc=Pc, Pn=Pn: e.tensor_tensor(Pn.t[:], Pc.t[:], pc3, ALU.add),
                         reads=[Pc.b, pC.b], writes=[Pn.b])
                else:
                    s.op("dve", lambda e, Pc=Pc: e.tensor_tensor(T["TT"].t[:], Pc.t[:], pc3, ALU.add),
                         reads=[Pc.b, pC.b], writes=[T["TT"].b])
                cur = nxt
                pc = 1 - pc
            TT = T["TT"]
            bv, kb, kd, qd = T["bv"], T["kb"], T["kd"], T["qd"]
            s.op("pool", lambda e: e.tensor_tensor(bv.t[:], T["vtok"].t[:], bc_last(axt[:, 2, d8], 128), ALU.mult),
                 reads=[T["vtok"].b, T["axt"].b], writes=[bv.b])
            s.op("pool", lambda e: e.tensor_tensor(kb.t[:], T["ktok"].t[:], bc_last(axt[:, 3, d8], 128), ALU.mult),
                 reads=[T["ktok"].b, T["axt"].b], writes=[kb.b])
            s.op("pool", lambda e: e.tensor_tensor(kd.t[:], T["ktok"].t[:], bc_last(axt[:, 4, d8], 128), ALU.mult),
                 reads=[T["ktok"].b, T["axt"].b], writes=[kd.b])
            s.op("pool", lambda e: e.tensor_tensor(qd.t[:], qT.t[:], T["EC"].t[:], ALU.mult),
                 reads=[qT.b, T["EC"].b], writes=[qd.b])
            pu = pDE.t[0:64, :].rearrange("p (h n) -> p h n", h=8)
            pw = v3(pF.t[:, :], 64)
            s.mm([(pu[:, h, :], TT.t[:, h, :], bv.t[:, h, :], True, True) for h in range(8)],
                 reads=[TT.b, bv.b], writes=[pDE.b])
            s.mm([(pw[:, h, :], kb.t[:, h, :], TT.t[:, h, :], True, True) for h in range(8)],
                 reads=[TT.b, kb.b], writes=[pF.b])
            u, wT = T["u"], T["wT"]
            s.op("act", lambda e: e.activation(u.t[:], pu, AF.Copy), reads=[pDE.b], writes=[u.b])
            s.op("act", lambda e: e.activation(wT.t[:], pw, AF.Copy), reads=[pF.b], writes=[wT.b])
            Sd, S3 = Sb[d], S32[d]
            s.mm([(pu[:, h, :], wT.t[:, h, :], Sd.t[:, h, :], True, True) for h in range(8)],
                 reads=[wT.b, Sd.b], writes=[pDE.b])
            vn = T["vn"]
            s.op("dve", lambda e: e.tensor_tensor(vn.t[:], u.t[:], pu, ALU.subtract), reads=[u.b, pDE.b], writes=[vn.b])
            if is_x:
                mms = []
                for h in range(8):
                    mms.append((pw[:, h, :], Sd.t[:, h, :], qd.t[:, h, :], True, False))
                    mms.append((pw[:, h, :], vn.t[:, h, :], aqk.t[:, h, :], False, True))
                s.mm(mms, reads=[Sd.b, qd.b, vn.b, aqk.b], writes=[pF.b])
                o = T["o"]
                s.op("act", lambda e: e.activation(o.t[:], pw, AF.Copy), reads=[pF.b], writes=[o.b])
                s.dma("sp", k.OT[d, ch - NCC], o.t[:], reads=[o.b])
            pS = pGH.t[:, :].rearrange("p (h n) -> p h n", h=8)
            s.mm([(pS[:, h, :], kd.t[:, h, :], vn.t[:, h, :], True, True) for h in range(8)],
                 reads=[kd.b, vn.b], writes=[pGH.b])
            s.op("pool", lambda e: e.tensor_tensor(S3.t[:], S3.t[:], bc_last(T["BL"].t[:, :], 128), ALU.mult),
                 reads=[S3.b, T["BL"].b], writes=[S3.b])
            s.op("dve", lambda e: e.tensor_tensor(S3.t[:], S3.t[:], pS, ALU.add), reads=[S3.b, pGH.b], writes=[S3.b])
            s.op("act", lambda e: e.activation(Sd.t[:], S3.t[:], AF.Copy), reads=[S3.b], writes=[Sd.b])

        order_f = list(range(NCH))
        order_b = list(range(NCC - 1, -1, -1)) + list(range(NCH - 1, NCC - 1, -1))
        n = 0
        for st in range(NCH):
            instance(n, order_f[st], 0, order_f[st] >= NCC)
            n += 1
            instance(n, order_b[st], 1, order_b[st] >= NCC)
            n += 1
        s.barrier()


def pool_operators(SEQ):
    R = SEQ // GRID_W
    NB = SEQ // 128
    wins = (2, 4, 8, 16)
    mats = []
    index = {}
    cache = {}
    cc = np.arange(GRID_W)
    for g, w in enumerate(wins):
        clo = np.clip(cc - w // 2, 0, GRID_W)
        chi = np.clip(cc + w - w // 2, 0, GRID_W)
        cm = ((cc[:, None] >= clo[None, :]) & (cc[:, None] < chi[None, :])).astype(np.float64)
        car = (chi - clo).astype(np.float64)
        maxoff = (w // 2 + 1) // 2 + 1
        for b in range(NB):
            for off in range(-maxoff, maxoff + 1):
                bp = b + off
                if bp < 0 or bp >= NB:
                    continue
                M = np.zeros((128, 128), np.float64)
                for ro in range(2):
                    r = 2 * b + ro
                    rlo = max(r - w // 2, 0)
                    rhi = min(r + w - w // 2, R)
                    for ri in range(2):
                        rp = 2 * bp + ri
                        if rlo <= rp < rhi:
                            M[ri * 64:(ri + 1) * 64, ro * 64:(ro + 1) * 64] = cm / (car[None, :] * (rhi - rlo))
                if off == 0:
                    M -= np.eye(128)
                if not M.any():
                    continue
                key = (g, off, M.tobytes())
                if key not in cache:
                    cache[key] = len(mats)
                    mats.append(M.astype(np.float32))
                index[(g, b, off)] = cache[key]
    return np.stack(mats, 0), index


def pass4_merge(k, C):
    s = k.s
    SEQ = k.SEQ
    NB = SEQ // 128
    tab = C["tab"]
    with ExitStack() as cx:
        Wg = k.sb(cx, [128, 8, D], BF16, "wg")
        Wp = k.sb(cx, [128, 4, D], BF16, "wp")
        Wmo = k.sb(cx, [128, 8, D], BF16, "wmo")
        bW = load_weight_bf16(k, Wg, k.w_gdn_proj, 8)
        bW2 = load_weight_bf16(k, Wp, k.w_pool_proj, 4)
        bW3 = load_weight_bf16(k, Wmo, k.w_mix_out, 8)
        nmat = k.opm_n
        opm = k.sb(cx, [128, nmat, 128], BF16, "opm")
        poolw = k.sb(cx, [128, 4, 128], BF16, "poolw")
        bW4 = Buf()
        s.dma("pool", opm[:], k.opm.rearrange("n p t -> p n t"), writes=[bW4])
        s.dma("pool", poolw[:], k.poolw, writes=[bW4])
        small = k.sb(cx, [128, 8], F32, "small")
        s.dma("sp", small[:, 0:1], k.gnw, writes=[bW4])
        s.dma("sp", small[:, 1:5], k.pscale, writes=[bW4])
        cb2 = k.sb(cx, [128, 2], F32, "cb2")
        s.op("dve", lambda e: e.memset(cb2[:, 0:1], EPS), writes=[bW4])
        wts = [bW, bW2, bW3, bW4]

        def dbl(shape, dt, p):
            return [TB(k.sb(cx, shape, dt, p)) for _ in range(2)]

        x1T = dbl([128, 8, NT], F32, "x1T")
        of = [TB(k.sb(cx, [128, 4, 8, 64], F32, "of"))] * 2
        ob = [TB(k.sb(cx, [128, 4, 8, 64], F32, "ob"))] * 2
        sgT = dbl([128, 8, NT], F32, "sgT")
        gts = [TB(k.sb(cx, [128, 16, NT], F32, "gts"))] * 2
        ppt = dbl([128, 10, 512], BF16, "ppt")
        o = TB(k.sb(cx, [128, 8, NT], F32, "o"))
        sq = TB(k.sb(cx, [128, 8, NT], F32, "sq"))
        rs = sq
        og = TB(k.sb(cx, [128, 8, NT], BF16, "og"))
        pd = TB(k.sb(cx, [128, 4, NT], BF16, "pd"))
        yp1 = TB(k.sb(cx, [128, 4, NT], BF16, "yp1"))
        t1 = dbl([128, NT], F32, "t1")
        t2 = dbl([128, NT], F32, "t2")
        msT = TB(k.sb(cx, [128, 8, NT], BF16, "msT"))
        pn = TB(k.ps(cx, [128, 4, NT], F32))
        pgp = [TB(k.ps(cx, [128, 2, NT], F32)) for _ in range(2)]
        ppd = TB(k.ps(cx, [128, 4, NT], F32))
        pmx = [TB(k.ps(cx, [128, 512], F32)) for _ in range(2)]
        for ti in range(SEQ // NT):
            tx0 = ti * NT
            cx0 = tx0 // CH
            b0 = tx0 // 128
            X, OF, OB, SG, GT, PP = x1T[ti % 2], of[ti % 2], ob[ti % 2], sgT[ti % 2], gts[ti % 2], ppt[ti % 2]
            s.dma("sp", X.t[:], k.X1T[:, CTX + tx0:CTX + tx0 + NT].rearrange("(kc p) t -> p kc t", p=128), writes=[X.b])
            s.dma("sp", OF.t[:].rearrange("p c h t -> p c (h t)"),
                  k.OT[0, cx0:cx0 + 4].rearrange("c d h t -> d c (h t)"), writes=[OF.b])
            s.dma("sp", OB.t[:].rearrange("p c h t -> p c (h t)"),
                  k.OT[1, cx0:cx0 + 4].rearrange("c d h t -> d c (h t)"), writes=[OB.b])
            s.dma("sp", SG.t[:], k.SGATE[:, tx0:tx0 + NT].rearrange("(kc p) t -> p kc t", p=128), writes=[SG.b])
            s.dma("sp", GT.t[:], k.GATES[:, tx0:tx0 + NT].rearrange("(kc p) t -> p kc t", p=128), writes=[GT.b])
            blo = max(b0 - 4, 0)
            bhi = min(b0 + 6, NB)
            s.dma("sp", PP.t[:, 0:bhi - blo, :], k.PPT[blo * 128:bhi * 128, :].rearrange("(b p) c -> p b c", p=128),
                  writes=[PP.b])
            o4 = o.t[:].rearrange("p h (c t) -> p h c t", c=4)
            s.op("pool", lambda e, OF=OF, OB=OB: e.tensor_tensor(
                o4, OF.t[:].rearrange("p c h t -> p h c t"), OB.t[:].rearrange("p c h t -> p h c t"), ALU.add),
                reads=[OF.b, OB.b], writes=[o.b])
            s.op("act", lambda e: e.activation(sq.t[:], o.t[:], AF.Square), reads=[o.b], writes=[sq.b])
            for hh in range(2):
                s.mm([(pn.t[:, i, :], C["ones"], sq.t[:, hh * 4 + i, :], True, True) for i in range(4)],
                     reads=[sq.b, C["bconst"]], writes=[pn.b])
                s.op("act", lambda e, hh=hh: e.activation(rs.t[:, hh * 4:hh * 4 + 4, :], pn.t[:], AF.Ln,
                                                          bias=cb2[:, 0:1], scale=1.0 / HD),
                     reads=[pn.b, bW4], writes=[rs.b])
            s.op("act", lambda e: e.activation(rs.t[:], rs.t[:], AF.Exp, scale=-0.5), reads=[rs.b], writes=[rs.b])
            s.op("dve", lambda e: e.tensor_tensor(o.t[:], o.t[:], rs.t[:], ALU.mult), reads=[o.b, rs.b], writes=[o.b])
            s.op("dve", lambda e, SG=SG: e.scalar_tensor_tensor(
                og.t[:].rearrange("p h t -> p (h t)"), o.t[:].rearrange("p h t -> p (h t)"), small[:, 0:1],
                SG.t[:].rearrange("p h t -> p (h t)"), ALU.mult, ALU.mult), reads=[o.b, SG.b, bW4], writes=[og.b])
            mms = []
            for g in range(4):
                for obk in range(2):
                    b = b0 + obk
                    offs = [off for off in range(-5, 6) if (g, b, off) in k.opm_index]
                    for j, off in enumerate(offs):
                        mms.append((ppd.t[:, g, obk * 128:(obk + 1) * 128],
                                    PP.t[:, b + off - blo, g * 128:(g + 1) * 128],
                                    opm[:, k.opm_index[(g, b, off)], :], j == 0, j == len(offs) - 1))
            s.mm(mms, reads=[PP.b, bW4], writes=[ppd.b])
            s.op("act", lambda e: e.activation(pd.t[:], ppd.t[:], AF.Copy), reads=[ppd.b], writes=[pd.b])
            s.mm([(ppd.t[:, g, :], poolw[:, g, :], pd.t[:, g, :], True, True) for g in range(4)],
                 reads=[pd.b, bW4], writes=[ppd.b])
            s.op("dve", lambda e: e.tensor_tensor(yp1.t[:], ppd.t[:], bc_last(small[:, 1:5], NT), ALU.mult),
                 reads=[ppd.b, bW4], writes=[yp1.b])
            for dc in range(8):
                pg = pgp[dc % 2]
                mms = [(pg.t[:, 0, :], Wg[:, h, dc * 128:(dc + 1) * 128], og.t[:, h, :], h == 0, h == 7)
                       for h in range(8)]
                mms += [(pg.t[:, 1, :], Wp[:, g, dc * 128:(dc + 1) * 128], yp1.t[:, g, :], g == 0, g == 3)
                        for g in range(4)]
                s.mm(mms, reads=[og.b, yp1.b] + wts, writes=[pg.b])
                a1, a2 = t1[dc % 2], t2[dc % 2]
                s.op("dve", lambda e, pg=pg, a1=a1, GT=GT, dc=dc: e.tensor_tensor(
                    a1.t[:], pg.t[:, 0, :], GT.t[:, 8 + dc, :], ALU.mult), reads=[pg.b, GT.b], writes=[a1.b])
                s.op("dve", lambda e, pg=pg, a2=a2, GT=GT, dc=dc: e.tensor_tensor(
                    a2.t[:], pg.t[:, 1, :], GT.t[:, dc, :], ALU.mult), reads=[pg.b, GT.b], writes=[a2.b])
                s.op("pool", lambda e, a1=a1, a2=a2, dc=dc: e.tensor_tensor(msT.t[:, dc, :], a1.t[:], a2.t[:], ALU.add),
                     reads=[a1.b, a2.b], writes=[msT.b])
            for dc in range(8):
                pm = pmx[dc % 2]
                s.mm([(pm.t[:, 0:NT], Wmo[:, kc, dc * 128:(dc + 1) * 128], msT.t[:, kc, :], kc == 0, kc == 7)
                      for kc in range(8)], reads=[msT.b] + wts, writes=[pm.b])
                s.op("dve", lambda e, pm=pm, X=X, dc=dc: e.scalar_tensor_tensor(
                    X.t[:, dc, :], pm.t[:, 0:NT], tab[:, 0, 5, dc:dc + 1], X.t[:, dc, :], ALU.mult, ALU.add),
                    reads=[pm.b, X.b, C["btab"]], writes=[X.b])
            s.dma("sp", k.X2T[:, tx0:tx0 + NT].rearrange("(kc p) t -> p kc t", p=128), X.t[:], reads=[X.b])
        s.barrier()


def pass3b_scan2(k, C):
    s = k.s
    NTOK = k.NTOK
    NCH = NTOK // CH
    NCC = CTX // CH
    with ExitStack() as cx:
        def sbt(shape, dt, p="t"):
            return TB(k.sb(cx, shape, dt, p))

        msk = sbt([128, 3, 8, 64], F32, "msk")
        s.dma("sp", msk.t[:], k.masks2, writes=[msk.b])
        idr = sbt([128, 8, 64], F32, "idr")
        s.dma("sp", idr.t[:], k.identrep2, writes=[idr.b])
        S32 = [sbt([128, 8, 128], F32, "S32") for _ in range(2)]
        Sb = [sbt([128, 8, 128], BF16, "Sb") for _ in range(2)]
        for d in range(2):
            s.op("dve", lambda e, d=d: e.memset(S32[d].t[:], 0.0), writes=[S32[d].b])
            s.op("pool", lambda e, d=d: e.memset(Sb[d].t[:], 0.0), writes=[Sb[d].b])

        def mk_loaded():
            return dict(
                qT=[sbt([128, 8, 64], BF16, "qT") for _ in range(2)], kT=[sbt([128, 8, 64], BF16, "kT") for _ in range(2)],
                EC=[sbt([128, 8, 64], F32, "EC") for _ in range(2)], BL=[sbt([128, 8], F32, "BL") for _ in range(2)],
                ktok=sbt([128, 8, 128], BF16, "ktok"), vtok=sbt([128, 8, 128], BF16, "vtok"),
                axt=sbt([128, 5, 8], F32, "axt"), R1=sbt([128, 8, 64], F32, "R1"), R2=sbt([128, 8, 64], F32, "R2"))

        def mk_comp():
            return dict(
                EA=sbt([128, 8, 64], F32, "EA"), EM=sbt([128, 8, 64], F32, "EM"), EL=sbt([128, 8, 64], F32, "EL"),
                aqk=sbt([128, 8, 64], BF16, "aqk"),
                Mp=[sbt([128, 8, 64], F32, "Mp") for _ in range(2)],
                Lp=[sbt([128, 8, 64], F32, "Lp") for _ in range(2)],
                P=[sbt([128, 8, 64], F32, "P") for _ in range(2)],
                TT=sbt([128, 8, 64], BF16, "TT"),
                bv=sbt([128, 8, 128], BF16, "bv"), kb=sbt([128, 8, 128], BF16, "kb"), kd=sbt([128, 8, 128], BF16, "kd"),
                u=sbt([128, 8, 128], F32, "u"), vn=sbt([128, 8, 128], BF16, "vn"),
                wT=[sbt([128, 8, 64], BF16, "wT") for _ in range(2)], qd=[sbt([128, 8, 64], BF16, "qd") for _ in range(2)],
                o=[sbt([128, 8, 64], F32, "o") for _ in range(2)])

        LS = [mk_loaded() for _ in range(3)]
        CS = [mk_comp() for _ in range(2)]
        pAB = TB(k.ps(cx, [128, 1024], F32))
        pC = TB(k.ps(cx, [128, 512], F32))
        pWS = TB(k.ps(cx, [128, 1024], F32))
        pO = TB(k.ps(cx, [128, 512], F32))
        pS2 = TB(k.ps(cx, [128, 1024], F32))

        def v64(ap):
            return ap.rearrange("p (h n) -> p h n", h=8)

        pa3 = v64(pAB.t[:, 0:512])
        pb3 = v64(pAB.t[:, 512:1024])
        pc3 = v64(pC.t[:, :])
        pu3 = v64(pAB.t[:, :])
        pws3 = v64(pWS.t[:, :])
        po3 = v64(pO.t[:, :])
        ps3 = v64(pS2.t[:, :])
        HF = [slice(0, 64), slice(64, 128)]
        order = [list(range(NCH)), list(range(NCC - 1, -1, -1)) + list(range(NCH - 1, NCC - 1, -1))]

        def loads(st):
            L = LS[st % 3]
            for d in range(2):
                ch = order[d][st]
                tok0 = ch * CH
                d8 = slice(d * 8, (d + 1) * 8)
                hf = HF[d]
                s.dma("sp", L["qT"][d].t[:], k.QT[ch], writes=[L["qT"][d].b])
                s.dma("sp", L["kT"][d].t[:], k.KT[ch], writes=[L["kT"][d].b])
                s.dma("sp", L["ktok"].t[hf], k.KTOK[tok0:tok0 + CH], writes=[L["ktok"].b])
                s.dma("sp", L["vtok"].t[hf], k.VTOK[tok0:tok0 + CH], writes=[L["vtok"].b])
                s.dma("sp", L["axt"].t[hf], k.AUXT[tok0:tok0 + CH, :, d8], writes=[L["axt"].b])
                s.dma("sp", L["R1"].t[hf], k.AUXR[0, :, d, tok0:tok0 + CH].partition_broadcast(64), writes=[L["R1"].b])
                s.dma("sp", L["R2"].t[hf], k.AUXR[1, :, d, tok0:tok0 + CH].partition_broadcast(64), writes=[L["R2"].b])
                s.dma("sp", L["EC"][d].t[:], k.AUXR[2, :, d, tok0:tok0 + CH].partition_broadcast(128),
                      writes=[L["EC"][d].b])
                s.dma("sp", L["BL"][d].t[:], k.BLT[tok0:tok0 + 1, d8].partition_broadcast(128), writes=[L["BL"][d].b])

        def prep(st):
            L = LS[st % 3]
            T = CS[st % 2]
            axt = L["axt"].t
            c_b = bc_last(axt[:, 0, :], 64)
            cb_b = bc_last(axt[:, 1, :], 64)
            kT, qT = L["kT"], L["qT"]
            s.mm([(pa3[HF[d], h, :], kT[d].t[:, h, :], kT[d].t[:, h, :], True, True) for h in range(8) for d in range(2)],
                 reads=[kT[0].b, kT[1].b], writes=[pAB.b])
            s.mm([(pb3[HF[d], h, :], kT[d].t[:, h, :], qT[d].t[:, h, :], True, True) for h in range(8) for d in range(2)],
                 reads=[kT[0].b, kT[1].b, qT[0].b, qT[1].b], writes=[pAB.b])
            EA, EM, EL = T["EA"], T["EM"], T["EL"]
            s.op("dve", lambda e: e.tensor_tensor(EA.t[:], L["R1"].t[:], c_b, ALU.subtract),
                 reads=[L["R1"].b, L["axt"].b], writes=[EA.b])
            s.op("dve", lambda e: e.tensor_tensor(EM.t[:], L["R2"].t[:], c_b, ALU.subtract),
                 reads=[L["R2"].b, L["axt"].b], writes=[EM.b])
            s.op("dve", lambda e: e.scalar_tensor_tensor(EL.t[:], L["R1"].t[:], -1.0, cb_b, ALU.mult, ALU.add),
                 reads=[L["R1"].b, L["axt"].b], writes=[EL.b])
            yield
            for (E, mi) in ((EA, 0), (EM, 1), (EL, 2)):
                s.op("dve", lambda e, E=E, mi=mi: e.scalar_tensor_tensor(E.t[:], E.t[:], 0.0, msk.t[:, mi, :, :],
                                                                       ALU.min, ALU.add),
                     reads=[E.b, msk.b], writes=[E.b])
            yield
            for E in (EA, EM, EL):
                s.op("act", lambda e, E=E: e.activation(E.t[:], E.t[:], AF.Exp), reads=[E.b], writes=[E.b])
            yield
            aqk = T["aqk"]
            Mp, Lp, P = T["Mp"], T["Lp"], T["P"]
            s.op("dve", lambda e: e.tensor_tensor(Mp[0].t[:], pa3, EM.t[:], ALU.mult), reads=[pAB.b, EM.b], writes=[Mp[0].b])
            s.op("dve", lambda e: e.tensor_tensor(Lp[0].t[:], pa3, EL.t[:], ALU.mult), reads=[pAB.b, EL.b], writes=[Lp[0].b])
            s.op("dve", lambda e: e.tensor_tensor(aqk.t[:], pb3, EA.t[:], ALU.mult), reads=[pAB.b, EA.b], writes=[aqk.b])
            s.op("dve", lambda e: e.tensor_tensor(P[0].t[:], idr.t[:], Mp[0].t[:], ALU.subtract),
                 reads=[idr.b, Mp[0].b], writes=[P[0].b])
            bv, kb, kd, qd = T["bv"], T["kb"], T["kd"], T["qd"]
            s.op("pool", lambda e: e.tensor_tensor(bv.t[:], L["vtok"].t[:], bc_last(axt[:, 2, :], 128), ALU.mult),
                 reads=[L["vtok"].b, L["axt"].b], writes=[bv.b])
            s.op("pool", lambda e: e.tensor_tensor(kb.t[:], L["ktok"].t[:], bc_last(axt[:, 3, :], 128), ALU.mult),
                 reads=[L["ktok"].b, L["axt"].b], writes=[kb.b])
            yield
            cur = 0
            pcx = 0
            for lvl in range(1, 6):
                nxt = 1 - cur
                Lc, Mc, Ln, Mn = Lp[cur], Mp[cur], Lp[nxt], Mp[nxt]
                s.mm([(pb3[HF[d], h, :], Mc.t[HF[d], h, :], Lc.t[HF[d], h, :], True, True)
                      for h in range(8) for d in range(2)], reads=[Mc.b, Lc.b], writes=[pAB.b])
                if lvl < 5:
                    s.mm([(pa3[HF[d], h, :], Lc.t[HF[d], h, :], Mc.t[HF[d], h, :], True, True)
                          for h in range(8) for d in range(2)], reads=[Mc.b, Lc.b], writes=[pAB.b])
                yield
                s.op("act", lambda e, Ln=Ln: e.activation(Ln.t[:], pb3, AF.Copy), reads=[pAB.b], writes=[Ln.b])
                if lvl < 5:
                    s.op("act", lambda e, Mn=Mn: e.activation(Mn.t[:], pa3, AF.Copy), reads=[pAB.b], writes=[Mn.b])
                if lvl == 1:
                    s.op("pool", lambda e: e.tensor_tensor(kd.t[:], L["ktok"].t[:], bc_last(axt[:, 4, :], 128), ALU.mult),
                         reads=[L["ktok"].b, L["axt"].b], writes=[kd.b])
                if lvl == 2:
                    for d in range(2):
                        s.op("pool", lambda e, d=d: e.tensor_tensor(qd[d].t[:], qT[d].t[:], L["EC"][d].t[:], ALU.mult),
                             reads=[qT[d].b, L["EC"][d].b], writes=[qd[d].b])
                yield
                Pc, Pn = P[pcx], P[1 - pcx]
                s.mm([(pc3[HF[d], h, :], Ln.t[HF[d], h, :], Pc.t[HF[d], h, :], True, True)
                      for h in range(8) for d in range(2)], reads=[Ln.b, Pc.b], writes=[pC.b])
                yield
                if lvl < 5:
                    s.op("dve", lambda e, Pc=Pc, Pn=Pn: e.tensor_tensor(Pn.t[:], Pc.t[:], pc3, ALU.add),
                         reads=[Pc.b, pC.b], writes=[Pn.b])
                else:
                    s.op("dve", lambda e, Pc=Pc: e.tensor_tensor(T["TT"].t[:], Pc.t[:], pc3, ALU.add),
                         reads=[Pc.b, pC.b], writes=[T["TT"].b])
                cur = nxt
                pcx = 1 - pcx
            yield
            TT = T["TT"]
            s.mm([(pu3[HF[d], h, :], TT.t[HF[d], h, :], bv.t[HF[d], h, :], True, True)
                  for h in range(8) for d in range(2)], reads=[TT.b, bv.b], writes=[pAB.b])
            yield
            s.op("act", lambda e: e.activation(T["u"].t[:], pu3, AF.Copy), reads=[pAB.b], writes=[T["u"].b])
            for d in range(2):
                s.mm([(pc3[:, h, :], kb.t[HF[d], h, :], TT.t[HF[d], h, :], True, True) for h in range(8)],
                     reads=[TT.b, kb.b], writes=[pC.b])
                yield
                s.op("act", lambda e, d=d: e.activation(T["wT"][d].t[:], pc3, AF.Copy), reads=[pC.b], writes=[T["wT"][d].b])
                yield

        def seq(st):
            L = LS[st % 3]
            T = CS[st % 2]
            is_x = order[0][st] >= NCC
            wT, qd, vn, u, kd, aqk = T["wT"], T["qd"], T["vn"], T["u"], T["kd"], T["aqk"]
            s.mm([(pws3[HF[d], h, :], wT[d].t[:, h, :], Sb[d].t[:, h, :], True, True) for h in range(8) for d in range(2)],
                 reads=[wT[0].b, wT[1].b, Sb[0].b, Sb[1].b], writes=[pWS.b])
            yield
            s.op("dve", lambda e: e.tensor_tensor(vn.t[:], u.t[:], pws3, ALU.subtract), reads=[u.b, pWS.b], writes=[vn.b])
            yield
            for d in range(2):
                ch = order[d][st]
                if is_x:
                    mms = []
                    for h in range(8):
                        mms.append((po3[:, h, :], Sb[d].t[:, h, :], qd[d].t[:, h, :], True, False))
                        mms.append((po3[:, h, :], vn.t[HF[d], h, :], aqk.t[HF[d], h, :], False, True))
                    s.mm(mms, reads=[Sb[d].b, qd[d].b, vn.b, aqk.b], writes=[pO.b])
                s.mm([(ps3[:, h, :], kd.t[HF[d], h, :], vn.t[HF[d], h, :], True, True) for h in range(8)],
                     reads=[kd.b, vn.b], writes=[pS2.b])
                s.op("pool", lambda e, d=d: e.tensor_tensor(S32[d].t[:], S32[d].t[:], bc_last(L["BL"][d].t[:, :], 128),
                                                            ALU.mult), reads=[S32[d].b, L["BL"][d].b], writes=[S32[d].b])
                yield
                if is_x:
                    s.op("act", lambda e, d=d: e.activation(T["o"][d].t[:], po3, AF.Copy), reads=[pO.b], writes=[T["o"][d].b])
                    s.dma("sp", k.OT[d, ch - NCC], T["o"][d].t[:], reads=[T["o"][d].b])
                s.op("dve", lambda e, d=d: e.tensor_tensor(S32[d].t[:], S32[d].t[:], ps3, ALU.add),
                     reads=[S32[d].b, pS2.b], writes=[S32[d].b])
                yield
                s.op("act", lambda e, d=d: e.activation(Sb[d].t[:], S32[d].t[:], AF.Copy), reads=[S32[d].b], writes=[Sb[d].b])
                yield

        loads(0)
        for st in range(NCH + 1):
            if st + 1 < NCH:
                loads(st + 1)
            gens = []
            if st < NCH:
                gens.append(prep(st))
            if st >= 1:
                gens.append(seq(st - 1))
            while gens:
                for g in list(gens):
                    try:
                        next(g)
                    except StopIteration:
                        gens.remove(g)
        s.barrier()


def build(SEQ=8192, debug=(), upto=99):
    nc = bass.Bass("TRN2", target_bir_lowering=False)
    es = ExitStack()
    k = K(nc, es, SEQ, debug)
    s = k.s
    NTOK = k.NTOK
    k.xt = k.din("xt", [D, NTOK])
    k.cvec = k.din("cvec", [128, 8, 2])
    k.w_ada = k.din("w_ada", [D, NMOD * D])
    k.b_ada = k.din("b_ada", [128, 72])
    k.nw = k.din("nw", [128, 4, 8])
    k.ffn1_w_in = k.din("ffn1_w_in", [D, 2 * DFF])
    k.ffn1_w_out = k.din("ffn1_w_out", [DFF, D])
    k.ffn2_w_in = k.din("ffn2_w_in", [D, 2 * DFF])
    k.ffn2_w_out = k.din("ffn2_w_out", [DFF, D])
    k.cst = k.din("cst", [128, 2, 128])
    k.w_mix_in = k.din("w_mix_in", [D, MIX_IN])
    k.convw = k.din("convw", [128, 24, 5])
    k.cst2 = k.din("cst2", [128, 3, 128])
    k.adt = k.din("adt", [1, 32])
    k.masks2 = k.din("masks2", [128, 3, 8, 64])
    k.identrep2 = k.din("identrep2", [128, 8, 64])
    mats, k.opm_index = pool_operators(SEQ)
    k.opm_n = mats.shape[0]
    k.opm = k.din("opm", [k.opm_n, 128, 128])
    k.poolw = k.din("poolw", [128, 4, 128])
    k.gnw = k.din("gnw", [128, 1])
    k.pscale = k.din("pscale", [128, 4])
    k.w_gdn_proj = k.din("w_gdn_proj", [D, D])
    k.w_pool_proj = k.din("w_pool_proj", [512, D])
    k.w_mix_out = k.din("w_mix_out", [D, D])
    k.X1T = k.dscr("X1T", [D, NTOK])
    k.PQKV = k.dscr("PQKV", [QKV, NTOK])
    k.ABT = k.dscr("ABT", [NTOK, 32])
    k.SGATE = k.dscr("SGATE", [D, SEQ])
    k.PPT = k.dscr("PPT", [SEQ, 512], BF16)
    k.X2T = k.dscr("X2T", [D, SEQ])
    k.GATES = k.dscr("GATES", [2 * D, SEQ])
    NCH = NTOK // CH
    k.QT = k.dscr("QT", [NCH, 128, 8, CH], BF16)
    k.KT = k.dscr("KT", [NCH, 128, 8, CH], BF16)
    k.KTOK = k.dscr("KTOK", [NTOK, 8, 128], BF16)
    k.VTOK = k.dscr("VTOK", [NTOK, 8, 128], BF16)
    k.AUXT = k.dscr("AUXT", [NTOK, 5, 16])
    k.BLT = k.dscr("BLT", [NTOK, 16])
    k.AUXR = k.dscr("AUXR", [3, 8, 2, NTOK])
    k.OT = k.dscr("OT", [2, SEQ // CH, 128, 8, CH])
    k.outT = nc.dram_tensor("outT", [D, SEQ], F32, kind="ExternalOutput").ap()
    C = {}
    cst = k.sb(es, [128, 2, 128], F32, "cst")
    C["bconst"] = Buf()
    s.dma("sp", cst[:], k.cst, writes=[C["bconst"]])
    C["ones"] = cst[:, 0, :]
    C["ident"] = cst[:, 1, :]
    C["eps"] = k.sb(es, [128, 2], F32, "eps")
    s.op("dve", lambda e: e.memset(C["eps"][:], EPS), writes=[C["bconst"]])
    C["tab"] = k.sb(es, [128, 2, 9, 8], F32, "tab")
    C["fnw"] = k.sb(es, [128, 8], F32, "fnw")
    C["btab"] = Buf()

    x_tiles = [(0, 0, 1)] + [(CTX + i * NT, CTX + i * NT, 0) for i in range(SEQ // NT)]
    pass0_mod(k, C)
    if upto >= 1:
        ffn_pass(k, C, k.xt, k.X1T, k.ffn1_w_in, k.ffn1_w_out, 0, x_tiles)
    if upto >= 2:
        pass2_mixin(k, C, x_tiles)
    if upto >= 3:
        pass3a_gdnprep(k, C, x_tiles)
    if upto >= 4:
        pass3b_scan2(k, C)
    if upto >= 5:
        pass4_merge(k, C)
    if upto >= 6:
        tiles5 = [(i * NT, i * NT, 0) for i in range(SEQ // NT)]
        ffn_pass(k, C, k.X2T, None, k.ffn2_w_in, k.ffn2_w_out, 2, tiles5, final_out=k.outT)
    s.barrier()
    es.close()
    return nc, k


def host_inputs(inp, b, SEQ):
    f = np.float32
    x = np.asarray(inp["x"], f)[b, :SEQ]
    ctx = np.asarray(inp["ctx"], f)[b]
    m = {}
    m["xt"] = np.ascontiguousarray(np.concatenate([ctx.T, x.T], axis=1))
    cv = np.stack([np.asarray(inp["c"], f)[b], np.asarray(inp["c_ctx"], f)], axis=-1)
    m["cvec"] = np.ascontiguousarray(cv.reshape(8, 128, 2).transpose(1, 0, 2))
    m["w_ada"] = np.ascontiguousarray(np.asarray(inp["w_ada"], f)[0])
    m["b_ada"] = np.ascontiguousarray(np.asarray(inp["b_ada"], f)[0].reshape(72, 128).T)
    nws = np.stack([np.asarray(inp["norm1_w"], f)[0], np.asarray(inp["norm2_w"], f)[0],
                    np.asarray(inp["norm3_w"], f)[0], np.asarray(inp["final_norm_w"], f)], axis=0)
    m["nw"] = np.ascontiguousarray(nws.reshape(4, 8, 128).transpose(2, 0, 1))
    m["opm"] = pool_operators(SEQ)[0]
    m["poolw"] = np.ascontiguousarray(np.asarray(inp["pool_w"], f)[0].transpose(1, 0, 2))
    m["gnw"] = np.asarray(inp["gdn_norm_w"], f)[0].reshape(128, 1).copy()
    m["pscale"] = np.ascontiguousarray(np.asarray(inp["pool_scale"], f)[0].reshape(4, 128).T)
    for nm in ["ffn1_w_in", "ffn1_w_out", "ffn2_w_in", "ffn2_w_out", "w_mix_in", "w_gdn_proj", "w_pool_proj",
               "w_mix_out"]:
        m[nm] = np.ascontiguousarray(np.asarray(inp[nm], f)[0])
    cw = np.asarray(inp["conv_w"], f)[0]
    m["convw"] = np.ascontiguousarray(cw.reshape(5, 24, 128).transpose(2, 1, 0))
    m["adt"] = np.concatenate([np.asarray(inp["a_log"], f)[0].reshape(-1),
                               np.asarray(inp["dt_bias"], f)[0].reshape(-1)])[None, :].copy()
    jj = np.arange(128)
    same = (jj[:, None] // 64) == (jj[None, :] // 64)
    c2 = np.zeros((128, 3, 128), f)
    c2[:, 0, :] = same & (jj[:, None] <= jj[None, :])
    c2[:, 1, :] = same & (jj[:, None] >= jj[None, :])
    c2[:, 2, :] = same
    m["cst2"] = c2
    pp = np.arange(64)[:, None]
    ff = np.arange(64)[None, :]
    mk = np.stack([ff >= pp, ff > pp, ff <= pp, ff < pp], 0)
    mk = np.where(mk, 0.0, NEG).astype(f)
    m2 = np.concatenate([mk[[0, 1, 3]].transpose(1, 0, 2), mk[[2, 3, 1]].transpose(1, 0, 2)], axis=0)
    m["masks2"] = np.ascontiguousarray(np.broadcast_to(m2[:, :, None, :], (128, 3, 8, 64)))
    e2 = np.concatenate([np.eye(64, dtype=f), np.eye(64, dtype=f)], axis=0)
    m["identrep2"] = np.ascontiguousarray(np.broadcast_to(e2[:, None, :], (128, 8, 64)))
    cst = np.zeros((128, 2, 128), f)
    cst[:, 0, :] = 1.0
    cst[:, 1, :] = np.eye(128, dtype=f)
    m["cst"] = cst
    return m


def kernel(**inputs):
    SEQ = int(np.asarray(inputs["x"]).shape[1])
    B = int(np.asarray(inputs["x"]).shape[0])
    nc, k = build(SEQ)
    shared = None
    in_maps = []
    for b in range(B):
        m = host_inputs(inputs, b, SEQ)
        if shared is None:
            shared = {n: v for n, v in m.items() if n not in ("xt", "cvec")}
        else:
            for n in shared:
                m[n] = shared[n]
        in_maps.append({n: v for n, v in m.items() if n in k.dram_in})
    res = run_bass_kernel_spmd(nc, in_maps, core_ids=list(range(B)))
    out = np.stack([np.asarray(res.results[b]["outT"]).T for b in range(B)], axis=0)
    return np.ascontiguousarray(out.astype(np.float32))
```

```python
import numpy as np
import ml_dtypes
from contextlib import ExitStack
import concourse.bass as bass
import concourse.mybir as mybir
from concourse.bass_utils import run_bass_kernel_spmd

F32 = mybir.dt.float32
BF16 = mybir.dt.bfloat16
AF = mybir.ActivationFunctionType
ALU = mybir.AluOpType

D = 1024
DFF = 2816
NFF = DFF // 128
NMOD = 9
CTX = 256
H = 8
HD = 128
CH = 64
QKV = 3072
AB_END = 3104
GATE_END = 4128
POOL_END = 4640
MIX_IN = 6688
NT = 256
EPS = 1e-6
GRID_W = 64
NEG = -30000.0


class Buf:
    __slots__ = ("w", "r", "name")

    def __init__(self, name=""):
        self.w = {}
        self.r = {}
        self.name = name


class Sched:
    NDS = 8

    def __init__(self, nc, es):
        self.nc = nc
        self.engs = {"pe": nc.tensor, "dve": nc.vector, "act": nc.scalar, "pool": nc.gpsimd, "sp": nc.sync}
        self.sems = {}
        self.cnt = {}
        for k in ["pe", "dve", "act", "pool"]:
            self.sems[k] = es.enter_context(nc.semaphore("s_" + k))
            self.cnt[k] = 0
        self.dq = {}
        for q in ["sp", "pool", "act"]:
            lst = []
            for i in range(self.NDS):
                key = "d_%s%d" % (q, i)
                self.sems[key] = es.enter_context(nc.semaphore(key))
                self.cnt[key] = 0
                lst.append(key)
            self.dq[q] = [lst, 0]
        self.seen = {e: {} for e in self.engs}
        self.nwaits = 0
        self.nins = 0

    def _wait(self, e, key, val, raw=True):
        if key == e:
            if e == "pe" or not raw or self.cnt[e] - val >= 2:
                return
        if self.seen[e].get(key, 0) >= val:
            return
        self.engs[e].wait_ge(self.sems[key], val)
        self.seen[e][key] = val
        self.nwaits += 1

    def _deps(self, e, reads, writes):
        for b in reads:
            for key, val in b.w.items():
                self._wait(e, key, val)
        for b in writes:
            for key, val in b.w.items():
                self._wait(e, key, val, raw=False)
            for key, val in b.r.items():
                self._wait(e, key, val, raw=False)

    def _done(self, key, val, reads, writes):
        for b in reads:
            b.r[key] = val
        for b in writes:
            b.w[key] = val
            b.r = {}

    def op(self, e, fn, reads=(), writes=()):
        self._deps(e, reads, writes)
        ins = fn(self.engs[e])
        self.cnt[e] += 1
        ins.then_inc(self.sems[e], 1)
        self._done(e, self.cnt[e], reads, writes)
        self.nins += 1

    def mm(self, mms, reads=(), writes=()):
        self._deps("pe", reads, writes)
        ins = None
        for (o, l, r, st, sp) in mms:
            ins = self.nc.tensor.matmul(o, l, r, start=st, stop=sp)
        self.cnt["pe"] += 1
        ins.then_inc(self.sems["pe"], 1)
        self._done("pe", self.cnt["pe"], reads, writes)
        self.nins += len(mms)

    def tr(self, trs, reads=(), writes=()):
        self._deps("pe", reads, writes)
        ins = None
        for (o, i, idn) in trs:
            ins = self.nc.tensor.transpose(o, i, idn)
        self.cnt["pe"] += 1
        ins.then_inc(self.sems["pe"], 1)
        self._done("pe", self.cnt["pe"], reads, writes)
        self.nins += len(trs)

    def dma(self, q, out, in_, reads=(), writes=(), **kw):
        self._deps(q, reads, writes)
        lst, i = self.dq[q]
        key = lst[i % self.NDS]
        self.dq[q][1] = i + 1
        if self.cnt[key] > 0:
            self._wait(q, key, self.cnt[key])
        ins = self.engs[q].dma_start(out=out, in_=in_, **kw)
        self.cnt[key] += 16
        ins.then_inc(self.sems[key], 16)
        self._done(key, self.cnt[key], reads, writes)
        self.nins += 1

    def barrier(self):
        for e in self.engs:
            for key in self.sems:
                if self.cnt[key] > 0:
                    self._wait(e, key, self.cnt[key])


class K:
    def __init__(self, nc, es, SEQ, debug=()):
        self.nc = nc
        self.es = es
        self.s = Sched(nc, es)
        self.SEQ = SEQ
        self.NTOK = CTX + SEQ
        self.debug = set(debug)
        self.uid = 0
        self.dram_in = {}

    def name(self, p):
        self.uid += 1
        return "%s_%d" % (p, self.uid)

    def sb(self, ctx, shape, dt, p="t"):
        return ctx.enter_context(self.nc.sbuf_tensor(self.name(p), list(shape), dt))

    def ps(self, ctx, shape, dt, p="ps"):
        return ctx.enter_context(self.nc.psum_tensor(self.name(p), list(shape), dt))

    def din(self, name, shape, dt=F32):
        t = self.nc.dram_tensor(name, list(shape), dt, kind="ExternalInput")
        self.dram_in[name] = t
        return t.ap()

    def dscr(self, name, shape, dt=F32):
        kind = "ExternalOutput" if name in self.debug else "Internal"
        return self.nc.dram_tensor(name, list(shape), dt, kind=kind).ap()


def bc_mid(ap2, n):
    P, Fd = ap2.shape
    return ap2.unsqueeze(1).broadcast_to([P, n, Fd])


def bc_last(ap2, n):
    P, Fd = ap2.shape
    return ap2.unsqueeze(2).broadcast_to([P, Fd, n])


def load_weight_bf16(k, dst3, w_dram, nk, pieces=1):
    cols = w_dram.shape[1]
    step = (cols + pieces - 1) // pieces
    wb = Buf()
    for kc in range(nk):
        for c0 in range(0, cols, step):
            c1 = min(cols, c0 + step)
            k.s.dma("pool", dst3[:, kc, c0:c1], w_dram[kc * 128:(kc + 1) * 128, c0:c1], writes=[wb])
    return wb


def pass0_mod(k, C):
    s = k.s
    nc = k.nc
    tab = C["tab"]
    btab = C["btab"]
    with ExitStack() as cx:
        sc = k.sb(cx, [128, 8, 2], F32)
        bT = k.sb(cx, [128, 72], F32)
        nw = k.sb(cx, [128, 4, 8], F32)
        mod = k.sb(cx, [128, 2, 72], F32)
        wa = [k.sb(cx, [128, 8, 1024], F32) for _ in range(2)]
        pm = k.ps(cx, [128, 72, 2], F32)
        bsc, bbT, bnw, bmod, bpm = Buf(), Buf(), Buf(), Buf(), Buf()
        bwa = [Buf(), Buf()]
        s.dma("sp", sc[:], k.cvec, writes=[bsc])
        s.dma("sp", bT[:], k.b_ada, writes=[bbT])
        s.dma("sp", nw[:], k.nw, writes=[bnw])
        s.op("act", lambda e: e.activation(sc[:], sc[:], AF.Silu), reads=[bsc], writes=[bsc])
        for j in range(NMOD):
            w = wa[j % 2]
            s.dma("sp", w[:], k.w_ada[:, j * 1024:(j + 1) * 1024].rearrange("(kc p) c -> p kc c", p=128),
                  writes=[bwa[j % 2]])
            mms = []
            for dc in range(8):
                for kc in range(8):
                    mms.append((pm[:, j * 8 + dc, :], w[:, kc, dc * 128:(dc + 1) * 128], sc[:, kc, :],
                                kc == 0, kc == 7))
            s.mm(mms, reads=[bwa[j % 2], bsc], writes=[bpm])
        for wch in range(2):
            s.op("dve", lambda e, wch=wch: e.tensor_tensor(mod[:, wch, :], pm[:, :, wch], bT[:], ALU.add),
                 reads=[bpm, bbT], writes=[bmod])
        for wch in range(2):
            for sub, (jsh, jsc, jg, half) in enumerate([(0, 1, 2, True), (3, 4, 5, False), (6, 7, 8, True)]):
                s.op("dve", lambda e, wch=wch, sub=sub, jsc=jsc: e.scalar_tensor_tensor(
                    tab[:, wch, sub * 3 + 0, :], mod[:, wch, jsc * 8:(jsc + 1) * 8], 1.0, nw[:, sub, :],
                    ALU.add, ALU.mult), reads=[bmod, bnw], writes=[btab])
                s.op("dve", lambda e, wch=wch, sub=sub, jsh=jsh: e.tensor_copy(
                    tab[:, wch, sub * 3 + 1, :], mod[:, wch, jsh * 8:(jsh + 1) * 8]), reads=[bmod], writes=[btab])
                s.op("dve", lambda e, wch=wch, sub=sub, jg=jg, half=half: e.tensor_scalar(
                    tab[:, wch, sub * 3 + 2, :], mod[:, wch, jg * 8:(jg + 1) * 8], 0.5 if half else 1.0, None,
                    ALU.mult), reads=[bmod], writes=[btab])
        s.op("dve", lambda e: e.tensor_copy(C["fnw"][:], nw[:, 3, :]), reads=[bnw], writes=[btab])
        s.barrier()


def ffn_pass(k, C, src, dst, w_in, w_out, sub, tiles, final_out=None):
    s = k.s
    tab = C["tab"]
    btab = C["btab"]
    with ExitStack() as cx:
        Win = k.sb(cx, [128, 8, 2 * DFF], BF16, "win")
        Wout = k.sb(cx, [128, NFF, D], BF16, "wout")
        bWin = load_weight_bf16(k, Win, w_in, 8, pieces=2)
        bWout = load_weight_bf16(k, Wout, w_out, NFF)
        xTs = [k.sb(cx, [128, 8, NT], F32, "xT") for _ in range(2)]
        bxT = [Buf(), Buf()]
        sq = k.sb(cx, [128, 8, NT], F32, "sq")
        bsq = Buf()
        hTs = [k.sb(cx, [128, 8, NT], BF16, "hT") for _ in range(2)]
        bhTs = [Buf(), Buf()]
        aT = k.sb(cx, [128, NFF, NT], BF16, "aT")
        baT = Buf()
        rstd = k.sb(cx, [128, NT], F32, "rstd")
        brstd = Buf()
        pss = k.ps(cx, [128, 512], F32)
        bpss = Buf()
        NPB = 4
        pgu = [k.ps(cx, [128, 512], F32) for _ in range(NPB)]
        pys = [k.ps(cx, [128, 512], F32) for _ in range(2)]
        bpgu = [Buf() for _ in range(NPB)]
        bpy = [Buf(), Buf()]
        sg = [k.sb(cx, [128, NT], F32, "sg") for _ in range(NPB)]
        bsg = [Buf() for _ in range(NPB)]
        ones = C["ones"]
        epsb = C["eps"]

        def rms(xT, bx):
            s.op("act", lambda e: e.activation(sq[:], xT[:], AF.Square), reads=[bx], writes=[bsq])
            s.mm([(pss[:, 0:NT], ones[:], sq[:, kc, :], kc == 0, kc == 7) for kc in range(8)],
                 reads=[bsq, C["bconst"]], writes=[bpss])
            s.op("act", lambda e: e.activation(rstd[:], pss[:, 0:NT], AF.Sqrt, bias=epsb[:, 0:1], scale=1.0 / D),
                 reads=[bpss, C["bconst"]], writes=[brstd])
            s.op("dve", lambda e: e.reciprocal(rstd[:], rstd[:]), reads=[brstd], writes=[brstd])

        def rms_g(xT, bx):
            s.op("act", lambda e: e.activation(sq[:], xT[:], AF.Square), reads=[bx], writes=[bsq])
            yield
            s.mm([(pss[:, 0:NT], ones[:], sq[:, kc, :], kc == 0, kc == 7) for kc in range(8)],
                 reads=[bsq, C["bconst"]], writes=[bpss])
            yield
            s.op("act", lambda e: e.activation(rstd[:], pss[:, 0:NT], AF.Sqrt, bias=epsb[:, 0:1], scale=1.0 / D),
                 reads=[bpss, C["bconst"]], writes=[brstd])
            yield
            s.op("dve", lambda e: e.reciprocal(rstd[:], rstd[:]), reads=[brstd], writes=[brstd])
            yield

        def prologue(ti):
            t0s, t0d, wch = tiles[ti]
            xT = xTs[ti % 2]
            bx = bxT[ti % 2]
            hT, bhT = hTs[ti % 2], bhTs[ti % 2]
            A = tab[:, wch, sub * 3 + 0, :]
            B = tab[:, wch, sub * 3 + 1, :]
            s.dma("sp", xT[:], src[:, t0s:t0s + NT].rearrange("(kc p) t -> p kc t", p=128), writes=[bx])
            yield
            yield from rms_g(xT, bx)
            s.op("dve", lambda e: e.tensor_tensor(sq[:], xT[:], bc_mid(rstd[:], 8), ALU.mult),
                 reads=[bx, brstd], writes=[bsq])
            yield
            s.op("pool", lambda e: e.tensor_tensor(sq[:], sq[:], bc_last(A, NT), ALU.mult),
                 reads=[bsq, btab], writes=[bsq])
            yield
            s.op("pool", lambda e: e.tensor_tensor(hT[:], sq[:], bc_last(B, NT), ALU.add),
                 reads=[bsq, btab], writes=[bhT])

        def main(ti):
            t0s, t0d, wch = tiles[ti]
            xT = xTs[ti % 2]
            bx = bxT[ti % 2]
            hT, bhT = hTs[ti % 2], bhTs[ti % 2]
            HG = tab[:, wch, sub * 3 + 2, :]
            for m in range(NFF):
                pp = pgu[m % NPB]
                bpp = bpgu[m % NPB]
                s.mm([(pp[:, 0:NT], Win[:, kc, m * 128:(m + 1) * 128], hT[:, kc, :], kc == 0, kc == 7)
                      for kc in range(8)] +
                     [(pp[:, NT:2 * NT], Win[:, kc, DFF + m * 128:DFF + (m + 1) * 128], hT[:, kc, :], kc == 0, kc == 7)
                      for kc in range(8)], reads=[bWin, bhT], writes=[bpp])
                s.op("act", lambda e, m=m, pp=pp: e.activation(sg[m % NPB][:], pp[:, 0:NT], AF.Silu),
                     reads=[bpp], writes=[bsg[m % NPB]])
                s.op("dve", lambda e, m=m, pp=pp: e.tensor_tensor(aT[:, m, :], sg[m % NPB][:], pp[:, NT:2 * NT], ALU.mult),
                     reads=[bsg[m % NPB], bpp], writes=[baT])
                yield
            for dc in range(8):
                py = pys[dc % 2]
                s.mm([(py[:, 0:NT], Wout[:, m, dc * 128:(dc + 1) * 128], aT[:, m, :], m == 0, m == NFF - 1)
                      for m in range(NFF)], reads=[bWout, baT], writes=[bpy[dc % 2]])
                s.op("dve", lambda e, dc=dc, py=py: e.scalar_tensor_tensor(
                    xT[:, dc, :], py[:, 0:NT], HG[:, dc:dc + 1], xT[:, dc, :], ALU.mult, ALU.add),
                    reads=[bpy[dc % 2], bx, btab], writes=[bx])
                yield
            if dst is not None:
                s.dma("sp", dst[:, t0d:t0d + NT].rearrange("(kc p) t -> p kc t", p=128), xT[:], reads=[bx])
            if final_out is not None:
                rms(xT, bx)
                s.op("dve", lambda e: e.tensor_tensor(sq[:], xT[:], bc_mid(rstd[:], 8), ALU.mult),
                     reads=[bx, brstd], writes=[bsq])
                s.op("pool", lambda e: e.tensor_tensor(sq[:], sq[:], bc_last(C["fnw"][:], NT), ALU.mult),
                     reads=[bsq, btab], writes=[bsq])
                s.dma("sp", final_out[:, t0d:t0d + NT].rearrange("(kc p) t -> p kc t", p=128), sq[:], reads=[bsq])

        def run_all(gs):
            gs = list(gs)
            while gs:
                for g in list(gs):
                    try:
                        next(g)
                    except StopIteration:
                        gs.remove(g)

        run_all([prologue(0)])
        for ti in range(len(tiles)):
            gs = [main(ti)]
            if ti + 1 < len(tiles):
                gs.append(prologue(ti + 1))
            run_all(gs)
        s.barrier()


def rms_mod(k, C, cx_bufs, xT, bx, A, B, outT, bout):
    s = k.s
    sq, bsq, rstd, brstd, pss, bpss = cx_bufs
    s.op("act", lambda e: e.activation(sq[:], xT[:], AF.Square), reads=[bx], writes=[bsq])
    s.mm([(pss[:, 0:NT], C["ones"][:], sq[:, kc, :], kc == 0, kc == 7) for kc in range(8)],
         reads=[bsq, C["bconst"]], writes=[bpss])
    s.op("act", lambda e: e.activation(rstd[:], pss[:, 0:NT], AF.Sqrt, bias=C["eps"][:, 0:1], scale=1.0 / D),
         reads=[bpss, C["bconst"]], writes=[brstd])
    s.op("dve", lambda e: e.reciprocal(rstd[:], rstd[:]), reads=[brstd], writes=[brstd])
    s.op("dve", lambda e: e.tensor_tensor(sq[:], xT[:], bc_mid(rstd[:], 8), ALU.mult),
         reads=[bx, brstd], writes=[bsq])
    s.op("pool", lambda e: e.tensor_tensor(sq[:], sq[:], bc_last(A, NT), ALU.mult),
         reads=[bsq, C["btab"]], writes=[bsq])
    s.op("pool", lambda e: e.tensor_tensor(outT[:], sq[:], bc_last(B, NT), ALU.add),
         reads=[bsq, C["btab"]], writes=[bout])


def rms_mod_g(k, C, cx_bufs, xT, bx, A, B, outT, bout):
    s = k.s
    sq, bsq, rstd, brstd, pss, bpss = cx_bufs
    s.op("act", lambda e: e.activation(sq[:], xT[:], AF.Square), reads=[bx], writes=[bsq])
    yield
    s.mm([(pss[:, 0:NT], C["ones"][:], sq[:, kc, :], kc == 0, kc == 7) for kc in range(8)],
         reads=[bsq, C["bconst"]], writes=[bpss])
    yield
    s.op("act", lambda e: e.activation(rstd[:], pss[:, 0:NT], AF.Sqrt, bias=C["eps"][:, 0:1], scale=1.0 / D),
         reads=[bpss, C["bconst"]], writes=[brstd])
    yield
    s.op("dve", lambda e: e.reciprocal(rstd[:], rstd[:]), reads=[brstd], writes=[brstd])
    yield
    s.op("dve", lambda e: e.tensor_tensor(sq[:], xT[:], bc_mid(rstd[:], 8), ALU.mult),
         reads=[bx, brstd], writes=[bsq])
    yield
    s.op("pool", lambda e: e.tensor_tensor(sq[:], sq[:], bc_last(A, NT), ALU.mult),
         reads=[bsq, C["btab"]], writes=[bsq])
    yield
    s.op("pool", lambda e: e.tensor_tensor(outT[:], sq[:], bc_last(B, NT), ALU.add),
         reads=[bsq, C["btab"]], writes=[bout])


def pass2_mixin(k, C, tiles):
    s = k.s
    tab = C["tab"]
    with ExitStack() as cx:
        Wm = k.sb(cx, [128, 8, MIX_IN], BF16, "wmix")
        bWm = load_weight_bf16(k, Wm, k.w_mix_in, 8, pieces=2)
        xTs = [k.sb(cx, [128, 8, NT], F32, "xT") for _ in range(2)]
        bxT = [Buf(), Buf()]
        sq = k.sb(cx, [128, 8, NT], F32, "sq")
        rstd = k.sb(cx, [128, NT], F32, "rstd")
        uTs = [k.sb(cx, [128, 8, NT], BF16, "uT") for _ in range(2)]
        buTs = [Buf(), Buf()]
        pss = k.ps(cx, [128, 512], F32)
        cxb = (sq, Buf(), rstd, Buf(), pss, Buf())
        NPQ = 4
        pqb = [k.ps(cx, [128, 512], F32) for _ in range(NPQ)]
        pq = [pqb[i][:, 0:NT] for i in range(NPQ)]
        bpq = [Buf() for _ in range(NPQ)]
        ppl = [k.ps(cx, [128, 512], F32) for _ in range(2)]
        bppl = [Buf(), Buf()]
        pab = k.ps(cx, [128, 512], F32)
        bpab = Buf()
        stg = [k.sb(cx, [128, 4, NT], F32, "stg") for _ in range(3)]
        bstg = [Buf(), Buf(), Buf()]
        abs_ = k.sb(cx, [128, 2, 32], F32, "abs")
        babs = Buf()
        pps = k.sb(cx, [128, 2, 512], BF16, "pps")
        bpps = Buf()
        cnt = {"nst": 0, "nq": 0}

        def prologue(ti):
            t0, _, wch = tiles[ti]
            xT = xTs[ti % 2]
            bx = bxT[ti % 2]
            s.dma("sp", xT[:], k.X1T[:, t0:t0 + NT].rearrange("(kc p) t -> p kc t", p=128), writes=[bx])
            yield
            yield from rms_mod_g(k, C, cxb, xT, bx, tab[:, wch, 3, :], tab[:, wch, 4, :], uTs[ti % 2], buTs[ti % 2])

        def main(ti):
            t0, _, wch = tiles[ti]
            uT, buT = uTs[ti % 2], buTs[ti % 2]
            nst, nq = cnt["nst"], cnt["nq"]
            if True:
              s.mm([(pab[:, blk * 32:(blk + 1) * 32], uT[:, kc, blk * 128:(blk + 1) * 128], Wm[:, kc, QKV:AB_END],
                   kc == 0, kc == 7) for blk in range(2) for kc in range(8)], reads=[buT, bWm], writes=[bpab])
              s.op("dve", lambda e: e.tensor_copy(abs_[:], pab[:, 0:64].rearrange("p (b c) -> p b c", b=2)),
                 reads=[bpab], writes=[babs])
              s.dma("sp", k.ABT[t0:t0 + NT, :].rearrange("(b p) c -> p b c", p=128), abs_[:], reads=[babs])
            groups = [(0, 24, AF.Copy, k.PQKV, 0, t0)]
            if wch == 0:
                tx = t0 - CTX
                gsel = "123"
                allg = [(AB_END, 8, AF.Silu, k.SGATE, 0, tx), None,
                           (POOL_END, 16, AF.Sigmoid, k.GATES, 0, tx)]
                groups += [allg[int(ch) - 1] for ch in gsel if ch != "2"]
                for blk in range(2):
                    pp = ppl[blk]
                    bp = bppl[blk]
                    s.mm([(pp[:, :], uT[:, kc, blk * 128:(blk + 1) * 128], Wm[:, kc, GATE_END:POOL_END],
                           kc == 0, kc == 7) for kc in range(8)], reads=[buT, bWm], writes=[bp])
                    s.op("act", lambda e, pp=pp, blk=blk: e.activation(pps[:, blk, :], pp[:, :], AF.Copy),
                         reads=[bp], writes=[bpps])
                s.dma("sp", k.PPT[tx:tx + NT, :].rearrange("(b p) c -> p b c", p=128), pps[:], reads=[bpps])
            for (col0, nch, func, dram, row0, tq) in groups:
                for c0 in range(0, nch, 4):
                    st = stg[nst % 3]
                    bst = bstg[nst % 3]
                    nst += 1
                    for c in range(c0, c0 + 4):
                        pp = pq[nq % NPQ]
                        bp = bpq[nq % NPQ]
                        nq += 1
                        cc = col0 + c * 128
                        s.mm([(pp, Wm[:, kc, cc:cc + 128], uT[:, kc, :], kc == 0, kc == 7)
                              for kc in range(8)], reads=[buT, bWm], writes=[bp])
                        s.op("act", lambda e, st=st, c=c, c0=c0, pp=pp, func=func: e.activation(
                            st[:, c - c0, :], pp, func), reads=[bp], writes=[bst])
                    r0 = row0 + c0 * 128
                    s.dma("sp", dram[r0:r0 + 512, tq:tq + NT].rearrange("(c p) t -> p c t", p=128), st[:],
                          reads=[bst])
                    yield
            cnt["nst"], cnt["nq"] = nst, nq

        def run_all(gs):
            gs = list(gs)
            while gs:
                for g in list(gs):
                    try:
                        next(g)
                    except StopIteration:
                        gs.remove(g)

        run_all([prologue(0)])
        for ti in range(len(tiles)):
            gs = [main(ti)]
            if ti + 1 < len(tiles):
                gs.append(prologue(ti + 1))
            run_all(gs)
        s.barrier()


def pass3a_gdnprep(k, C, tiles):
    s = k.s
    NTOK = k.NTOK
    with ExitStack() as cx:
        cw = k.sb(cx, [128, 24, 5], F32, "cw")
        bcw = Buf()
        s.dma("sp", cw[:], k.convw, writes=[bcw])
        c2 = k.sb(cx, [128, 3, 128], F32, "c2")
        s.dma("sp", c2[:], k.cst2, writes=[bcw])
        identb = k.sb(cx, [128, 128], BF16, "identb")
        s.op("dve", lambda e: e.tensor_copy(identb[:], C["ident"]), reads=[C["bconst"]], writes=[bcw])
        adt = k.sb(cx, [128, 32], F32, "adt")
        s.dma("sp", adt[:], k.adt.partition_broadcast(128), writes=[bcw])
        nega = k.sb(cx, [128, 16], F32, "nega")
        s.op("act", lambda e: e.activation(nega[:], adt[:, 0:16], AF.Exp), reads=[bcw], writes=[bcw])
        s.op("dve", lambda e: e.tensor_scalar(nega[:], nega[:], -1.0, None, ALU.mult), reads=[bcw], writes=[bcw])
        cbias = k.sb(cx, [128, 4], F32, "cbias")
        s.op("dve", lambda e: e.memset(cbias[:, 0:1], EPS), writes=[bcw])
        s.op("dve", lambda e: e.memset(cbias[:, 1:2], 128.0 * EPS), writes=[bcw])
        s.op("dve", lambda e: e.memset(cbias[:, 2:3], 1.0), writes=[bcw])

        W = NT + 4
        NBF = 3
        pin4 = [k.sb(cx, [128, 4, W], F32, "pin4") for _ in range(NBF)]
        bpin = [Buf() for _ in range(NBF)]
        acc4 = [k.sb(cx, [128, 4, NT], F32, "acc4") for _ in range(NBF)]
        bacc = [Buf() for _ in range(NBF)]
        sact4 = [k.sb(cx, [128, 4, NT], F32, "sact4") for _ in range(NBF)]
        bsact = [Buf() for _ in range(NBF)]
        sq4s = [k.sb(cx, [128, 4, NT], F32, "sq4") for _ in range(2)]
        bsq4s = [Buf(), Buf()]
        rn4s = [k.sb(cx, [128, 4, NT], F32, "rn4") for _ in range(2)]
        brn4s = [Buf(), Buf()]
        knf4 = [k.sb(cx, [128, 4, NT], BF16, "knf4") for _ in range(NBF)]
        bknf = [Buf() for _ in range(NBF)]
        QTs = [k.sb(cx, [128, 4, 8, 64], BF16, "QTs") for _ in range(2)]
        KTs = [k.sb(cx, [128, 4, 8, 64], BF16, "KTs") for _ in range(2)]
        KTOKs = [k.sb(cx, [128, 2, 8, 128], BF16, "KTOKs") for _ in range(2)]
        VTOKs = [k.sb(cx, [128, 2, 8, 128], BF16, "VTOKs") for _ in range(2)]
        bQTs, bKTs, bKTOKs, bVTOKs = [Buf(), Buf()], [Buf(), Buf()], [Buf(), Buf()], [Buf(), Buf()]
        pns = [k.ps(cx, [128, 4, NT], F32) for _ in range(2)]
        bpns = [Buf(), Buf()]
        ptr = [k.ps(cx, [128, 2, 4, 128], BF16) for _ in range(2)]
        bptr = [Buf(), Buf()]
        pa = k.ps(cx, [128, 512], F32)
        bpa = Buf()
        pr = k.ps(cx, [128, 512], F32)
        bpr = Buf()
        abt = [k.sb(cx, [128, 32], F32, "abt") for _ in range(2)]
        babt = [Buf(), Buf()]
        aux = [k.sb(cx, [128, 5, 16], F32, "aux") for _ in range(2)]
        baux = [Buf(), Buf()]
        lb = k.sb(cx, [128, 16], F32, "lb")
        gt = k.sb(cx, [128, 16], F32, "gt")
        cl = k.sb(cx, [128, 16], F32, "cl")
        bl = [k.sb(cx, [128, 16], F32, "bl") for _ in range(2)]
        bbl = [Buf(), Buf()]
        rowt = [k.sb(cx, [8, 3, 2, 128], F32, "rowt") for _ in range(2)]
        browt = [Buf(), Buf()]
        bsm = Buf()
        state = {"ng": 0, "nblk": 0, "ntr": 0}
        done_groups = {}

        def grp(ti, t0, wch, g4):
                seq_lo = 0 if wch == 1 else CTX
                seq_hi = CTX if wch == 1 else NTOK
                qts, kts, ktoks, vtoks = QTs[ti % 2], KTs[ti % 2], KTOKs[ti % 2], VTOKs[ti % 2]
                kind = g4 // 2
                h0 = (g4 % 2) * 4
                ng = state["ng"]
                state["ng"] += 1
                pin = pin4[ng % NBF]
                bp = bpin[ng % NBF]
                acc = acc4[ng % NBF]
                ba = bacc[ng % NBF]
                sact = sact4[ng % NBF]
                bs = bsact[ng % NBF]
                sq4, bsq4, rn4, brn4, pn, bpn = sq4s[ng % 2], bsq4s[ng % 2], rn4s[ng % 2], brn4s[ng % 2], pns[ng % 2], bpns[ng % 2]
                ng += 1
                lo = max(t0 - 2, seq_lo)
                hi = min(t0 + NT + 2, seq_hi)
                if lo > t0 - 2:
                    s.op("pool", lambda e, pin=pin: e.memset(pin[:, :, 0:2], 0.0), writes=[bp])
                if hi < t0 + NT + 2:
                    s.op("pool", lambda e, pin=pin: e.memset(pin[:, :, W - 2:W], 0.0), writes=[bp])
                s.dma("sp", pin[:, :, lo - (t0 - 2):hi - (t0 - 2)],
                      k.PQKV[g4 * 512:(g4 + 1) * 512, lo:hi].rearrange("(c p) t -> p c t", p=128), writes=[bp])
                bai = [Buf() for _ in range(4)]
                for i in range(4):
                    cc = g4 * 4 + i
                    s.op("act", lambda e, i=i, cc=cc, pin=pin, acc=acc: e.activation(
                        acc[:, i, :], pin[:, i, 0:NT], AF.Copy, scale=cw[:, cc, 0:1]), reads=[bp, bcw], writes=[ba, bai[i]])
                yield
                for kk in range(1, 5):
                    for i in range(4):
                        cc = g4 * 4 + i
                        s.op("dve", lambda e, i=i, cc=cc, kk=kk, pin=pin, acc=acc: e.scalar_tensor_tensor(
                            acc[:, i, :], pin[:, i, kk:kk + NT], cw[:, cc, kk:kk + 1], acc[:, i, :],
                            ALU.mult, ALU.add), reads=[bp, bcw, bai[i]], writes=[bai[i]] + ([ba] if kk == 4 else []))
                    yield
                if kind == 2:
                    vb = knf4[ng % NBF]
                    bvb = bknf[ng % NBF]
                    s.op("act", lambda e, acc=acc, vb=vb: e.activation(vb[:], acc[:], AF.Silu), reads=[ba], writes=[bvb])
                    yield
                    src_t, bsrc, dstk, bdst = vb, bvb, vtoks, bVTOKs[ti % 2]
                else:
                    s.op("act", lambda e, acc=acc, sact=sact: e.activation(sact[:], acc[:], AF.Silu),
                         reads=[ba], writes=[bs])
                    yield
                    s.op("act", lambda e, sact=sact: e.activation(sq4[:], sact[:], AF.Square), reads=[bs], writes=[bsq4])
                    yield
                    s.mm([(pn[:, i, :], C["ones"], sq4[:, i, :], True, True) for i in range(4)],
                         reads=[bsq4, C["bconst"]], writes=[bpn])
                    yield
                    if kind == 0:
                        s.op("act", lambda e: e.activation(rn4[:], pn[:], AF.Ln, bias=cbias[:, 1:2], scale=128.0),
                             reads=[bpn, bcw], writes=[brn4])
                    else:
                        s.op("act", lambda e: e.activation(rn4[:], pn[:], AF.Ln, bias=cbias[:, 0:1], scale=1.0),
                             reads=[bpn, bcw], writes=[brn4])
                    yield
                    s.op("act", lambda e: e.activation(rn4[:], rn4[:], AF.Exp, scale=-0.5), reads=[brn4], writes=[brn4])
                    yield
                    if kind == 0:
                        for i in range(4):
                            s.op("dve", lambda e, i=i, sact=sact: e.tensor_tensor(
                                qts[:, :, h0 + i, :], sact[:, i, :].rearrange("p (c t) -> p c t", c=4),
                                rn4[:, i, :].rearrange("p (c t) -> p c t", c=4), ALU.mult),
                                reads=[bs, brn4], writes=[bQTs[ti % 2]])
                        done_groups[ti] = done_groups.get(ti, 0) + 1
                        return
                    kn = knf4[ng % NBF]
                    bkn = bknf[ng % NBF]
                    s.op("dve", lambda e, sact=sact, kn=kn: e.tensor_tensor(kn[:], sact[:], rn4[:], ALU.mult),
                         reads=[bs, brn4], writes=[bkn])
                    for i in range(4):
                        s.op("pool", lambda e, i=i, kn=kn: e.tensor_copy(
                            kts[:, :, h0 + i, :], kn[:, i, :].rearrange("p (c t) -> p c t", c=4)),
                            reads=[bkn], writes=[bKTs[ti % 2]])
                    yield
                    src_t, bsrc, dstk, bdst = kn, bkn, ktoks, bKTOKs[ti % 2]
                pt = ptr[state["ntr"] % 2]
                bpt = bptr[state["ntr"] % 2]
                state["ntr"] += 1
                s.tr([(pt[:, blk, i, :], src_t[:, i, blk * 128:(blk + 1) * 128], identb[:])
                      for blk in range(2) for i in range(4)], reads=[bsrc, bcw], writes=[bpt])
                yield
                s.op("act", lambda e, pt=pt, dstk=dstk: e.activation(dstk[:, :, h0:h0 + 4, :], pt[:], AF.Copy),
                     reads=[bpt], writes=[bdst])
                done_groups[ti] = done_groups.get(ti, 0) + 1

        def tail(ti, t0, wch):
            qts, kts, ktoks, vtoks = QTs[ti % 2], KTs[ti % 2], KTOKs[ti % 2], VTOKs[ti % 2]
            while done_groups.get(ti, 0) < 6:
                yield
            c0 = t0 // CH
            s.dma("sp", k.QT[c0:c0 + 4].rearrange("c d h t -> d c (h t)"), qts[:].rearrange("d c h t -> d c (h t)"),
                  reads=[bQTs[ti % 2]])
            s.dma("sp", k.KT[c0:c0 + 4].rearrange("c d h t -> d c (h t)"), kts[:].rearrange("d c h t -> d c (h t)"),
                  reads=[bKTs[ti % 2]])
            s.dma("sp", k.KTOK[t0:t0 + NT].rearrange("(b p) h d -> p b (h d)", p=128),
                  ktoks[:].rearrange("p b h d -> p b (h d)"), reads=[bKTOKs[ti % 2]])
            s.dma("sp", k.VTOK[t0:t0 + NT].rearrange("(b p) h d -> p b (h d)", p=128),
                  vtoks[:].rearrange("p b h d -> p b (h d)"), reads=[bVTOKs[ti % 2]])
            for blk in range(2):
                tb = t0 + blk * 128
                nblk = state["nblk"]
                state["nblk"] += 1
                ab = abt[nblk % 2]
                bab = babt[nblk % 2]
                ax = aux[nblk % 2]
                bax = baux[nblk % 2]
                blt = bl[nblk % 2]
                bblt = bbl[nblk % 2]
                rw = rowt[nblk % 2]
                brw = browt[nblk % 2]
                yield
                s.dma("sp", ab[:], k.ABT[tb:tb + 128, :], writes=[bab])
                s.op("act", lambda e, ab=ab, ax=ax: e.activation(ax[:, 2, :], ab[:, 0:16], AF.Sigmoid),
                     reads=[bab], writes=[bax])
                s.op("act", lambda e, ax=ax: e.activation(lb[:], ax[:, 2, :], AF.Ln), reads=[bax], writes=[bsm])
                s.op("dve", lambda e, ab=ab: e.tensor_tensor(gt[:], ab[:, 16:32], adt[:, 16:32], ALU.add),
                     reads=[bab, bcw, bsm], writes=[bsm])
                s.op("act", lambda e: e.activation(gt[:], gt[:], AF.Exp), reads=[bsm], writes=[bsm])
                s.op("act", lambda e: e.activation(gt[:], gt[:], AF.Ln, bias=cbias[:, 2:3], scale=1.0),
                     reads=[bsm, bcw], writes=[bsm])
                s.op("dve", lambda e: e.tensor_tensor(gt[:], gt[:], nega[:], ALU.mult), reads=[bsm, bcw], writes=[bsm])
                s.mm([(pa[:, 0:8], c2[:, 0, :], gt[:, 0:8], True, True),
                      (pa[:, 8:16], c2[:, 1, :], gt[:, 8:16], True, True),
                      (pa[:, 16:32], c2[:, 2, :], gt[:, 0:16], True, True)], reads=[bsm, bcw], writes=[bpa])
                s.mm([(pr[0:8, 0:128], gt[:, 0:8], c2[:, 0, :], True, True),
                      (pr[0:8, 128:256], gt[:, 8:16], c2[:, 1, :], True, True),
                      (pr[0:8, 256:384], gt[:, 0:8], c2[:, 0, :], True, False),
                      (pr[0:8, 256:384], lb[:, 0:8], C["ident"], False, True),
                      (pr[0:8, 384:512], gt[:, 8:16], c2[:, 1, :], True, False),
                      (pr[0:8, 384:512], lb[:, 8:16], C["ident"], False, True)],
                     reads=[bsm, bcw, C["bconst"]], writes=[bpr])
                s.op("dve", lambda e, ax=ax: e.tensor_copy(ax[:, 0, :], pa[:, 0:16]), reads=[bpa], writes=[bax])
                s.op("dve", lambda e: e.tensor_copy(cl[:], pa[:, 16:32]), reads=[bpa, bsm], writes=[bsm])
                s.op("dve", lambda e, ax=ax: e.tensor_tensor(ax[:, 1, :], ax[:, 0, :], lb[:], ALU.add),
                     reads=[bax, bsm], writes=[bax])
                s.op("act", lambda e, ax=ax: e.activation(ax[:, 3, :], ax[:, 1, :], AF.Exp), reads=[bax], writes=[bax])
                s.op("dve", lambda e, ax=ax: e.tensor_tensor(ax[:, 4, :], cl[:], ax[:, 0, :], ALU.subtract),
                     reads=[bax, bsm], writes=[bax])
                s.op("act", lambda e, ax=ax: e.activation(ax[:, 4, :], ax[:, 4, :], AF.Exp), reads=[bax], writes=[bax])
                s.op("act", lambda e, blt=blt: e.activation(blt[:], cl[:], AF.Exp), reads=[bsm], writes=[bblt])
                s.op("act", lambda e, rw=rw: e.activation(rw[:, 0:2, :, :].rearrange("p a b t -> p (a b t)"),
                                                         pr[0:8, :], AF.Copy), reads=[bpr], writes=[brw])
                s.op("act", lambda e, rw=rw: e.activation(rw[:, 2, :, :].rearrange("p b t -> p (b t)"),
                                                         pr[0:8, 0:256], AF.Exp), reads=[bpr], writes=[brw])
                s.dma("sp", k.AUXT[tb:tb + 128], ax[:], reads=[bax])
                s.dma("sp", k.BLT[tb:tb + 128], blt[:], reads=[bblt])
                for a in range(3):
                    s.dma("sp", k.AUXR[a, :, :, tb:tb + 128], rw[:, a, :, :], reads=[brw])
                yield

        items = []
        for ti, (t0, _, wch) in enumerate(tiles):
            for g4 in range(6):
                items.append(grp(ti, t0, wch, g4))
            items.append(tail(ti, t0, wch))
        active = []
        pos = 0
        WIN = 2
        while pos < len(items) or active:
            while len(active) < WIN and pos < len(items):
                active.append(items[pos])
                pos += 1
            for g in list(active):
                try:
                    next(g)
                except StopIteration:
                    active.remove(g)
        s.barrier()


class TB:
    def __init__(self, t):
        self.t = t
        self.b = Buf()


def pass3b_scan(k, C):
    s = k.s
    NTOK = k.NTOK
    NCH = NTOK // CH
    NCC = CTX // CH
    with ExitStack() as cx:
        def sbt(shape, dt, p="t"):
            return TB(k.sb(cx, shape, dt, p))

        msk = sbt([64, 4, 8, 64], F32, "msk")
        s.dma("sp", msk.t[:], k.masks, writes=[msk.b])
        idr = sbt([64, 8, 64], F32, "idr")
        s.dma("sp", idr.t[:], k.identrep, writes=[idr.b])
        S32 = [sbt([128, 8, 128], F32, "S32") for _ in range(2)]
        Sb = [sbt([128, 8, 128], BF16, "Sb") for _ in range(2)]
        for d in range(2):
            s.op("dve", lambda e, d=d: e.memset(S32[d].t[:], 0.0), writes=[S32[d].b])
            s.op("pool", lambda e, d=d: e.memset(Sb[d].t[:], 0.0), writes=[Sb[d].b])

        NSET = 2

        def mk():
            return dict(
                qT=sbt([128, 8, 64], BF16, "qT"), kT=sbt([128, 8, 64], BF16, "kT"),
                ktok=sbt([64, 8, 128], BF16, "ktok"), vtok=sbt([64, 8, 128], BF16, "vtok"),
                axt=sbt([64, 5, 16], F32, "axt"), R1=sbt([64, 8, 64], F32, "R1"), R2=sbt([64, 8, 64], F32, "R2"),
                EC=sbt([128, 8, 64], F32, "EC"), BL=sbt([128, 8], F32, "BL"),
                EA=sbt([64, 8, 64], F32, "EA"), EM=sbt([64, 8, 64], F32, "EM"), EL=sbt([64, 8, 64], F32, "EL"),
                aqk=sbt([64, 8, 64], BF16, "aqk"),
                Mp=[sbt([64, 8, 64], F32, "Mp") for _ in range(2)],
                Lp=[sbt([64, 8, 64], F32, "Lp") for _ in range(2)],
                P=[sbt([64, 8, 64], F32, "P") for _ in range(2)],
                TT=sbt([64, 8, 64], BF16, "TT"),
                bv=sbt([64, 8, 128], BF16, "bv"), kb=sbt([64, 8, 128], BF16, "kb"), kd=sbt([64, 8, 128], BF16, "kd"),
                u=sbt([64, 8, 128], F32, "u"), wT=sbt([128, 8, 64], BF16, "wT"), qd=sbt([128, 8, 64], BF16, "qd"),
                vn=sbt([64, 8, 128], BF16, "vn"), o=sbt([128, 8, 64], F32, "o"),
            )

        sets = [mk() for _ in range(NSET)]
        pA = TB(k.ps(cx, [128, 512], F32))
        pB = TB(k.ps(cx, [128, 512], F32))
        pC = TB(k.ps(cx, [128, 512], F32))
        pDE = TB(k.ps(cx, [128, 1024], F32))
        pF = TB(k.ps(cx, [128, 512], F32))
        pGH = TB(k.ps(cx, [128, 1024], F32))

        def v3(ap, n):
            return ap.rearrange("p (h n) -> p h n", h=8)

        def instance(n, ch, d, is_x):
            T = sets[n % NSET]
            tok0 = ch * CH
            d8 = slice(d * 8, (d + 1) * 8)
            s.dma("sp", T["qT"].t[:], k.QT[ch], writes=[T["qT"].b])
            s.dma("sp", T["kT"].t[:], k.KT[ch], writes=[T["kT"].b])
            s.dma("sp", T["ktok"].t[:], k.KTOK[tok0:tok0 + CH], writes=[T["ktok"].b])
            s.dma("sp", T["vtok"].t[:], k.VTOK[tok0:tok0 + CH], writes=[T["vtok"].b])
            s.dma("sp", T["axt"].t[:], k.AUXT[tok0:tok0 + CH], writes=[T["axt"].b])
            s.dma("sp", T["R1"].t[:], k.AUXR[0, :, d, tok0:tok0 + CH].partition_broadcast(64), writes=[T["R1"].b])
            s.dma("sp", T["R2"].t[:], k.AUXR[1, :, d, tok0:tok0 + CH].partition_broadcast(64), writes=[T["R2"].b])
            s.dma("sp", T["EC"].t[:], k.AUXR[2, :, d, tok0:tok0 + CH].partition_broadcast(128), writes=[T["EC"].b])
            s.dma("sp", T["BL"].t[:], k.BLT[tok0:tok0 + 1, d8].partition_broadcast(128), writes=[T["BL"].b])
            axt = T["axt"].t
            c_b = bc_last(axt[:, 0, d8], 64)
            cb_b = bc_last(axt[:, 1, d8], 64)
            m_incl = msk.t[:, 0 if d == 0 else 2, :, :]
            m_strT = msk.t[:, 1 if d == 0 else 3, :, :]
            m_strL = msk.t[:, 3 if d == 0 else 1, :, :]
            kT, qT = T["kT"], T["qT"]
            s.mm([(v3(pA.t[0:64, :], 64)[:, h, :], kT.t[:, h, :], kT.t[:, h, :], True, True) for h in range(8)],
                 reads=[kT.b], writes=[pA.b])
            s.mm([(v3(pB.t[0:64, :], 64)[:, h, :], kT.t[:, h, :], qT.t[:, h, :], True, True) for h in range(8)],
                 reads=[kT.b, qT.b], writes=[pB.b])
            G = v3(pA.t[0:64, :], 64)
            QK = v3(pB.t[0:64, :], 64)
            EA, EM, EL = T["EA"], T["EM"], T["EL"]
            s.op("dve", lambda e: e.tensor_tensor(EA.t[:], T["R1"].t[:], c_b, ALU.subtract),
                 reads=[T["R1"].b, T["axt"].b], writes=[EA.b])
            s.op("dve", lambda e: e.tensor_tensor(EM.t[:], T["R2"].t[:], c_b, ALU.subtract),
                 reads=[T["R2"].b, T["axt"].b], writes=[EM.b])
            s.op("dve", lambda e: e.scalar_tensor_tensor(EL.t[:], T["R1"].t[:], -1.0, cb_b, ALU.mult, ALU.add),
                 reads=[T["R1"].b, T["axt"].b], writes=[EL.b])
            for (E, mk_) in ((EA, m_incl), (EM, m_strT), (EL, m_strL)):
                s.op("pool", lambda e, E=E: e.tensor_scalar(E.t[:], E.t[:], 0.0, None, ALU.min),
                     reads=[E.b], writes=[E.b])
                s.op("pool", lambda e, E=E, mk_=mk_: e.tensor_tensor(E.t[:], E.t[:], mk_, ALU.add),
                     reads=[E.b, msk.b], writes=[E.b])
                s.op("act", lambda e, E=E: e.activation(E.t[:], E.t[:], AF.Exp), reads=[E.b], writes=[E.b])
            aqk = T["aqk"]
            s.op("dve", lambda e: e.tensor_tensor(aqk.t[:], QK, EA.t[:], ALU.mult), reads=[pB.b, EA.b], writes=[aqk.b])
            Mp, Lp, P = T["Mp"], T["Lp"], T["P"]
            s.op("dve", lambda e: e.tensor_tensor(Mp[0].t[:], G, EM.t[:], ALU.mult), reads=[pA.b, EM.b], writes=[Mp[0].b])
            s.op("dve", lambda e: e.tensor_tensor(Lp[0].t[:], G, EL.t[:], ALU.mult), reads=[pA.b, EL.b], writes=[Lp[0].b])
            s.op("dve", lambda e: e.tensor_tensor(P[0].t[:], idr.t[:], Mp[0].t[:], ALU.subtract),
                 reads=[idr.b, Mp[0].b], writes=[P[0].b])
            cur = 0
            pc = 0
            for lvl in range(1, 6):
                nxt = 1 - cur
                Lc, Mc, Ln, Mn = Lp[cur], Mp[cur], Lp[nxt], Mp[nxt]
                pa3, pb3, pc3 = v3(pA.t[0:64, :], 64), v3(pB.t[0:64, :], 64), v3(pC.t[0:64, :], 64)
                s.mm([(pb3[:, h, :], Mc.t[:, h, :], Lc.t[:, h, :], True, True) for h in range(8)],
                     reads=[Mc.b, Lc.b], writes=[pB.b])
                if lvl < 5:
                    s.mm([(pa3[:, h, :], Lc.t[:, h, :], Mc.t[:, h, :], True, True) for h in range(8)],
                         reads=[Mc.b, Lc.b], writes=[pA.b])
                s.op("act", lambda e, Ln=Ln: e.activation(Ln.t[:], pb3, AF.Copy), reads=[pB.b], writes=[Ln.b])
                if lvl < 5:
                    s.op("act", lambda e, Mn=Mn: e.activation(Mn.t[:], pa3, AF.Copy), reads=[pA.b], writes=[Mn.b])
                Pc, Pn = P[pc], P[1 - pc]
                s.mm([(pc3[:, h, :], Ln.t[:, h, :], Pc.t[:, h, :], True, True) for h in range(8)],
                     reads=[Ln.b, Pc.b], writes=[pC.b])
                if lvl < 5:
                    s.op("dve", lambda e, Pc=Pc, Pn=Pn: e.tensor_tensor(Pn.t[:], Pc.t[:], pc3, ALU.add),
                         reads=[Pc.b, pC.b], writes=[Pn.b])
                else:
                    s.op("dve", lambda e, Pc=Pc: e.tensor_tensor(T["TT"].t[:], Pc.t[:], pc3, ALU.add),
                         reads=[Pc.b, pC.b], writes=[T["TT"].b])
                cur = nxt
                pc = 1 - pc
            TT = T["TT"]
            bv, kb, kd, qd = T["bv"], T["kb"], T["kd"], T["qd"]
            s.op("pool", lambda e: e.tensor_tensor(bv.t[:], T["vtok"].t[:], bc_last(axt[:, 2, d8], 128), ALU.mult),
                 reads=[T["vtok"].b, T["axt"].b], writes=[bv.b])
            s.op("pool", lambda e: e.tensor_tensor(kb.t[:], T["ktok"].t[:], bc_last(axt[:, 3, d8], 128), ALU.mult),
                 reads=[T["ktok"].b, T["axt"].b], writes=[kb.b])
            s.op("pool", lambda e: e.tensor_tensor(kd.t[:], T["ktok"].t[:], bc_last(axt[:, 4, d8], 128), ALU.mult),
                 reads=[T["ktok"].b, T["axt"].b], writes=[kd.b])
            s.op("pool", lambda e: e.tensor_tensor(qd.t[:], qT.t[:], T["EC"].t[:], ALU.mult),
                 reads=[qT.b, T["EC"].b], writes=[qd.b])
            pu = pDE.t[0:64, :].rearrange("p (h n) -> p h n", h=8)
            pw = v3(pF.t[:, :], 64)
            s.mm([(pu[:, h, :], TT.t[:, h, :], bv.t[:, h, :], True, True) for h in range(8)],
                 reads=[TT.b, bv.b], writes=[pDE.b])
            s.mm([(pw[:, h, :], kb.t[:, h, :], TT.t[:, h, :], True, True) for h in range(8)],
                 reads=[TT.b, kb.b], writes=[pF.b])
            u, wT = T["u"], T["wT"]
            s.op("act", lambda e: e.activation(u.t[:], pu, AF.Copy), reads=[pDE.b], writes=[u.b])
            s.op("act", lambda e: e.activation(wT.t[:], pw, AF.Copy), reads=[pF.b], writes=[wT.b])
            Sd, S3 = Sb[d], S32[d]
            s.mm([(pu[:, h, :], wT.t[:, h, :], Sd.t[:, h, :], True, True) for h in range(8)],
                 reads=[wT.b, Sd.b], writes=[pDE.b])
            vn = T["vn"]
            s.op("dve", lambda e: e.tensor_tensor(vn.t[:], u.t[:], pu, ALU.subtract), reads=[u.b, pDE.b], writes=[vn.b])
            if is_x:
                mms = []
                for h in range(8):
                    mms.append((pw[:, h, :], Sd.t[:, h, :], qd.t[:, h, :], True, False))
                    mms.append((pw[:, h, :], vn.t[:, h, :], aqk.t[:, h, :], False, True))
                s.mm(mms, reads=[Sd.b, qd.b, vn.b, aqk.b], writes=[pF.b])
                o = T["o"]
                s.op("act", lambda e: e.activation(o.t[:], pw, AF.Copy), reads=[pF.b], writes=[o.b])
                s.dma("sp", k.OT[d, ch - NCC], o.t[:], reads=[o.b])
            pS = pGH.t[:, :].rearrange("p (h n) -> p h n", h=8)
            s.mm([(pS[:, h, :], kd.t[:, h, :], vn.t[:, h, :], True, True) for h in range(8)],
                 reads=[kd.b, vn.b], writes=[pGH.b])
            s.op("pool", lambda e: e.tensor_tensor(S3.t[:], S3.t[:], bc_last(T["BL"].t[:, :], 128), ALU.mult),
                 reads=[S3.b, T["BL"].b], writes=[S3.b])
            s.op("dve", lambda e: e.tensor_tensor(S3.t[:], S3.t[:], pS, ALU.add), reads=[S3.b, pGH.b], writes=[S3.b])
            s.op("act", lambda e: e.activation(Sd.t[:], S3.t[:], AF.Copy), reads=[S3.b], writes=[Sd.b])

        order_f = list(range(NCH))
        order_b = list(range(NCC - 1, -1, -1)) + list(range(NCH - 1, NCC - 1, -1))
        n = 0
        for st in range(NCH):
            instance(n, order_f[st], 0, order_f[st] >= NCC)
            n += 1
            instance(n, order_b[st], 1, order_b[st] >= NCC)
            n += 1
        s.barrier()


def pool_operators(SEQ):
    R = SEQ // GRID_W
    NB = SEQ // 128
    wins = (2, 4, 8, 16)
    mats = []
    index = {}
    cache = {}
    cc = np.arange(GRID_W)
    for g, w in enumerate(wins):
        clo = np.clip(cc - w // 2, 0, GRID_W)
        chi = np.clip(cc + w - w // 2, 0, GRID_W)
        cm = ((cc[:, None] >= clo[None, :]) & (cc[:, None] < chi[None, :])).astype(np.float64)
        car = (chi - clo).astype(np.float64)
        maxoff = (w // 2 + 1) // 2 + 1
        for b in range(NB):
            for off in range(-maxoff, maxoff + 1):
                bp = b + off
                if bp < 0 or bp >= NB:
                    continue
                M = np.zeros((128, 128), np.float64)
                for ro in range(2):
                    r = 2 * b + ro
                    rlo = max(r - w // 2, 0)
                    rhi = min(r + w - w // 2, R)
                    for ri in range(2):
                        rp = 2 * bp + ri
                        if rlo <= rp < rhi:
                            M[ri * 64:(ri + 1) * 64, ro * 64:(ro + 1) * 64] = cm / (car[None, :] * (rhi - rlo))
                if off == 0:
                    M -= np.eye(128)
                if not M.any():
                    continue
                key = (g, off, M.tobytes())
                if key not in cache:
                    cache[key] = len(mats)
                    mats.append(M.astype(np.float32))
                index[(g, b, off)] = cache[key]
    return np.stack(mats, 0), index


def pass4_merge(k, C):
    s = k.s
    SEQ = k.SEQ
    NB = SEQ // 128
    tab = C["tab"]
    with ExitStack() as cx:
        Wg = k.sb(cx, [128, 8, D], BF16, "wg")
        Wp = k.sb(cx, [128, 4, D], BF16, "wp")
        Wmo = k.sb(cx, [128, 8, D], BF16, "wmo")
        bW = load_weight_bf16(k, Wg, k.w_gdn_proj, 8)
        bW2 = load_weight_bf16(k, Wp, k.w_pool_proj, 4)
        bW3 = load_weight_bf16(k, Wmo, k.w_mix_out, 8)
        nmat = k.opm_n
        opm = k.sb(cx, [128, nmat, 128], BF16, "opm")
        poolw = k.sb(cx, [128, 4, 128], BF16, "poolw")
        bW4 = Buf()
        s.dma("pool", opm[:], k.opm.rearrange("n p t -> p n t"), writes=[bW4])
        s.dma("pool", poolw[:], k.poolw, writes=[bW4])
        small = k.sb(cx, [128, 8], F32, "small")
        s.dma("sp", small[:, 0:1], k.gnw, writes=[bW4])
        s.dma("sp", small[:, 1:5], k.pscale, writes=[bW4])
        cb2 = k.sb(cx, [128, 2], F32, "cb2")
        s.op("dve", lambda e: e.memset(cb2[:, 0:1], EPS), writes=[bW4])
        wts = [bW, bW2, bW3, bW4]

        def dbl(shape, dt, p):
            return [TB(k.sb(cx, shape, dt, p)) for _ in range(2)]

        x1T = dbl([128, 8, NT], F32, "x1T")
        of = [TB(k.sb(cx, [128, 4, 8, 64], F32, "of"))] * 2
        ob = [TB(k.sb(cx, [128, 4, 8, 64], F32, "ob"))] * 2
        sgT = dbl([128, 8, NT], F32, "sgT")
        gts = [TB(k.sb(cx, [128, 16, NT], F32, "gts"))] * 2
        ppt = dbl([128, 10, 512], BF16, "ppt")
        o = TB(k.sb(cx, [128, 8, NT], F32, "o"))
        sq = TB(k.sb(cx, [128, 8, NT], F32, "sq"))
        rs = sq
        og = TB(k.sb(cx, [128, 8, NT], BF16, "og"))
        pd = TB(k.sb(cx, [128, 4, NT], BF16, "pd"))
        yp1 = TB(k.sb(cx, [128, 4, NT], BF16, "yp1"))
        t1 = dbl([128, NT], F32, "t1")
        t2 = dbl([128, NT], F32, "t2")
        msT = TB(k.sb(cx, [128, 8, NT], BF16, "msT"))
        pn = TB(k.ps(cx, [128, 4, NT], F32))
        pgp = [TB(k.ps(cx, [128, 2, NT], F32)) for _ in range(2)]
        ppd = TB(k.ps(cx, [128, 4, NT], F32))
        pmx = [TB(k.ps(cx, [128, 512], F32)) for _ in range(2)]
        for ti in range(SEQ // NT):
            tx0 = ti * NT
            cx0 = tx0 // CH
            b0 = tx0 // 128
            X, OF, OB, SG, GT, PP = x1T[ti % 2], of[ti % 2], ob[ti % 2], sgT[ti % 2], gts[ti % 2], ppt[ti % 2]
            s.dma("sp", X.t[:], k.X1T[:, CTX + tx0:CTX + tx0 + NT].rearrange("(kc p) t -> p kc t", p=128), writes=[X.b])
            s.dma("sp", OF.t[:].rearrange("p c h t -> p c (h t)"),
                  k.OT[0, cx0:cx0 + 4].rearrange("c d h t -> d c (h t)"), writes=[OF.b])
            s.dma("sp", OB.t[:].rearrange("p c h t -> p c (h t)"),
                  k.OT[1, cx0:cx0 + 4].rearrange("c d h t -> d c (h t)"), writes=[OB.b])
            s.dma("sp", SG.t[:], k.SGATE[:, tx0:tx0 + NT].rearrange("(kc p) t -> p kc t", p=128), writes=[SG.b])
            s.dma("sp", GT.t[:], k.GATES[:, tx0:tx0 + NT].rearrange("(kc p) t -> p kc t", p=128), writes=[GT.b])
            blo = max(b0 - 4, 0)
            bhi = min(b0 + 6, NB)
            s.dma("sp", PP.t[:, 0:bhi - blo, :], k.PPT[blo * 128:bhi * 128, :].rearrange("(b p) c -> p b c", p=128),
                  writes=[PP.b])
            o4 = o.t[:].rearrange("p h (c t) -> p h c t", c=4)
            s.op("pool", lambda e, OF=OF, OB=OB: e.tensor_tensor(
                o4, OF.t[:].rearrange("p c h t -> p h c t"), OB.t[:].rearrange("p c h t -> p h c t"), ALU.add),
                reads=[OF.b, OB.b], writes=[o.b])
            s.op("act", lambda e: e.activation(sq.t[:], o.t[:], AF.Square), reads=[o.b], writes=[sq.b])
            for hh in range(2):
                s.mm([(pn.t[:, i, :], C["ones"], sq.t[:, hh * 4 + i, :], True, True) for i in range(4)],
                     reads=[sq.b, C["bconst"]], writes=[pn.b])
                s.op("act", lambda e, hh=hh: e.activation(rs.t[:, hh * 4:hh * 4 + 4, :], pn.t[:], AF.Ln,
                                                          bias=cb2[:, 0:1], scale=1.0 / HD),
                     reads=[pn.b, bW4], writes=[rs.b])
            s.op("act", lambda e: e.activation(rs.t[:], rs.t[:], AF.Exp, scale=-0.5), reads=[rs.b], writes=[rs.b])
            s.op("dve", lambda e: e.tensor_tensor(o.t[:], o.t[:], rs.t[:], ALU.mult), reads=[o.b, rs.b], writes=[o.b])
            s.op("dve", lambda e, SG=SG: e.scalar_tensor_tensor(
                og.t[:].rearrange("p h t -> p (h t)"), o.t[:].rearrange("p h t -> p (h t)"), small[:, 0:1],
                SG.t[:].rearrange("p h t -> p (h t)"), ALU.mult, ALU.mult), reads=[o.b, SG.b, bW4], writes=[og.b])
            mms = []
            for g in range(4):
                for obk in range(2):
                    b = b0 + obk
                    offs = [off for off in range(-5, 6) if (g, b, off) in k.opm_index]
                    for j, off in enumerate(offs):
                        mms.append((ppd.t[:, g, obk * 128:(obk + 1) * 128],
                                    PP.t[:, b + off - blo, g * 128:(g + 1) * 128],
                                    opm[:, k.opm_index[(g, b, off)], :], j == 0, j == len(offs) - 1))
            s.mm(mms, reads=[PP.b, bW4], writes=[ppd.b])
            s.op("act", lambda e: e.activation(pd.t[:], ppd.t[:], AF.Copy), reads=[ppd.b], writes=[pd.b])
            s.mm([(ppd.t[:, g, :], poolw[:, g, :], pd.t[:, g, :], True, True) for g in range(4)],
                 reads=[pd.b, bW4], writes=[ppd.b])
            s.op("dve", lambda e: e.tensor_tensor(yp1.t[:], ppd.t[:], bc_last(small[:, 1:5], NT), ALU.mult),
                 reads=[ppd.b, bW4], writes=[yp1.b])
            for dc in range(8):
                pg = pgp[dc % 2]
                mms = [(pg.t[:, 0, :], Wg[:, h, dc * 128:(dc + 1) * 128], og.t[:, h, :], h == 0, h == 7)
                       for h in range(8)]
                mms += [(pg.t[:, 1, :], Wp[:, g, dc * 128:(dc + 1) * 128], yp1.t[:, g, :], g == 0, g == 3)
                        for g in range(4)]
                s.mm(mms, reads=[og.b, yp1.b] + wts, writes=[pg.b])
                a1, a2 = t1[dc % 2], t2[dc % 2]
                s.op("dve", lambda e, pg=pg, a1=a1, GT=GT, dc=dc: e.tensor_tensor(
                    a1.t[:], pg.t[:, 0, :], GT.t[:, 8 + dc, :], ALU.mult), reads=[pg.b, GT.b], writes=[a1.b])
                s.op("dve", lambda e, pg=pg, a2=a2, GT=GT, dc=dc: e.tensor_tensor(
                    a2.t[:], pg.t[:, 1, :], GT.t[:, dc, :], ALU.mult), reads=[pg.b, GT.b], writes=[a2.b])
                s.op("pool", lambda e, a1=a1, a2=a2, dc=dc: e.tensor_tensor(msT.t[:, dc, :], a1.t[:], a2.t[:], ALU.add),
                     reads=[a1.b, a2.b], writes=[msT.b])
            for dc in range(8):
                pm = pmx[dc % 2]
                s.mm([(pm.t[:, 0:NT], Wmo[:, kc, dc * 128:(dc + 1) * 128], msT.t[:, kc, :], kc == 0, kc == 7)
                      for kc in range(8)], reads=[msT.b] + wts, writes=[pm.b])
                s.op("dve", lambda e, pm=pm, X=X, dc=dc: e.scalar_tensor_tensor(
                    X.t[:, dc, :], pm.t[:, 0:NT], tab[:, 0, 5, dc:dc + 1], X.t[:, dc, :], ALU.mult, ALU.add),
                    reads=[pm.b, X.b, C["btab"]], writes=[X.b])
            s.dma("sp", k.X2T[:, tx0:tx0 + NT].rearrange("(kc p) t -> p kc t", p=128), X.t[:], reads=[X.b])
        s.barrier()


def pass3b_scan2(k, C):
    s = k.s
    NTOK = k.NTOK
    NCH = NTOK // CH
    NCC = CTX // CH
    with ExitStack() as cx:
        def sbt(shape, dt, p="t"):
            return TB(k.sb(cx, shape, dt, p))

        msk = sbt([128, 3, 8, 64], F32, "msk")
        s.dma("sp", msk.t[:], k.masks2, writes=[msk.b])
        idr = sbt([128, 8, 64], F32, "idr")
        s.dma("sp", idr.t[:], k.identrep2, writes=[idr.b])
        S32 = [sbt([128, 8, 128], F32, "S32") for _ in range(2)]
        Sb = [sbt([128, 8, 128], BF16, "Sb") for _ in range(2)]
        for d in range(2):
            s.op("dve", lambda e, d=d: e.memset(S32[d].t[:], 0.0), writes=[S32[d].b])
            s.op("pool", lambda e, d=d: e.memset(Sb[d].t[:], 0.0), writes=[Sb[d].b])

        def mk_loaded():
            return dict(
                qT=[sbt([128, 8, 64], BF16, "qT") for _ in range(2)], kT=[sbt([128, 8, 64], BF16, "kT") for _ in range(2)],
                EC=[sbt([128, 8, 64], F32, "EC") for _ in range(2)], BL=[sbt([128, 8], F32, "BL") for _ in range(2)],
                ktok=sbt([128, 8, 128], BF16, "ktok"), vtok=sbt([128, 8, 128], BF16, "vtok"),
                axt=sbt([128, 5, 8], F32, "axt"), R1=sbt([128, 8, 64], F32, "R1"), R2=sbt([128, 8, 64], F32, "R2"))

        def mk_comp():
            return dict(
                EA=sbt([128, 8, 64], F32, "EA"), EM=sbt([128, 8, 64], F32, "EM"), EL=sbt([128, 8, 64], F32, "EL"),
                aqk=sbt([128, 8, 64], BF16, "aqk"),
                Mp=[sbt([128, 8, 64], F32, "Mp") for _ in range(2)],
                Lp=[sbt([128, 8, 64], F32, "Lp") for _ in range(2)],
                P=[sbt([128, 8, 64], F32, "P") for _ in range(2)],
                TT=sbt([128, 8, 64], BF16, "TT"),
                bv=sbt([128, 8, 128], BF16, "bv"), kb=sbt([128, 8, 128], BF16, "kb"), kd=sbt([128, 8, 128], BF16, "kd"),
                u=sbt([128, 8, 128], F32, "u"), vn=sbt([128, 8, 128], BF16, "vn"),
                wT=[sbt([128, 8, 64], BF16, "wT") for _ in range(2)], qd=[sbt([128, 8, 64], BF16, "qd") for _ in range(2)],
                o=[sbt([128, 8, 64], F32, "o") for _ in range(2)])

        LS = [mk_loaded() for _ in range(3)]
        CS = [mk_comp() for _ in range(2)]
        pAB = TB(k.ps(cx, [128, 1024], F32))
        pC = TB(k.ps(cx, [128, 512], F32))
        pWS = TB(k.ps(cx, [128, 1024], F32))
        pO = TB(k.ps(cx, [128, 512], F32))
        pS2 = TB(k.ps(cx, [128, 1024], F32))

        def v64(ap):
            return ap.rearrange("p (h n) -> p h n", h=8)

        pa3 = v64(pAB.t[:, 0:512])
        pb3 = v64(pAB.t[:, 512:1024])
        pc3 = v64(pC.t[:, :])
        pu3 = v64(pAB.t[:, :])
        pws3 = v64(pWS.t[:, :])
        po3 = v64(pO.t[:, :])
        ps3 = v64(pS2.t[:, :])
        HF = [slice(0, 64), slice(64, 128)]
        order = [list(range(NCH)), list(range(NCC - 1, -1, -1)) + list(range(NCH - 1, NCC - 1, -1))]

        def loads(st):
            L = LS[st % 3]
            for d in range(2):
                ch = order[d][st]
                tok0 = ch * CH
                d8 = slice(d * 8, (d + 1) * 8)
                hf = HF[d]
                s.dma("sp", L["qT"][d].t[:], k.QT[ch], writes=[L["qT"][d].b])
                s.dma("sp", L["kT"][d].t[:], k.KT[ch], writes=[L["kT"][d].b])
                s.dma("sp", L["ktok"].t[hf], k.KTOK[tok0:tok0 + CH], writes=[L["ktok"].b])
                s.dma("sp", L["vtok"].t[hf], k.VTOK[tok0:tok0 + CH], writes=[L["vtok"].b])
                s.dma("sp", L["axt"].t[hf], k.AUXT[tok0:tok0 + CH, :, d8], writes=[L["axt"].b])
                s.dma("sp", L["R1"].t[hf], k.AUXR[0, :, d, tok0:tok0 + CH].partition_broadcast(64), writes=[L["R1"].b])
                s.dma("sp", L["R2"].t[hf], k.AUXR[1, :, d, tok0:tok0 + CH].partition_broadcast(64), writes=[L["R2"].b])
                s.dma("sp", L["EC"][d].t[:], k.AUXR[2, :, d, tok0:tok0 + CH].partition_broadcast(128),
                      writes=[L["EC"][d].b])
                s.dma("sp", L["BL"][d].t[:], k.BLT[tok0:tok0 + 1, d8].partition_broadcast(128), writes=[L["BL"][d].b])

        def prep(st):
            L = LS[st % 3]
            T = CS[st % 2]
            axt = L["axt"].t
            c_b = bc_last(axt[:, 0, :], 64)
            cb_b = bc_last(axt[:, 1, :], 64)
            kT, qT = L["kT"], L["qT"]
            s.mm([(pa3[HF[d], h, :], kT[d].t[:, h, :], kT[d].t[:, h, :], True, True) for h in range(8) for d in range(2)],
                 reads=[kT[0].b, kT[1].b], writes=[pAB.b])
            s.mm([(pb3[HF[d], h, :], kT[d].t[:, h, :], qT[d].t[:, h, :], True, True) for h in range(8) for d in range(2)],
                 reads=[kT[0].b, kT[1].b, qT[0].b, qT[1].b], writes=[pAB.b])
            EA, EM, EL = T["EA"], T["EM"], T["EL"]
            s.op("dve", lambda e: e.tensor_tensor(EA.t[:], L["R1"].t[:], c_b, ALU.subtract),
                 reads=[L["R1"].b, L["axt"].b], writes=[EA.b])
            s.op("dve", lambda e: e.tensor_tensor(EM.t[:], L["R2"].t[:], c_b, ALU.subtract),
                 reads=[L["R2"].b, L["axt"].b], writes=[EM.b])
            s.op("dve", lambda e: e.scalar_tensor_tensor(EL.t[:], L["R1"].t[:], -1.0, cb_b, ALU.mult, ALU.add),
                 reads=[L["R1"].b, L["axt"].b], writes=[EL.b])
            yield
            for (E, mi) in ((EA, 0), (EM, 1), (EL, 2)):
                s.op("dve", lambda e, E=E, mi=mi: e.scalar_tensor_tensor(E.t[:], E.t[:], 0.0, msk.t[:, mi, :, :],
                                                                       ALU.min, ALU.add),
                     reads=[E.b, msk.b], writes=[E.b])
            yield
            for E in (EA, EM, EL):
                s.op("act", lambda e, E=E: e.activation(E.t[:], E.t[:], AF.Exp), reads=[E.b], writes=[E.b])
            yield
            aqk = T["aqk"]
            Mp, Lp, P = T["Mp"], T["Lp"], T["P"]
            s.op("dve", lambda e: e.tensor_tensor(Mp[0].t[:], pa3, EM.t[:], ALU.mult), reads=[pAB.b, EM.b], writes=[Mp[0].b])
            s.op("dve", lambda e: e.tensor_tensor(Lp[0].t[:], pa3, EL.t[:], ALU.mult), reads=[pAB.b, EL.b], writes=[Lp[0].b])
            s.op("dve", lambda e: e.tensor_tensor(aqk.t[:], pb3, EA.t[:], ALU.mult), reads=[pAB.b, EA.b], writes=[aqk.b])
            s.op("dve", lambda e: e.tensor_tensor(P[0].t[:], idr.t[:], Mp[0].t[:], ALU.subtract),
                 reads=[idr.b, Mp[0].b], writes=[P[0].b])
            bv, kb, kd, qd = T["bv"], T["kb"], T["kd"], T["qd"]
            s.op("pool", lambda e: e.tensor_tensor(bv.t[:], L["vtok"].t[:], bc_last(axt[:, 2, :], 128), ALU.mult),
                 reads=[L["vtok"].b, L["axt"].b], writes=[bv.b])
            s.op("pool", lambda e: e.tensor_tensor(kb.t[:], L["ktok"].t[:], bc_last(axt[:, 3, :], 128), ALU.mult),
                 reads=[L["ktok"].b, L["axt"].b], writes=[kb.b])
            yield
            cur = 0
            pcx = 0
            for lvl in range(1, 6):
                nxt = 1 - cur
                Lc, Mc, Ln, Mn = Lp[cur], Mp[cur], Lp[nxt], Mp[nxt]
                s.mm([(pb3[HF[d], h, :], Mc.t[HF[d], h, :], Lc.t[HF[d], h, :], True, True)
                      for h in range(8) for d in range(2)], reads=[Mc.b, Lc.b], writes=[pAB.b])
                if lvl < 5:
                    s.mm([(pa3[HF[d], h, :], Lc.t[HF[d], h, :], Mc.t[HF[d], h, :], True, True)
                          for h in range(8) for d in range(2)], reads=[Mc.b, Lc.b], writes=[pAB.b])
                yield
                s.op("act", lambda e, Ln=Ln: e.activation(Ln.t[:], pb3, AF.Copy), reads=[pAB.b], writes=[Ln.b])
                if lvl < 5:
                    s.op("act", lambda e, Mn=Mn: e.activation(Mn.t[:], pa3, AF.Copy), reads=[pAB.b], writes=[Mn.b])
                if lvl == 1:
                    s.op("pool", lambda e: e.tensor_tensor(kd.t[:], L["ktok"].t[:], bc_last(axt[:, 4, :], 128), ALU.mult),
                         reads=[L["ktok"].b, L["axt"].b], writes=[kd.b])
                if lvl == 2:
                    for d in range(2):
                        s.op("pool", lambda e, d=d: e.tensor_tensor(qd[d].t[:], qT[d].t[:], L["EC"][d].t[:], ALU.mult),
                             reads=[qT[d].b, L["EC"][d].b], writes=[qd[d].b])
                yield
                Pc, Pn = P[pcx], P[1 - pcx]
                s.mm([(pc3[HF[d], h, :], Ln.t[HF[d], h, :], Pc.t[HF[d], h, :], True, True)
                      for h in range(8) for d in range(2)], reads=[Ln.b, Pc.b], writes=[pC.b])
                yield
                if lvl < 5:
                    s.op("dve", lambda e, Pc=Pc, Pn=Pn: e.tensor_tensor(Pn.t[:], Pc.t[:], pc3, ALU.add),
                         reads=[Pc.b, pC.b], writes=[Pn.b])
                else:
                    s.op("dve", lambda e, Pc=Pc: e.tensor_tensor(T["TT"].t[:], Pc.t[:], pc3, ALU.add),
                         reads=[Pc.b, pC.b], writes=[T["TT"].b])
                cur = nxt
                pcx = 1 - pcx
            yield
            TT = T["TT"]
            s.mm([(pu3[HF[d], h, :], TT.t[HF[d], h, :], bv.t[HF[d], h, :], True, True)
                  for h in range(8) for d in range(2)], reads=[TT.b, bv.b], writes=[pAB.b])
            yield
            s.op("act", lambda e: e.activation(T["u"].t[:], pu3, AF.Copy), reads=[pAB.b], writes=[T["u"].b])
            for d in range(2):
                s.mm([(pc3[:, h, :], kb.t[HF[d], h, :], TT.t[HF[d], h, :], True, True) for h in range(8)],
                     reads=[TT.b, kb.b], writes=[pC.b])
                yield
                s.op("act", lambda e, d=d: e.activation(T["wT"][d].t[:], pc3, AF.Copy), reads=[pC.b], writes=[T["wT"][d].b])
                yield

        def seq(st):
            L = LS[st % 3]
            T = CS[st % 2]
            is_x = order[0][st] >= NCC
            wT, qd, vn, u, kd, aqk = T["wT"], T["qd"], T["vn"], T["u"], T["kd"], T["aqk"]
            s.mm([(pws3[HF[d], h, :], wT[d].t[:, h, :], Sb[d].t[:, h, :], True, True) for h in range(8) for d in range(2)],
                 reads=[wT[0].b, wT[1].b, Sb[0].b, Sb[1].b], writes=[pWS.b])
            yield
            s.op("dve", lambda e: e.tensor_tensor(vn.t[:], u.t[:], pws3, ALU.subtract), reads=[u.b, pWS.b], writes=[vn.b])
            yield
            for d in range(2):
                ch = order[d][st]
                if is_x:
                    mms = []
                    for h in range(8):
                        mms.append((po3[:, h, :], Sb[d].t[:, h, :], qd[d].t[:, h, :], True, False))
                        mms.append((po3[:, h, :], vn.t[HF[d], h, :], aqk.t[HF[d], h, :], False, True))
                    s.mm(mms, reads=[Sb[d].b, qd[d].b, vn.b, aqk.b], writes=[pO.b])
                s.mm([(ps3[:, h, :], kd.t[HF[d], h, :], vn.t[HF[d], h, :], True, True) for h in range(8)],
                     reads=[kd.b, vn.b], writes=[pS2.b])
                s.op("pool", lambda e, d=d: e.tensor_tensor(S32[d].t[:], S32[d].t[:], bc_last(L["BL"][d].t[:, :], 128),
                                                            ALU.mult), reads=[S32[d].b, L["BL"][d].b], writes=[S32[d].b])
                yield
                if is_x:
                    s.op("act", lambda e, d=d: e.activation(T["o"][d].t[:], po3, AF.Copy), reads=[pO.b], writes=[T["o"][d].b])
                    s.dma("sp", k.OT[d, ch - NCC], T["o"][d].t[:], reads=[T["o"][d].b])
                s.op("dve", lambda e, d=d: e.tensor_tensor(S32[d].t[:], S32[d].t[:], ps3, ALU.add),
                     reads=[S32[d].b, pS2.b], writes=[S32[d].b])
                yield
                s.op("act", lambda e, d=d: e.activation(Sb[d].t[:], S32[d].t[:], AF.Copy), reads=[S32[d].b], writes=[Sb[d].b])
                yield

        loads(0)
        for st in range(NCH + 1):
            if st + 1 < NCH:
                loads(st + 1)
            gens = []
            if st < NCH:
                gens.append(prep(st))
            if st >= 1:
                gens.append(seq(st - 1))
            while gens:
                for g in list(gens):
                    try:
                        next(g)
                    except StopIteration:
                        gens.remove(g)
        s.barrier()


def build(SEQ=8192, debug=(), upto=99):
    nc = bass.Bass("TRN2", target_bir_lowering=False)
    es = ExitStack()
    k = K(nc, es, SEQ, debug)
    s = k.s
    NTOK = k.NTOK
    k.xt = k.din("xt", [D, NTOK])
    k.cvec = k.din("cvec", [128, 8, 2])
    k.w_ada = k.din("w_ada", [D, NMOD * D])
    k.b_ada = k.din("b_ada", [128, 72])
    k.nw = k.din("nw", [128, 4, 8])
    k.ffn1_w_in = k.din("ffn1_w_in", [D, 2 * DFF])
    k.ffn1_w_out = k.din("ffn1_w_out", [DFF, D])
    k.ffn2_w_in = k.din("ffn2_w_in", [D, 2 * DFF])
    k.ffn2_w_out = k.din("ffn2_w_out", [DFF, D])
    k.cst = k.din("cst", [128, 2, 128])
    k.w_mix_in = k.din("w_mix_in", [D, MIX_IN])
    k.convw = k.din("convw", [128, 24, 5])
    k.cst2 = k.din("cst2", [128, 3, 128])
    k.adt = k.din("adt", [1, 32])
    k.masks2 = k.din("masks2", [128, 3, 8, 64])
    k.identrep2 = k.din("identrep2", [128, 8, 64])
    mats, k.opm_index = pool_operators(SEQ)
    k.opm_n = mats.shape[0]
    k.opm = k.din("opm", [k.opm_n, 128, 128])
    k.poolw = k.din("poolw", [128, 4, 128])
    k.gnw = k.din("gnw", [128, 1])
    k.pscale = k.din("pscale", [128, 4])
    k.w_gdn_proj = k.din("w_gdn_proj", [D, D])
    k.w_pool_proj = k.din("w_pool_proj", [512, D])
    k.w_mix_out = k.din("w_mix_out", [D, D])
    k.X1T = k.dscr("X1T", [D, NTOK])
    k.PQKV = k.dscr("PQKV", [QKV, NTOK])
    k.ABT = k.dscr("ABT", [NTOK, 32])
    k.SGATE = k.dscr("SGATE", [D, SEQ])
    k.PPT = k.dscr("PPT", [SEQ, 512], BF16)
    k.X2T = k.dscr("X2T", [D, SEQ])
    k.GATES = k.dscr("GATES", [2 * D, SEQ])
    NCH = NTOK // CH
    k.QT = k.dscr("QT", [NCH, 128, 8, CH], BF16)
    k.KT = k.dscr("KT", [NCH, 128, 8, CH], BF16)
    k.KTOK = k.dscr("KTOK", [NTOK, 8, 128], BF16)
    k.VTOK = k.dscr("VTOK", [NTOK, 8, 128], BF16)
    k.AUXT = k.dscr("AUXT", [NTOK, 5, 16])
    k.BLT = k.dscr("BLT", [NTOK, 16])
    k.AUXR = k.dscr("AUXR", [3, 8, 2, NTOK])
    k.OT = k.dscr("OT", [2, SEQ // CH, 128, 8, CH])
    k.outT = nc.dram_tensor("outT", [D, SEQ], F32, kind="ExternalOutput").ap()
    C = {}
    cst = k.sb(es, [128, 2, 128], F32, "cst")
    C["bconst"] = Buf()
    s.dma("sp", cst[:], k.cst, writes=[C["bconst"]])
    C["ones"] = cst[:, 0, :]
    C["ident"] = cst[:, 1, :]
    C["eps"] = k.sb(es, [128, 2], F32, "eps")
    s.op("dve", lambda e: e.memset(C["eps"][:], EPS), writes=[C["bconst"]])
    C["tab"] = k.sb(es, [128, 2, 9, 8], F32, "tab")
    C["fnw"] = k.sb(es, [128, 8], F32, "fnw")
    C["btab"] = Buf()

    x_tiles = [(0, 0, 1)] + [(CTX + i * NT, CTX + i * NT, 0) for i in range(SEQ // NT)]
    pass0_mod(k, C)
    if upto >= 1:
        ffn_pass(k, C, k.xt, k.X1T, k.ffn1_w_in, k.ffn1_w_out, 0, x_tiles)
    if upto >= 2:
        pass2_mixin(k, C, x_tiles)
    if upto >= 3:
        pass3a_gdnprep(k, C, x_tiles)
    if upto >= 4:
        pass3b_scan2(k, C)
    if upto >= 5:
        pass4_merge(k, C)
    if upto >= 6:
        tiles5 = [(i * NT, i * NT, 0) for i in range(SEQ // NT)]
        ffn_pass(k, C, k.X2T, None, k.ffn2_w_in, k.ffn2_w_out, 2, tiles5, final_out=k.outT)
    s.barrier()
    es.close()
    return nc, k


def host_inputs(inp, b, SEQ):
    f = np.float32
    x = np.asarray(inp["x"], f)[b, :SEQ]
    ctx = np.asarray(inp["ctx"], f)[b]
    m = {}
    m["xt"] = np.ascontiguousarray(np.concatenate([ctx.T, x.T], axis=1))
    cv = np.stack([np.asarray(inp["c"], f)[b], np.asarray(inp["c_ctx"], f)], axis=-1)
    m["cvec"] = np.ascontiguousarray(cv.reshape(8, 128, 2).transpose(1, 0, 2))
    m["w_ada"] = np.ascontiguousarray(np.asarray(inp["w_ada"], f)[0])
    m["b_ada"] = np.ascontiguousarray(np.asarray(inp["b_ada"], f)[0].reshape(72, 128).T)
    nws = np.stack([np.asarray(inp["norm1_w"], f)[0], np.asarray(inp["norm2_w"], f)[0],
                    np.asarray(inp["norm3_w"], f)[0], np.asarray(inp["final_norm_w"], f)], axis=0)
    m["nw"] = np.ascontiguousarray(nws.reshape(4, 8, 128).transpose(2, 0, 1))
    m["opm"] = pool_operators(SEQ)[0]
    m["poolw"] = np.ascontiguousarray(np.asarray(inp["pool_w"], f)[0].transpose(1, 0, 2))
    m["gnw"] = np.asarray(inp["gdn_norm_w"], f)[0].reshape(128, 1).copy()
    m["pscale"] = np.ascontiguousarray(np.asarray(inp["pool_scale"], f)[0].reshape(4, 128).T)
    for nm in ["ffn1_w_in", "ffn1_w_out", "ffn2_w_in", "ffn2_w_out", "w_mix_in", "w_gdn_proj", "w_pool_proj",
               "w_mix_out"]:
        m[nm] = np.ascontiguousarray(np.asarray(inp[nm], f)[0])
    cw = np.asarray(inp["conv_w"], f)[0]
    m["convw"] = np.ascontiguousarray(cw.reshape(5, 24, 128).transpose(2, 1, 0))
    m["adt"] = np.concatenate([np.asarray(inp["a_log"], f)[0].reshape(-1),
                               np.asarray(inp["dt_bias"], f)[0].reshape(-1)])[None, :].copy()
    jj = np.arange(128)
    same = (jj[:, None] // 64) == (jj[None, :] // 64)
    c2 = np.zeros((128, 3, 128), f)
    c2[:, 0, :] = same & (jj[:, None] <= jj[None, :])
    c2[:, 1, :] = same & (jj[:, None] >= jj[None, :])
    c2[:, 2, :] = same
    m["cst2"] = c2
    pp = np.arange(64)[:, None]
    ff = np.arange(64)[None, :]
    mk = np.stack([ff >= pp, ff > pp, ff <= pp, ff < pp], 0)
    mk = np.where(mk, 0.0, NEG).astype(f)
    m2 = np.concatenate([mk[[0, 1, 3]].transpose(1, 0, 2), mk[[2, 3, 1]].transpose(1, 0, 2)], axis=0)
    m["masks2"] = np.ascontiguousarray(np.broadcast_to(m2[:, :, None, :], (128, 3, 8, 64)))
    e2 = np.concatenate([np.eye(64, dtype=f), np.eye(64, dtype=f)], axis=0)
    m["identrep2"] = np.ascontiguousarray(np.broadcast_to(e2[:, None, :], (128, 8, 64)))
    cst = np.zeros((128, 2, 128), f)
    cst[:, 0, :] = 1.0
    cst[:, 1, :] = np.eye(128, dtype=f)
    m["cst"] = cst
    return m


def kernel(**inputs):
    SEQ = int(np.asarray(inputs["x"]).shape[1])
    B = int(np.asarray(inputs["x"]).shape[0])
    nc, k = build(SEQ)
    shared = None
    in_maps = []
    for b in range(B):
        m = host_inputs(inputs, b, SEQ)
        if shared is None:
            shared = {n: v for n, v in m.items() if n not in ("xt", "cvec")}
        else:
            for n in shared:
                m[n] = shared[n]
        in_maps.append({n: v for n, v in m.items() if n in k.dram_in})
    res = run_bass_kernel_spmd(nc, in_maps, core_ids=list(range(B)))
    out = np.stack([np.asarray(res.results[b]["outT"]).T for b in range(B)], axis=0)
    return np.ascontiguousarray(out.astype(np.float32))
```

```python
import numpy as np
import ml_dtypes
from contextlib import ExitStack
import concourse.bass as bass
import concourse.mybir as mybir
from concourse.bass_utils import run_bass_kernel_spmd

F32 = mybir.dt.float32
BF16 = mybir.dt.bfloat16
AF = mybir.ActivationFunctionType
ALU = mybir.AluOpType

D = 1024
DFF = 2816
NFF = DFF // 128
NMOD = 9
CTX = 256
H = 8
HD = 128
CH = 64
QKV = 3072
AB_END = 3104
GATE_END = 4128
POOL_END = 4640
MIX_IN = 6688
NT = 256
EPS = 1e-6
GRID_W = 64
NEG = -30000.0


class Buf:
    __slots__ = ("w", "r", "name")

    def __init__(self, name=""):
        self.w = {}
        self.r = {}
        self.name = name


class Sched:
    NDS = 8

    def __init__(self, nc, es):
        self.nc = nc
        self.engs = {"pe": nc.tensor, "dve": nc.vector, "act": nc.scalar, "pool": nc.gpsimd, "sp": nc.sync}
        self.sems = {}
        self.cnt = {}
        for k in ["pe", "dve", "act", "pool"]:
            self.sems[k] = es.enter_context(nc.semaphore("s_" + k))
            self.cnt[k] = 0
        self.dq = {}
        for q in ["sp", "pool", "act"]:
            lst = []
            for i in range(self.NDS):
                key = "d_%s%d" % (q, i)
                self.sems[key] = es.enter_context(nc.semaphore(key))
                self.cnt[key] = 0
                lst.append(key)
            self.dq[q] = [lst, 0]
        self.seen = {e: {} for e in self.engs}
        self.nwaits = 0
        self.nins = 0

    def _wait(self, e, key, val, raw=True):
        if key == e:
            if e == "pe" or not raw or self.cnt[e] - val >= 2:
                return
        if self.seen[e].get(key, 0) >= val:
            return
        self.engs[e].wait_ge(self.sems[key], val)
        self.seen[e][key] = val
        self.nwaits += 1

    def _deps(self, e, reads, writes):
        for b in reads:
            for key, val in b.w.items():
                self._wait(e, key, val)
        for b in writes:
            for key, val in b.w.items():
                self._wait(e, key, val, raw=False)
            for key, val in b.r.items():
                self._wait(e, key, val, raw=False)

    def _done(self, key, val, reads, writes):
        for b in reads:
            b.r[key] = val
        for b in writes:
            b.w[key] = val
            b.r = {}

    def op(self, e, fn, reads=(), writes=()):
        self._deps(e, reads, writes)
        ins = fn(self.engs[e])
        self.cnt[e] += 1
        ins.then_inc(self.sems[e], 1)
        self._done(e, self.cnt[e], reads, writes)
        self.nins += 1

    def mm(self, mms, reads=(), writes=()):
        self._deps("pe", reads, writes)
        ins = None
        for (o, l, r, st, sp) in mms:
            ins = self.nc.tensor.matmul(o, l, r, start=st, stop=sp)
        self.cnt["pe"] += 1
        ins.then_inc(self.sems["pe"], 1)
        self._done("pe", self.cnt["pe"], reads, writes)
        self.nins += len(mms)

    def tr(self, trs, reads=(), writes=()):
        self._deps("pe", reads, writes)
        ins = None
        for (o, i, idn) in trs:
            ins = self.nc.tensor.transpose(o, i, idn)
        self.cnt["pe"] += 1
        ins.then_inc(self.sems["pe"], 1)
        self._done("pe", self.cnt["pe"], reads, writes)
        self.nins += len(trs)

    def dma(self, q, out, in_, reads=(), writes=(), **kw):
        self._deps(q, reads, writes)
        lst, i = self.dq[q]
        key = lst[i % self.NDS]
        self.dq[q][1] = i + 1
        if self.cnt[key] > 0:
            self._wait(q, key, self.cnt[key])
        ins = self.engs[q].dma_start(out=out, in_=in_, **kw)
        self.cnt[key] += 16
        ins.then_inc(self.sems[key], 16)
        self._done(key, self.cnt[key], reads, writes)
        self.nins += 1

    def barrier(self):
        for e in self.engs:
            for key in self.sems:
                if self.cnt[key] > 0:
                    self._wait(e, key, self.cnt[key])


class K:
    def __init__(self, nc, es, SEQ, debug=()):
        self.nc = nc
        self.es = es
        self.s = Sched(nc, es)
        self.SEQ = SEQ
        self.NTOK = CTX + SEQ
        self.debug = set(debug)
        self.uid = 0
        self.dram_in = {}

    def name(self, p):
        self.uid += 1
        return "%s_%d" % (p, self.uid)

    def sb(self, ctx, shape, dt, p="t"):
        return ctx.enter_context(self.nc.sbuf_tensor(self.name(p), list(shape), dt))

    def ps(self, ctx, shape, dt, p="ps"):
        return ctx.enter_context(self.nc.psum_tensor(self.name(p), list(shape), dt))

    def din(self, name, shape, dt=F32):
        t = self.nc.dram_tensor(name, list(shape), dt, kind="ExternalInput")
        self.dram_in[name] = t
        return t.ap()

    def dscr(self, name, shape, dt=F32):
        kind = "ExternalOutput" if name in self.debug else "Internal"
        return self.nc.dram_tensor(name, list(shape), dt, kind=kind).ap()


def bc_mid(ap2, n):
    P, Fd = ap2.shape
    return ap2.unsqueeze(1).broadcast_to([P, n, Fd])


def bc_last(ap2, n):
    P, Fd = ap2.shape
    return ap2.unsqueeze(2).broadcast_to([P, Fd, n])


def load_weight_bf16(k, dst3, w_dram, nk, pieces=1):
    cols = w_dram.shape[1]
    step = (cols + pieces - 1) // pieces
    wb = Buf()
    for kc in range(nk):
        for c0 in range(0, cols, step):
            c1 = min(cols, c0 + step)
            k.s.dma("pool", dst3[:, kc, c0:c1], w_dram[kc * 128:(kc + 1) * 128, c0:c1], writes=[wb])
    return wb


def pass0_mod(k, C):
    s = k.s
    nc = k.nc
    tab = C["tab"]
    btab = C["btab"]
    with ExitStack() as cx:
        sc = k.sb(cx, [128, 8, 2], F32)
        bT = k.sb(cx, [128, 72], F32)
        nw = k.sb(cx, [128, 4, 8], F32)
        mod = k.sb(cx, [128, 2, 72], F32)
        wa = [k.sb(cx, [128, 8, 1024], F32) for _ in range(2)]
        pm = k.ps(cx, [128, 72, 2], F32)
        bsc, bbT, bnw, bmod, bpm = Buf(), Buf(), Buf(), Buf(), Buf()
        bwa = [Buf(), Buf()]
        s.dma("sp", sc[:], k.cvec, writes=[bsc])
        s.dma("sp", bT[:], k.b_ada, writes=[bbT])
        s.dma("sp", nw[:], k.nw, writes=[bnw])
        s.op("act", lambda e: e.activation(sc[:], sc[:], AF.Silu), reads=[bsc], writes=[bsc])
        for j in range(NMOD):
            w = wa[j % 2]
            s.dma("sp", w[:], k.w_ada[:, j * 1024:(j + 1) * 1024].rearrange("(kc p) c -> p kc c", p=128),
                  writes=[bwa[j % 2]])
            mms = []
            for dc in range(8):
                for kc in range(8):
                    mms.append((pm[:, j * 8 + dc, :], w[:, kc, dc * 128:(dc + 1) * 128], sc[:, kc, :],
                                kc == 0, kc == 7))
            s.mm(mms, reads=[bwa[j % 2], bsc], writes=[bpm])
        for wch in range(2):
            s.op("dve", lambda e, wch=wch: e.tensor_tensor(mod[:, wch, :], pm[:, :, wch], bT[:], ALU.add),
                 reads=[bpm, bbT], writes=[bmod])
        for wch in range(2):
            for sub, (jsh, jsc, jg, half) in enumerate([(0, 1, 2, True), (3, 4, 5, False), (6, 7, 8, True)]):
                s.op("dve", lambda e, wch=wch, sub=sub, jsc=jsc: e.scalar_tensor_tensor(
                    tab[:, wch, sub * 3 + 0, :], mod[:, wch, jsc * 8:(jsc + 1) * 8], 1.0, nw[:, sub, :],
                    ALU.add, ALU.mult), reads=[bmod, bnw], writes=[btab])
                s.op("dve", lambda e, wch=wch, sub=sub, jsh=jsh: e.tensor_copy(
                    tab[:, wch, sub * 3 + 1, :], mod[:, wch, jsh * 8:(jsh + 1) * 8]), reads=[bmod], writes=[btab])
                s.op("dve", lambda e, wch=wch, sub=sub, jg=jg, half=half: e.tensor_scalar(
                    tab[:, wch, sub * 3 + 2, :], mod[:, wch, jg * 8:(jg + 1) * 8], 0.5 if half else 1.0, None,
                    ALU.mult), reads=[bmod], writes=[btab])
        s.op("dve", lambda e: e.tensor_copy(C["fnw"][:], nw[:, 3, :]), reads=[bnw], writes=[btab])
        s.barrier()


def ffn_pass(k, C, src, dst, w_in, w_out, sub, tiles, final_out=None):
    s = k.s
    tab = C["tab"]
    btab = C["btab"]
    with ExitStack() as cx:
        Win = k.sb(cx, [128, 8, 2 * DFF], BF16, "win")
        Wout = k.sb(cx, [128, NFF, D], BF16, "wout")
        bWin = load_weight_bf16(k, Win, w_in, 8, pieces=2)
        bWout = load_weight_bf16(k, Wout, w_out, NFF)
        xTs = [k.sb(cx, [128, 8, NT], F32, "xT") for _ in range(2)]
        bxT = [Buf(), Buf()]
        sq = k.sb(cx, [128, 8, NT], F32, "sq")
        bsq = Buf()
        hTs = [k.sb(cx, [128, 8, NT], BF16, "hT") for _ in range(2)]
        bhTs = [Buf(), Buf()]
        aT = k.sb(cx, [128, NFF, NT], BF16, "aT")
        baT = Buf()
        rstd = k.sb(cx, [128, NT], F32, "rstd")
        brstd = Buf()
        pss = k.ps(cx, [128, 512], F32)
        bpss = Buf()
        NPB = 4
        pgu = [k.ps(cx, [128, 512], F32) for _ in range(NPB)]
        pys = [k.ps(cx, [128, 512], F32) for _ in range(2)]
        bpgu = [Buf() for _ in range(NPB)]
        bpy = [Buf(), Buf()]
        sg = [k.sb(cx, [128, NT], F32, "sg") for _ in range(NPB)]
        bsg = [Buf() for _ in range(NPB)]
        ones = C["ones"]
        epsb = C["eps"]

        def rms(xT, bx):
            s.op("act", lambda e: e.activation(sq[:], xT[:], AF.Square), reads=[bx], writes=[bsq])
            s.mm([(pss[:, 0:NT], ones[:], sq[:, kc, :], kc == 0, kc == 7) for kc in range(8)],
                 reads=[bsq, C["bconst"]], writes=[bpss])
            s.op("act", lambda e: e.activation(rstd[:], pss[:, 0:NT], AF.Sqrt, bias=epsb[:, 0:1], scale=1.0 / D),
                 reads=[bpss, C["bconst"]], writes=[brstd])
            s.op("dve", lambda e: e.reciprocal(rstd[:], rstd[:]), reads=[brstd], writes=[brstd])

        def rms_g(xT, bx):
            s.op("act", lambda e: e.activation(sq[:], xT[:], AF.Square), reads=[bx], writes=[bsq])
            yield
            s.mm([(pss[:, 0:NT], ones[:], sq[:, kc, :], kc == 0, kc == 7) for kc in range(8)],
                 reads=[bsq, C["bconst"]], writes=[bpss])
            yield
            s.op("act", lambda e: e.activation(rstd[:], pss[:, 0:NT], AF.Sqrt, bias=epsb[:, 0:1], scale=1.0 / D),
                 reads=[bpss, C["bconst"]], writes=[brstd])
            yield
            s.op("dve", lambda e: e.reciprocal(rstd[:], rstd[:]), reads=[brstd], writes=[brstd])
            yield

        def prologue(ti):
            t0s, t0d, wch = tiles[ti]
            xT = xTs[ti % 2]
            bx = bxT[ti % 2]
            hT, bhT = hTs[ti % 2], bhTs[ti % 2]
            A = tab[:, wch, sub * 3 + 0, :]
            B = tab[:, wch, sub * 3 + 1, :]
            s.dma("sp", xT[:], src[:, t0s:t0s + NT].rearrange("(kc p) t -> p kc t", p=128), writes=[bx])
            yield
            yield from rms_g(xT, bx)
            s.op("dve", lambda e: e.tensor_tensor(sq[:], xT[:], bc_mid(rstd[:], 8), ALU.mult),
                 reads=[bx, brstd], writes=[bsq])
            yield
            s.op("pool", lambda e: e.tensor_tensor(sq[:], sq[:], bc_last(A, NT), ALU.mult),
                 reads=[bsq, btab], writes=[bsq])
            yield
            s.op("pool", lambda e: e.tensor_tensor(hT[:], sq[:], bc_last(B, NT), ALU.add),
                 reads=[bsq, btab], writes=[bhT])

        def main(ti):
            t0s, t0d, wch = tiles[ti]
            xT = xTs[ti % 2]
            bx = bxT[ti % 2]
            hT, bhT = hTs[ti % 2], bhTs[ti % 2]
            HG = tab[:, wch, sub * 3 + 2, :]
            for m in range(NFF):
                pp = pgu[m % NPB]
                bpp = bpgu[m % NPB]
                s.mm([(pp[:, 0:NT], Win[:, kc, m * 128:(m + 1) * 128], hT[:, kc, :], kc == 0, kc == 7)
                      for kc in range(8)] +
                     [(pp[:, NT:2 * NT], Win[:, kc, DFF + m * 128:DFF + (m + 1) * 128], hT[:, kc, :], kc == 0, kc == 7)
                      for kc in range(8)], reads=[bWin, bhT], writes=[bpp])
                s.op("act", lambda e, m=m, pp=pp: e.activation(sg[m % NPB][:], pp[:, 0:NT], AF.Silu),
                     reads=[bpp], writes=[bsg[m % NPB]])
                s.op("dve", lambda e, m=m, pp=pp: e.tensor_tensor(aT[:, m, :], sg[m % NPB][:], pp[:, NT:2 * NT], ALU.mult),
                     reads=[bsg[m % NPB], bpp], writes=[baT])
                yield
            for dc in range(8):
                py = pys[dc % 2]
                s.mm([(py[:, 0:NT], Wout[:, m, dc * 128:(dc + 1) * 128], aT[:, m, :], m == 0, m == NFF - 1)
                      for m in range(NFF)], reads=[bWout, baT], writes=[bpy[dc % 2]])
                s.op("dve", lambda e, dc=dc, py=py: e.scalar_tensor_tensor(
                    xT[:, dc, :], py[:, 0:NT], HG[:, dc:dc + 1], xT[:, dc, :], ALU.mult, ALU.add),
                    reads=[bpy[dc % 2], bx, btab], writes=[bx])
                yield
            if dst is not None:
                s.dma("sp", dst[:, t0d:t0d + NT].rearrange("(kc p) t -> p kc t", p=128), xT[:], reads=[bx])
            if final_out is not None:
                rms(xT, bx)
                s.op("dve", lambda e: e.tensor_tensor(sq[:], xT[:], bc_mid(rstd[:], 8), ALU.mult),
                     reads=[bx, brstd], writes=[bsq])
                s.op("pool", lambda e: e.tensor_tensor(sq[:], sq[:], bc_last(C["fnw"][:], NT), ALU.mult),
                     reads=[bsq, btab], writes=[bsq])
                s.dma("sp", final_out[:, t0d:t0d + NT].rearrange("(kc p) t -> p kc t", p=128), sq[:], reads=[bsq])

        def run_all(gs):
            gs = list(gs)
            while gs:
                for g in list(gs):
                    try:
                        next(g)
                    except StopIteration:
                        gs.remove(g)

        run_all([prologue(0)])
        for ti in range(len(tiles)):
            gs = [main(ti)]
            if ti + 1 < len(tiles):
                gs.append(prologue(ti + 1))
            run_all(gs)
        s.barrier()


def rms_mod(k, C, cx_bufs, xT, bx, A, B, outT, bout):
    s = k.s
    sq, bsq, rstd, brstd, pss, bpss = cx_bufs
    s.op("act", lambda e: e.activation(sq[:], xT[:], AF.Square), reads=[bx], writes=[bsq])
    s.mm([(pss[:, 0:NT], C["ones"][:], sq[:, kc, :], kc == 0, kc == 7) for kc in range(8)],
         reads=[bsq, C["bconst"]], writes=[bpss])
    s.op("act", lambda e: e.activation(rstd[:], pss[:, 0:NT], AF.Sqrt, bias=C["eps"][:, 0:1], scale=1.0 / D),
         reads=[bpss, C["bconst"]], writes=[brstd])
    s.op("dve", lambda e: e.reciprocal(rstd[:], rstd[:]), reads=[brstd], writes=[brstd])
    s.op("dve", lambda e: e.tensor_tensor(sq[:], xT[:], bc_mid(rstd[:], 8), ALU.mult),
         reads=[bx, brstd], writes=[bsq])
    s.op("pool", lambda e: e.tensor_tensor(sq[:], sq[:], bc_last(A, NT), ALU.mult),
         reads=[bsq, C["btab"]], writes=[bsq])
    s.op("pool", lambda e: e.tensor_tensor(outT[:], sq[:], bc_last(B, NT), ALU.add),
         reads=[bsq, C["btab"]], writes=[bout])


def rms_mod_g(k, C, cx_bufs, xT, bx, A, B, outT, bout):
    s = k.s
    sq, bsq, rstd, brstd, pss, bpss = cx_bufs
    s.op("act", lambda e: e.activation(sq[:], xT[:], AF.Square), reads=[bx], writes=[bsq])
    yield
    s.mm([(pss[:, 0:NT], C["ones"][:], sq[:, kc, :], kc == 0, kc == 7) for kc in range(8)],
         reads=[bsq, C["bconst"]], writes=[bpss])
    yield
    s.op("act", lambda e: e.activation(rstd[:], pss[:, 0:NT], AF.Sqrt, bias=C["eps"][:, 0:1], scale=1.0 / D),
         reads=[bpss, C["bconst"]], writes=[brstd])
    yield
    s.op("dve", lambda e: e.reciprocal(rstd[:], rstd[:]), reads=[brstd], writes=[brstd])
    yield
    s.op("dve", lambda e: e.tensor_tensor(sq[:], xT[:], bc_mid(rstd[:], 8), ALU.mult),
         reads=[bx, brstd], writes=[bsq])
    yield
    s.op("pool", lambda e: e.tensor_tensor(sq[:], sq[:], bc_last(A, NT), ALU.mult),
         reads=[bsq, C["btab"]], writes=[bsq])
    yield
    s.op("pool", lambda e: e.tensor_tensor(outT[:], sq[:], bc_last(B, NT), ALU.add),
         reads=[bsq, C["btab"]], writes=[bout])


def pass2_mixin(k, C, tiles):
    s = k.s
    tab = C["tab"]
    with ExitStack() as cx:
        Wm = k.sb(cx, [128, 8, MIX_IN], BF16, "wmix")
        bWm = load_weight_bf16(k, Wm, k.w_mix_in, 8, pieces=2)
        xTs = [k.sb(cx, [128, 8, NT], F32, "xT") for _ in range(2)]
        bxT = [Buf(), Buf()]
        sq = k.sb(cx, [128, 8, NT], F32, "sq")
        rstd = k.sb(cx, [128, NT], F32, "rstd")
        uTs = [k.sb(cx, [128, 8, NT], BF16, "uT") for _ in range(2)]
        buTs = [Buf(), Buf()]
        pss = k.ps(cx, [128, 512], F32)
        cxb = (sq, Buf(), rstd, Buf(), pss, Buf())
        NPQ = 4
        pqb = [k.ps(cx, [128, 512], F32) for _ in range(NPQ)]
        pq = [pqb[i][:, 0:NT] for i in range(NPQ)]
        bpq = [Buf() for _ in range(NPQ)]
        ppl = [k.ps(cx, [128, 512], F32) for _ in range(2)]
        bppl = [Buf(), Buf()]
        pab = k.ps(cx, [128, 512], F32)
        bpab = Buf()
        stg = [k.sb(cx, [128, 4, NT], F32, "stg") for _ in range(3)]
        bstg = [Buf(), Buf(), Buf()]
        abs_ = k.sb(cx, [128, 2, 32], F32, "abs")
        babs = Buf()
        pps = k.sb(cx, [128, 2, 512], BF16, "pps")
        bpps = Buf()
        cnt = {"nst": 0, "nq": 0}

        def prologue(ti):
            t0, _, wch = tiles[ti]
            xT = xTs[ti % 2]
            bx = bxT[ti % 2]
            s.dma("sp", xT[:], k.X1T[:, t0:t0 + NT].rearrange("(kc p) t -> p kc t", p=128), writes=[bx])
            yield
            yield from rms_mod_g(k, C, cxb, xT, bx, tab[:, wch, 3, :], tab[:, wch, 4, :], uTs[ti % 2], buTs[ti % 2])

        def main(ti):
            t0, _, wch = tiles[ti]
            uT, buT = uTs[ti % 2], buTs[ti % 2]
            nst, nq = cnt["nst"], cnt["nq"]
            if True:
              s.mm([(pab[:, blk * 32:(blk + 1) * 32], uT[:, kc, blk * 128:(blk + 1) * 128], Wm[:, kc, QKV:AB_END],
                   kc == 0, kc == 7) for blk in range(2) for kc in range(8)], reads=[buT, bWm], writes=[bpab])
              s.op("dve", lambda e: e.tensor_copy(abs_[:], pab[:, 0:64].rearrange("p (b c) -> p b c", b=2)),
                 reads=[bpab], writes=[babs])
              s.dma("sp", k.ABT[t0:t0 + NT, :].rearrange("(b p) c -> p b c", p=128), abs_[:], reads=[babs])
            groups = [(0, 24, AF.Copy, k.PQKV, 0, t0)]
            if wch == 0:
                tx = t0 - CTX
                gsel = "123"
                allg = [(AB_END, 8, AF.Silu, k.SGATE, 0, tx), None,
                           (POOL_END, 16, AF.Sigmoid, k.GATES, 0, tx)]
                groups += [allg[int(ch) - 1] for ch in gsel if ch != "2"]
                for blk in range(2):
                    pp = ppl[blk]
                    bp = bppl[blk]
                    s.mm([(pp[:, :], uT[:, kc, blk * 128:(blk + 1) * 128], Wm[:, kc, GATE_END:POOL_END],
                           kc == 0, kc == 7) for kc in range(8)], reads=[buT, bWm], writes=[bp])
                    s.op("act", lambda e, pp=pp, blk=blk: e.activation(pps[:, blk, :], pp[:, :], AF.Copy),
                         reads=[bp], writes=[bpps])
                s.dma("sp", k.PPT[tx:tx + NT, :].rearrange("(b p) c -> p b c", p=128), pps[:], reads=[bpps])
            for (col0, nch, func, dram, row0, tq) in groups:
                for c0 in range(0, nch, 4):
                    st = stg[nst % 3]
                    bst = bstg[nst % 3]
                    nst += 1
                    for c in range(c0, c0 + 4):
                        pp = pq[nq % NPQ]
                        bp = bpq[nq % NPQ]
                        nq += 1
                        cc = col0 + c * 128
                        s.mm([(pp, Wm[:, kc, cc:cc + 128], uT[:, kc, :], kc == 0, kc == 7)
                              for kc in range(8)], reads=[buT, bWm], writes=[bp])
                        s.op("act", lambda e, st=st, c=c, c0=c0, pp=pp, func=func: e.activation(
                            st[:, c - c0, :], pp, func), reads=[bp], writes=[bst])
                    r0 = row0 + c0 * 128
                    s.dma("sp", dram[r0:r0 + 512, tq:tq + NT].rearrange("(c p) t -> p c t", p=128), st[:],
                          reads=[bst])
                    yield
            cnt["nst"], cnt["nq"] = nst, nq

        def run_all(gs):
            gs = list(gs)
            while gs:
                for g in list(gs):
                    try:
                        next(g)
                    except StopIteration:
                        gs.remove(g)

        run_all([prologue(0)])
        for ti in range(len(tiles)):
            gs = [main(ti)]
            if ti + 1 < len(tiles):
                gs.append(prologue(ti + 1))
            run_all(gs)
        s.barrier()


def pass3a_gdnprep(k, C, tiles):
    s = k.s
    NTOK = k.NTOK
    with ExitStack() as cx:
        cw = k.sb(cx, [128, 24, 5], F32, "cw")
        bcw = Buf()
        s.dma("sp", cw[:], k.convw, writes=[bcw])
        c2 = k.sb(cx, [128, 3, 128], F32, "c2")
        s.dma("sp", c2[:], k.cst2, writes=[bcw])
        identb = k.sb(cx, [128, 128], BF16, "identb")
        s.op("dve", lambda e: e.tensor_copy(identb[:], C["ident"]), reads=[C["bconst"]], writes=[bcw])
        adt = k.sb(cx, [128, 32], F32, "adt")
        s.dma("sp", adt[:], k.adt.partition_broadcast(128), writes=[bcw])
        nega = k.sb(cx, [128, 16], F32, "nega")
        s.op("act", lambda e: e.activation(nega[:], adt[:, 0:16], AF.Exp), reads=[bcw], writes=[bcw])
        s.op("dve", lambda e: e.tensor_scalar(nega[:], nega[:], -1.0, None, ALU.mult), reads=[bcw], writes=[bcw])
        cbias = k.sb(cx, [128, 4], F32, "cbias")
        s.op("dve", lambda e: e.memset(cbias[:, 0:1], EPS), writes=[bcw])
        s.op("dve", lambda e: e.memset(cbias[:, 1:2], 128.0 * EPS), writes=[bcw])
        s.op("dve", lambda e: e.memset(cbias[:, 2:3], 1.0), writes=[bcw])

        W = NT + 4
        NBF = 3
        pin4 = [k.sb(cx, [128, 4, W], F32, "pin4") for _ in range(NBF)]
        bpin = [Buf() for _ in range(NBF)]
        acc4 = [k.sb(cx, [128, 4, NT], F32, "acc4") for _ in range(NBF)]
        bacc = [Buf() for _ in range(NBF)]
        sact4 = [k.sb(cx, [128, 4, NT], F32, "sact4") for _ in range(NBF)]
        bsact = [Buf() for _ in range(NBF)]
        sq4s = [k.sb(cx, [128, 4, NT], F32, "sq4") for _ in range(2)]
        bsq4s = [Buf(), Buf()]
        rn4s = [k.sb(cx, [128, 4, NT], F32, "rn4") for _ in range(2)]
        brn4s = [Buf(), Buf()]
        knf4 = [k.sb(cx, [128, 4, NT], BF16, "knf4") for _ in range(NBF)]
        bknf = [Buf() for _ in range(NBF)]
        QTs = [k.sb(cx, [128, 4, 8, 64], BF16, "QTs") for _ in range(2)]
        KTs = [k.sb(cx, [128, 4, 8, 64], BF16, "KTs") for _ in range(2)]
        KTOKs = [k.sb(cx, [128, 2, 8, 128], BF16, "KTOKs") for _ in range(2)]
        VTOKs = [k.sb(cx, [128, 2, 8, 128], BF16, "VTOKs") for _ in range(2)]
        bQTs, bKTs, bKTOKs, bVTOKs = [Buf(), Buf()], [Buf(), Buf()], [Buf(), Buf()], [Buf(), Buf()]
        pns = [k.ps(cx, [128, 4, NT], F32) for _ in range(2)]
        bpns = [Buf(), Buf()]
        ptr = [k.ps(cx, [128, 2, 4, 128], BF16) for _ in range(2)]
        bptr = [Buf(), Buf()]
        pa = k.ps(cx, [128, 512], F32)
        bpa = Buf()
        pr = k.ps(cx, [128, 512], F32)
        bpr = Buf()
        abt = [k.sb(cx, [128, 32], F32, "abt") for _ in range(2)]
        babt = [Buf(), Buf()]
        aux = [k.sb(cx, [128, 5, 16], F32, "aux") for _ in range(2)]
        baux = [Buf(), Buf()]
        lb = k.sb(cx, [128, 16], F32, "lb")
        gt = k.sb(cx, [128, 16], F32, "gt")
        cl = k.sb(cx, [128, 16], F32, "cl")
        bl = [k.sb(cx, [128, 16], F32, "bl") for _ in range(2)]
        bbl = [Buf(), Buf()]
        rowt = [k.sb(cx, [8, 3, 2, 128], F32, "rowt") for _ in range(2)]
        browt = [Buf(), Buf()]
        bsm = Buf()
        state = {"ng": 0, "nblk": 0, "ntr": 0}
        done_groups = {}

        def grp(ti, t0, wch, g4):
                seq_lo = 0 if wch == 1 else CTX
                seq_hi = CTX if wch == 1 else NTOK
                qts, kts, ktoks, vtoks = QTs[ti % 2], KTs[ti % 2], KTOKs[ti % 2], VTOKs[ti % 2]
                kind = g4 // 2
                h0 = (g4 % 2) * 4
                ng = state["ng"]
                state["ng"] += 1
                pin = pin4[ng % NBF]
                bp = bpin[ng % NBF]
                acc = acc4[ng % NBF]
                ba = bacc[ng % NBF]
                sact = sact4[ng % NBF]
                bs = bsact[ng % NBF]
                sq4, bsq4, rn4, brn4, pn, bpn = sq4s[ng % 2], bsq4s[ng % 2], rn4s[ng % 2], brn4s[ng % 2], pns[ng % 2], bpns[ng % 2]
                ng += 1
                lo = max(t0 - 2, seq_lo)
                hi = min(t0 + NT + 2, seq_hi)
                if lo > t0 - 2:
                    s.op("pool", lambda e, pin=pin: e.memset(pin[:, :, 0:2], 0.0), writes=[bp])
                if hi < t0 + NT + 2:
                    s.op("pool", lambda e, pin=pin: e.memset(pin[:, :, W - 2:W], 0.0), writes=[bp])
                s.dma("sp", pin[:, :, lo - (t0 - 2):hi - (t0 - 2)],
                      k.PQKV[g4 * 512:(g4 + 1) * 512, lo:hi].rearrange("(c p) t -> p c t", p=128), writes=[bp])
                bai = [Buf() for _ in range(4)]
                for i in range(4):
                    cc = g4 * 4 + i
                    s.op("act", lambda e, i=i, cc=cc, pin=pin, acc=acc: e.activation(
                        acc[:, i, :], pin[:, i, 0:NT], AF.Copy, scale=cw[:, cc, 0:1]), reads=[bp, bcw], writes=[ba, bai[i]])
                yield
                for kk in range(1, 5):
                    for i in range(4):
                        cc = g4 * 4 + i
                        s.op("dve", lambda e, i=i, cc=cc, kk=kk, pin=pin, acc=acc: e.scalar_tensor_tensor(
                            acc[:, i, :], pin[:, i, kk:kk + NT], cw[:, cc, kk:kk + 1], acc[:, i, :],
                            ALU.mult, ALU.add), reads=[bp, bcw, bai[i]], writes=[bai[i]] + ([ba] if kk == 4 else []))
                    yield
                if kind == 2:
                    vb = knf4[ng % NBF]
                    bvb = bknf[ng % NBF]
                    s.op("act", lambda e, acc=acc, vb=vb: e.activation(vb[:], acc[:], AF.Silu), reads=[ba], writes=[bvb])
                    yield
                    src_t, bsrc, dstk, bdst = vb, bvb, vtoks, bVTOKs[ti % 2]
                else:
                    s.op("act", lambda e, acc=acc, sact=sact: e.activation(sact[:], acc[:], AF.Silu),
                         reads=[ba], writes=[bs])
                    yield
                    s.op("act", lambda e, sact=sact: e.activation(sq4[:], sact[:], AF.Square), reads=[bs], writes=[bsq4])
                    yield
                    s.mm([(pn[:, i, :], C["ones"], sq4[:, i, :], True, True) for i in range(4)],
                         reads=[bsq4, C["bconst"]], writes=[bpn])
                    yield
                    if kind == 0:
                        s.op("act", lambda e: e.activation(rn4[:], pn[:], AF.Ln, bias=cbias[:, 1:2], scale=128.0),
                             reads=[bpn, bcw], writes=[brn4])
                    else:
                        s.op("act", lambda e: e.activation(rn4[:], pn[:], AF.Ln, bias=cbias[:, 0:1], scale=1.0),
                             reads=[bpn, bcw], writes=[brn4])
                    yield
                    s.op("act", lambda e: e.activation(rn4[:], rn4[:], AF.Exp, scale=-0.5), reads=[brn4], writes=[brn4])
                    yield
                    if kind == 0:
                        for i in range(4):
                            s.op("dve", lambda e, i=i, sact=sact: e.tensor_tensor(
                                qts[:, :, h0 + i, :], sact[:, i, :].rearrange("p (c t) -> p c t", c=4),
                                rn4[:, i, :].rearrange("p (c t) -> p c t", c=4), ALU.mult),
                                reads=[bs, brn4], writes=[bQTs[ti % 2]])
                        done_groups[ti] = done_groups.get(ti, 0) + 1
                        return
                    kn = knf4[ng % NBF]
                    bkn = bknf[ng % NBF]
                    s.op("dve", lambda e, sact=sact, kn=kn: e.tensor_tensor(kn[:], sact[:], rn4[:], ALU.mult),
                         reads=[bs, brn4], writes=[bkn])
                    for i in range(4):
                        s.op("pool", lambda e, i=i, kn=kn: e.tensor_copy(
                            kts[:, :, h0 + i, :], kn[:, i, :].rearrange("p (c t) -> p c t", c=4)),
                            reads=[bkn], writes=[bKTs[ti % 2]])
                    yield
                    src_t, bsrc, dstk, bdst = kn, bkn, ktoks, bKTOKs[ti % 2]
                pt = ptr[state["ntr"] % 2]
                bpt = bptr[state["ntr"] % 2]
                state["ntr"] += 1
                s.tr([(pt[:, blk, i, :], src_t[:, i, blk * 128:(blk + 1) * 128], identb[:])
                      for blk in range(2) for i in range(4)], reads=[bsrc, bcw], writes=[bpt])
                yield
                s.op("act", lambda e, pt=pt, dstk=dstk: e.activation(dstk[:, :, h0:h0 + 4, :], pt[:], AF.Copy),
                     reads=[bpt], writes=[bdst])
                done_groups[ti] = done_groups.get(ti, 0) + 1

        def tail(ti, t0, wch):
            qts, kts, ktoks, vtoks = QTs[ti % 2], KTs[ti % 2], KTOKs[ti % 2], VTOKs[ti % 2]
            while done_groups.get(ti, 0) < 6:
                yield
            c0 = t0 // CH
            s.dma("sp", k.QT[c0:c0 + 4].rearrange("c d h t -> d c (h t)"), qts[:].rearrange("d c h t -> d c (h t)"),
                  reads=[bQTs[ti % 2]])
            s.dma("sp", k.KT[c0:c0 + 4].rearrange("c d h t -> d c (h t)"), kts[:].rearrange("d c h t -> d c (h t)"),
                  reads=[bKTs[ti % 2]])
            s.dma("sp", k.KTOK[t0:t0 + NT].rearrange("(b p) h d -> p b (h d)", p=128),
                  ktoks[:].rearrange("p b h d -> p b (h d)"), reads=[bKTOKs[ti % 2]])
            s.dma("sp", k.VTOK[t0:t0 + NT].rearrange("(b p) h d -> p b (h d)", p=128),
                  vtoks[:].rearrange("p b h d -> p b (h d)"), reads=[bVTOKs[ti % 2]])
            for blk in range(2):
                tb = t0 + blk * 128
                nblk = state["nblk"]
                state["nblk"] += 1
                ab = abt[nblk % 2]
                bab = babt[nblk % 2]
                ax = aux[nblk % 2]
                bax = baux[nblk % 2]
                blt = bl[nblk % 2]
                bblt = bbl[nblk % 2]
                rw = rowt[nblk % 2]
                brw = browt[nblk % 2]
                yield
                s.dma("sp", ab[:], k.ABT[tb:tb + 128, :], writes=[bab])
                s.op("act", lambda e, ab=ab, ax=ax: e.activation(ax[:, 2, :], ab[:, 0:16], AF.Sigmoid),
                     reads=[bab], writes=[bax])
                s.op("act", lambda e, ax=ax: e.activation(lb[:], ax[:, 2, :], AF.Ln), reads=[bax], writes=[bsm])
                s.op("dve", lambda e, ab=ab: e.tensor_tensor(gt[:], ab[:, 16:32], adt[:, 16:32], ALU.add),
                     reads=[bab, bcw, bsm], writes=[bsm])
                s.op("act", lambda e: e.activation(gt[:], gt[:], AF.Exp), reads=[bsm], writes=[bsm])
                s.op("act", lambda e: e.activation(gt[:], gt[:], AF.Ln, bias=cbias[:, 2:3], scale=1.0),
                     reads=[bsm, bcw], writes=[bsm])
                s.op("dve", lambda e: e.tensor_tensor(gt[:], gt[:], nega[:], ALU.mult), reads=[bsm, bcw], writes=[bsm])
                s.mm([(pa[:, 0:8], c2[:, 0, :], gt[:, 0:8], True, True),
                      (pa[:, 8:16], c2[:, 1, :], gt[:, 8:16], True, True),
                      (pa[:, 16:32], c2[:, 2, :], gt[:, 0:16], True, True)], reads=[bsm, bcw], writes=[bpa])
                s.mm([(pr[0:8, 0:128], gt[:, 0:8], c2[:, 0, :], True, True),
                      (pr[0:8, 128:256], gt[:, 8:16], c2[:, 1, :], True, True),
                      (pr[0:8, 256:384], gt[:, 0:8], c2[:, 0, :], True, False),
                      (pr[0:8, 256:384], lb[:, 0:8], C["ident"], False, True),
                      (pr[0:8, 384:512], gt[:, 8:16], c2[:, 1, :], True, False),
                      (pr[0:8, 384:512], lb[:, 8:16], C["ident"], False, True)],
                     reads=[bsm, bcw, C["bconst"]], writes=[bpr])
                s.op("dve", lambda e, ax=ax: e.tensor_copy(ax[:, 0, :], pa[:, 0:16]), reads=[bpa], writes=[bax])
                s.op("dve", lambda e: e.tensor_copy(cl[:], pa[:, 16:32]), reads=[bpa, bsm], writes=[bsm])
                s.op("dve", lambda e, ax=ax: e.tensor_tensor(ax[:, 1, :], ax[:, 0, :], lb[:], ALU.add),
                     reads=[bax, bsm], writes=[bax])
                s.op("act", lambda e, ax=ax: e.activation(ax[:, 3, :], ax[:, 1, :], AF.Exp), reads=[bax], writes=[bax])
                s.op("dve", lambda e, ax=ax: e.tensor_tensor(ax[:, 4, :], cl[:], ax[:, 0, :], ALU.subtract),
                     reads=[bax, bsm], writes=[bax])
                s.op("act", lambda e, ax=ax: e.activation(ax[:, 4, :], ax[:, 4, :], AF.Exp), reads=[bax], writes=[bax])
                s.op("act", lambda e, blt=blt: e.activation(blt[:], cl[:], AF.Exp), reads=[bsm], writes=[bblt])
                s.op("act", lambda e, rw=rw: e.activation(rw[:, 0:2, :, :].rearrange("p a b t -> p (a b t)"),
                                                         pr[0:8, :], AF.Copy), reads=[bpr], writes=[brw])
                s.op("act", lambda e, rw=rw: e.activation(rw[:, 2, :, :].rearrange("p b t -> p (b t)"),
                                                         pr[0:8, 0:256], AF.Exp), reads=[bpr], writes=[brw])
                s.dma("sp", k.AUXT[tb:tb + 128], ax[:], reads=[bax])
                s.dma("sp", k.BLT[tb:tb + 128], blt[:], reads=[bblt])
                for a in range(3):
                    s.dma("sp", k.AUXR[a, :, :, tb:tb + 128], rw[:, a, :, :], reads=[brw])
                yield

        items = []
        for ti, (t0, _, wch) in enumerate(tiles):
            for g4 in range(6):
                items.append(grp(ti, t0, wch, g4))
            items.append(tail(ti, t0, wch))
        active = []
        pos = 0
        WIN = 2
        while pos < len(items) or active:
            while len(active) < WIN and pos < len(items):
                active.append(items[pos])
                pos += 1
            for g in list(active):
                try:
                    next(g)
                except StopIteration:
                    active.remove(g)
        s.barrier()


class TB:
    def __init__(self, t):
        self.t = t
        self.b = Buf()


def pass3b_scan(k, C):
    s = k.s
    NTOK = k.NTOK
    NCH = NTOK // CH
    NCC = CTX // CH
    with ExitStack() as cx:
        def sbt(shape, dt, p="t"):
            return TB(k.sb(cx, shape, dt, p))

        msk = sbt([64, 4, 8, 64], F32, "msk")
        s.dma("sp", msk.t[:], k.masks, writes=[msk.b])
        idr = sbt([64, 8, 64], F32, "idr")
        s.dma("sp", idr.t[:], k.identrep, writes=[idr.b])
        S32 = [sbt([128, 8, 128], F32, "S32") for _ in range(2)]
        Sb = [sbt([128, 8, 128], BF16, "Sb") for _ in range(2)]
        for d in range(2):
            s.op("dve", lambda e, d=d: e.memset(S32[d].t[:], 0.0), writes=[S32[d].b])
            s.op("pool", lambda e, d=d: e.memset(Sb[d].t[:], 0.0), writes=[Sb[d].b])

        NSET = 2

        def mk():
            return dict(
                qT=sbt([128, 8, 64], BF16, "qT"), kT=sbt([128, 8, 64], BF16, "kT"),
                ktok=sbt([64, 8, 128], BF16, "ktok"), vtok=sbt([64, 8, 128], BF16, "vtok"),
                axt=sbt([64, 5, 16], F32, "axt"), R1=sbt([64, 8, 64], F32, "R1"), R2=sbt([64, 8, 64], F32, "R2"),
                EC=sbt([128, 8, 64], F32, "EC"), BL=sbt([128, 8], F32, "BL"),
                EA=sbt([64, 8, 64], F32, "EA"), EM=sbt([64, 8, 64], F32, "EM"), EL=sbt([64, 8, 64], F32, "EL"),
                aqk=sbt([64, 8, 64], BF16, "aqk"),
                Mp=[sbt([64, 8, 64], F32, "Mp") for _ in range(2)],
                Lp=[sbt([64, 8, 64], F32, "Lp") for _ in range(2)],
                P=[sbt([64, 8, 64], F32, "P") for _ in range(2)],
                TT=sbt([64, 8, 64], BF16, "TT"),
                bv=sbt([64, 8, 128], BF16, "bv"), kb=sbt([64, 8, 128], BF16, "kb"), kd=sbt([64, 8, 128], BF16, "kd"),
                u=sbt([64, 8, 128], F32, "u"), wT=sbt([128, 8, 64], BF16, "wT"), qd=sbt([128, 8, 64], BF16, "qd"),
                vn=sbt([64, 8, 128], BF16, "vn"), o=sbt([128, 8, 64], F32, "o"),
            )

        sets = [mk() for _ in range(NSET)]
        pA = TB(k.ps(cx, [128, 512], F32))
        pB = TB(k.ps(cx, [128, 512], F32))
        pC = TB(k.ps(cx, [128, 512], F32))
        pDE = TB(k.ps(cx, [128, 1024], F32))
        pF = TB(k.ps(cx, [128, 512], F32))
        pGH = TB(k.ps(cx, [128, 1024], F32))

        def v3(ap, n):
            return ap.rearrange("p (h n) -> p h n", h=8)

        def instance(n, ch, d, is_x):
            T = sets[n % NSET]
            tok0 = ch * CH
            d8 = slice(d * 8, (d + 1) * 8)
            s.dma("sp", T["qT"].t[:], k.QT[ch], writes=[T["qT"].b])
            s.dma("sp", T["kT"].t[:], k.KT[ch], writes=[T["kT"].b])
            s.dma("sp", T["ktok"].t[:], k.KTOK[tok0:tok0 + CH], writes=[T["ktok"].b])
            s.dma("sp", T["vtok"].t[:], k.VTOK[tok0:tok0 + CH], writes=[T["vtok"].b])
            s.dma("sp", T["axt"].t[:], k.AUXT[tok0:tok0 + CH], writes=[T["axt"].b])
            s.dma("sp", T["R1"].t[:], k.AUXR[0, :, d, tok0:tok0 + CH].partition_broadcast(64), writes=[T["R1"].b])
            s.dma("sp", T["R2"].t[:], k.AUXR[1, :, d, tok0:tok0 + CH].partition_broadcast(64), writes=[T["R2"].b])
            s.dma("sp", T["EC"].t[:], k.AUXR[2, :, d, tok0:tok0 + CH].partition_broadcast(128), writes=[T["EC"].b])
            s.dma("sp", T["BL"].t[:], k.BLT[tok0:tok0 + 1, d8].partition_broadcast(128), writes=[T["BL"].b])
            axt = T["axt"].t
            c_b = bc_last(axt[:, 0, d8], 64)
            cb_b = bc_last(axt[:, 1, d8], 64)
            m_incl = msk.t[:, 0 if d == 0 else 2, :, :]
            m_strT = msk.t[:, 1 if d == 0 else 3, :, :]
            m_strL = msk.t[:, 3 if d == 0 else 1, :, :]
            kT, qT = T["kT"], T["qT"]
            s.mm([(v3(pA.t[0:64, :], 64)[:, h, :], kT.t[:, h, :], kT.t[:, h, :], True, True) for h in range(8)],
                 reads=[kT.b], writes=[pA.b])
            s.mm([(v3(pB.t[0:64, :], 64)[:, h, :], kT.t[:, h, :], qT.t[:, h, :], True, True) for h in range(8)],
                 reads=[kT.b, qT.b], writes=[pB.b])
            G = v3(pA.t[0:64, :], 64)
            QK = v3(pB.t[0:64, :], 64)
            EA, EM, EL = T["EA"], T["EM"], T["EL"]
            s.op("dve", lambda e: e.tensor_tensor(EA.t[:], T["R1"].t[:], c_b, ALU.subtract),
                 reads=[T["R1"].b, T["axt"].b], writes=[EA.b])
            s.op("dve", lambda e: e.tensor_tensor(EM.t[:], T["R2"].t[:], c_b, ALU.subtract),
                 reads=[T["R2"].b, T["axt"].b], writes=[EM.b])
            s.op("dve", lambda e: e.scalar_tensor_tensor(EL.t[:], T["R1"].t[:], -1.0, cb_b, ALU.mult, ALU.add),
                 reads=[T["R1"].b, T["axt"].b], writes=[EL.b])
            for (E, mk_) in ((EA, m_incl), (EM, m_strT), (EL, m_strL)):
                s.op("pool", lambda e, E=E: e.tensor_scalar(E.t[:], E.t[:], 0.0, None, ALU.min),
                     reads=[E.b], writes=[E.b])
                s.op("pool", lambda e, E=E, mk_=mk_: e.tensor_tensor(E.t[:], E.t[:], mk_, ALU.add),
                     reads=[E.b, msk.b], writes=[E.b])
                s.op("act", lambda e, E=E: e.activation(E.t[:], E.t[:], AF.Exp), reads=[E.b], writes=[E.b])
            aqk = T["aqk"]
            s.op("dve", lambda e: e.tensor_tensor(aqk.t[:], QK, EA.t[:], ALU.mult), reads=[pB.b, EA.b], writes=[aqk.b])
            Mp, Lp, P = T["Mp"], T["Lp"], T["P"]
            s.op("dve", lambda e: e.tensor_tensor(Mp[0].t[:], G, EM.t[:], ALU.mult), reads=[pA.b, EM.b], writes=[Mp[0].b])
            s.op("dve", lambda e: e.tensor_tensor(Lp[0].t[:], G, EL.t[:], ALU.mult), reads=[pA.b, EL.b], writes=[Lp[0].b])
            s.op("dve", lambda e: e.tensor_tensor(P[0].t[:], idr.t[:], Mp[0].t[:], ALU.subtract),
                 reads=[idr.b, Mp[0].b], writes=[P[0].b])
            cur = 0
            pc = 0
            for lvl in range(1, 6):
                nxt = 1 - cur
                Lc, Mc, Ln, Mn = Lp[cur], Mp[cur], Lp[nxt], Mp[nxt]
                pa3, pb3, pc3 = v3(pA.t[0:64, :], 64), v3(pB.t[0:64, :], 64), v3(pC.t[0:64, :], 64)
                s.mm([(pb3[:, h, :], Mc.t[:, h, :], Lc.t[:, h, :], True, True) for h in range(8)],
                     reads=[Mc.b, Lc.b], writes=[pB.b])
                if lvl < 5:
                    s.mm([(pa3[:, h, :], Lc.t[:, h, :], Mc.t[:, h, :], True, True) for h in range(8)],
                         reads=[Mc.b, Lc.b], writes=[pA.b])
                s.op("act", lambda e, Ln=Ln: e.activation(Ln.t[:], pb3, AF.Copy), reads=[pB.b], writes=[Ln.b])
                if lvl < 5:
                    s.op("act", lambda e, Mn=Mn: e.activation(Mn.t[:], pa3, AF.Copy), reads=[pA.b], writes=[Mn.b])
                Pc, Pn = P[pc], P[1 - pc]
                s.mm([(pc3[:, h, :], Ln.t[:, h, :], Pc.t[:, h, :], True, True) for h in range(8)],
                     reads=[Ln.b, Pc.b], writes=[pC.b])
                if lvl < 5:
                    s.op("dve", lambda e, Pc=Pc, Pn=Pn: e.tensor_tensor(Pn.t[:], Pc.t[:], pc3, ALU.add),
                         reads=[Pc.b, pC.b], writes=[Pn.b])
                else:
                    s.op("dve", lambda e, Pc=Pc: e.tensor_tensor(T["TT"].t[:], Pc.t[:], pc3, ALU.add),
                         reads=[Pc.b, pC.b], writes=[T["TT"].b])
                cur = nxt
                pc = 1 - pc
            TT = T["TT"]
            bv, kb, kd, qd = T["bv"], T["kb"], T["kd"], T["qd"]
            s.op("pool", lambda e: e.tensor_tensor(bv.t[:], T["vtok"].t[:], bc_last(axt[:, 2, d8], 128), ALU.mult),
                 reads=[T["vtok"].b, T["axt"].b], writes=[bv.b])
            s.op("pool", lambda e: e.tensor_tensor(kb.t[:], T["ktok"].t[:], bc_last(axt[:, 3, d8], 128), ALU.mult),
                 reads=[T["ktok"].b, T["axt"].b], writes=[kb.b])
            s.op("pool", lambda e: e.tensor_tensor(kd.t[:], T["ktok"].t[:], bc_last(axt[:, 4, d8], 128), ALU.mult),
                 reads=[T["ktok"].b, T["axt"].b], writes=[kd.b])
            s.op("pool", lambda e: e.tensor_tensor(qd.t[:], qT.t[:], T["EC"].t[:], ALU.mult),
                 reads=[qT.b, T["EC"].b], writes=[qd.b])
            pu = pDE.t[0:64, :].rearrange("p (h n) -> p h n", h=8)
            pw = v3(pF.t[:, :], 64)
            s.mm([(pu[:, h, :], TT.t[:, h, :], bv.t[:, h, :], True, True) for h in range(8)],
                 reads=[TT.b, bv.b], writes=[pDE.b])
            s.mm([(pw[:, h, :], kb.t[:, h, :], TT.t[:, h, :], True, True) for h in range(8)],
                 reads=[TT.b, kb.b], writes=[pF.b])
            u, wT = T["u"], T["wT"]
            s.op("act", lambda e: e.activation(u.t[:], pu, AF.Copy), reads=[pDE.b], writes=[u.b])
            s.op("act", lambda e: e.activation(wT.t[:], pw, AF.Copy), reads=[pF.b], writes=[wT.b])
            Sd, S3 = Sb[d], S32[d]
            s.mm([(pu[:, h, :], wT.t[:, h, :], Sd.t[:, h, :], True, True) for h in range(8)],
                 reads=[wT.b, Sd.b], writes=[pDE.b])
            vn = T["vn"]
            s.op("dve", lambda e: e.tensor_tensor(vn.t[:], u.t[:], pu, ALU.subtract), reads=[u.b, pDE.b], writes=[vn.b])
            if is_x:
                mms = []
                for h in range(8):
                    mms.append((pw[:, h, :], Sd.t[:, h, :], qd.t[:, h, :], True, False))
                    mms.append((pw[:, h, :], vn.t[:, h, :], aqk.t[:, h, :], False, True))
                s.mm(mms, reads=[Sd.b, qd.b, vn.b, aqk.b], writes=[pF.b])
                o = T["o"]
                s.op("act", lambda e: e.activation(o.t[:], pw, AF.Copy), reads=[pF.b], writes=[o.b])
                s.dma("sp", k.OT[d, ch - NCC], o.t[:], reads=[o.b])
            pS = pGH.t[:, :].rearrange("p (h n) -> p h n", h=8)
            s.mm([(pS[:, h, :], kd.t[:, h, :], vn.t[:, h, :], True, True) for h in range(8)],
                 reads=[kd.b, vn.b], writes=[pGH.b])
            s.op("pool", lambda e: e.tensor_tensor(S3.t[:], S3.t[:], bc_last(T["BL"].t[:, :], 128), ALU.mult),
                 reads=[S3.b, T["BL"].b], writes=[S3.b])
            s.op("dve", lambda e: e.tensor_tensor(S3.t[:], S3.t[:], pS, ALU.add), reads=[S3.b, pGH.b], writes=[S3.b])
            s.op("act", lambda e: e.activation(Sd.t[:], S3.t[:], AF.Copy), reads=[S3.b], writes=[Sd.b])

        order_f = list(range(NCH))
        order_b = list(range(NCC - 1, -1, -1)) + list(range(NCH - 1, NCC - 1, -1))
        n = 0
        for st in range(NCH):
            instance(n, order_f[st], 0, order_f[st] >= NCC)
            n += 1
            instance(n, order_b[st], 1, order_b[st] >= NCC)
            n += 1
        s.barrier()


def pool_operators(SEQ):
    R = SEQ // GRID_W
    NB = SEQ // 128
    wins = (2, 4, 8, 16)
    mats = []
    index = {}
    cache = {}
    cc = np.arange(GRID_W)
    for g, w in enumerate(wins):
        clo = np.clip(cc - w // 2, 0, GRID_W)
        chi = np.clip(cc + w - w // 2, 0, GRID_W)
        cm = ((cc[:, None] >= clo[None, :]) & (cc[:, None] < chi[None, :])).astype(np.float64)
        car = (chi - clo).astype(np.float64)
        maxoff = (w // 2 + 1) // 2 + 1
        for b in range(NB):
            for off in range(-maxoff, maxoff + 1):
                bp = b + off
                if bp < 0 or bp >= NB:
                    continue
                M = np.zeros((128, 128), np.float64)
                for ro in range(2):
                    r = 2 * b + ro
                    rlo = max(r - w // 2, 0)
                    rhi = min(r + w - w // 2, R)
                    for ri in range(2):
                        rp = 2 * bp + ri
                        if rlo <= rp < rhi:
                            M[ri * 64:(ri + 1) * 64, ro * 64:(ro + 1) * 64] = cm / (car[None, :] * (rhi - rlo))
                if off == 0:
                    M -= np.eye(128)
                if not M.any():
                    continue
                key = (g, off, M.tobytes())
                if key not in cache:
                    cache[key] = len(mats)
                    mats.append(M.astype(np.float32))
                index[(g, b, off)] = cache[key]
    return np.stack(mats, 0), index


def pass4_merge(k, C):
    s = k.s
    SEQ = k.SEQ
    NB = SEQ // 128
    tab = C["tab"]
    with ExitStack() as cx:
        Wg = k.sb(cx, [128, 8, D], BF16, "wg")
        Wp = k.sb(cx, [128, 4, D], BF16, "wp")
        Wmo = k.sb(cx, [128, 8, D], BF16, "wmo")
        bW = load_weight_bf16(k, Wg, k.w_gdn_proj, 8)
        bW2 = load_weight_bf16(k, Wp, k.w_pool_proj, 4)
        bW3 = load_weight_bf16(k, Wmo, k.w_mix_out, 8)
        nmat = k.opm_n
        opm = k.sb(cx, [128, nmat, 128], BF16, "opm")
        poolw = k.sb(cx, [128, 4, 128], BF16, "poolw")
        bW4 = Buf()
        s.dma("pool", opm[:], k.opm.rearrange("n p t -> p n t"), writes=[bW4])
        s.dma("pool", poolw[:], k.poolw, writes=[bW4])
        small = k.sb(cx, [128, 8], F32, "small")
        s.dma("sp", small[:, 0:1], k.gnw, writes=[bW4])
        s.dma("sp", small[:, 1:5], k.pscale, writes=[bW4])
        cb2 = k.sb(cx, [128, 2], F32, "cb2")
        s.op("dve", lambda e: e.memset(cb2[:, 0:1], EPS), writes=[bW4])
        wts = [bW, bW2, bW3, bW4]

        def dbl(shape, dt, p):
            return [TB(k.sb(cx, shape, dt, p)) for _ in range(2)]

        x1T = dbl([128, 8, NT], F32, "x1T")
        of = [TB(k.sb(cx, [128, 4, 8, 64], F32, "of"))] * 2
        ob = [TB(k.sb(cx, [128, 4, 8, 64], F32, "ob"))] * 2
        sgT = dbl([128, 8, NT], F32, "sgT")
        gts = dbl([128, 16, NT], F32, "gts")
        ppt = dbl([128, 10, 512], BF16, "ppt")
        o = TB(k.sb(cx, [128, 8, NT], F32, "o"))
        sq = TB(k.sb(cx, [128, 8, NT], F32, "sq"))
        rs = sq
        ogs = dbl([128, 8, NT], BF16, "og")
        pd = TB(k.sb(cx, [128, 4, NT], BF16, "pd"))
        yp1s = dbl([128, 4, NT], BF16, "yp1")
        t1 = dbl([128, NT], F32, "t1")
        t2 = dbl([128, NT], F32, "t2")
        msT = TB(k.sb(cx, [128, 8, NT], BF16, "msT"))
        pn = TB(k.ps(cx, [128, 4, NT], F32))
        pgp = [TB(k.ps(cx, [128, 2, NT], F32)) for _ in range(2)]
        ppd = TB(k.ps(cx, [128, 4, NT], F32))
        pmx = [TB(k.ps(cx, [128, 512], F32)) for _ in range(2)]
        def prologue(ti):
            tx0 = ti * NT
            cx0 = tx0 // CH
            b0 = tx0 // 128
            X, OF, OB, SG, GT, PP = x1T[ti % 2], of[ti % 2], ob[ti % 2], sgT[ti % 2], gts[ti % 2], ppt[ti % 2]
            og, yp1 = ogs[ti % 2], yp1s[ti % 2]
            s.dma("sp", X.t[:], k.X1T[:, CTX + tx0:CTX + tx0 + NT].rearrange("(kc p) t -> p kc t", p=128), writes=[X.b])
            s.dma("sp", OF.t[:].rearrange("p c h t -> p c (h t)"),
                  k.OT[0, cx0:cx0 + 4].rearrange("c d h t -> d c (h t)"), writes=[OF.b])
            s.dma("sp", OB.t[:].rearrange("p c h t -> p c (h t)"),
                  k.OT[1, cx0:cx0 + 4].rearrange("c d h t -> d c (h t)"), writes=[OB.b])
            s.dma("sp", SG.t[:], k.SGATE[:, tx0:tx0 + NT].rearrange("(kc p) t -> p kc t", p=128), writes=[SG.b])
            s.dma("sp", GT.t[:], k.GATES[:, tx0:tx0 + NT].rearrange("(kc p) t -> p kc t", p=128), writes=[GT.b])
            blo = max(b0 - 4, 0)
            bhi = min(b0 + 6, NB)
            s.dma("sp", PP.t[:, 0:bhi - blo, :], k.PPT[blo * 128:bhi * 128, :].rearrange("(b p) c -> p b c", p=128),
                  writes=[PP.b])
            o4 = o.t[:].rearrange("p h (c t) -> p h c t", c=4)
            s.op("pool", lambda e, OF=OF, OB=OB: e.tensor_tensor(
                o4, OF.t[:].rearrange("p c h t -> p h c t"), OB.t[:].rearrange("p c h t -> p h c t"), ALU.add),
                reads=[OF.b, OB.b], writes=[o.b])
            yield
            s.op("act", lambda e: e.activation(sq.t[:], o.t[:], AF.Square), reads=[o.b], writes=[sq.b])
            yield
            for hh in range(2):
                s.mm([(pn.t[:, i, :], C["ones"], sq.t[:, hh * 4 + i, :], True, True) for i in range(4)],
                     reads=[sq.b, C["bconst"]], writes=[pn.b])
                yield
                s.op("act", lambda e, hh=hh: e.activation(rs.t[:, hh * 4:hh * 4 + 4, :], pn.t[:], AF.Ln,
                                                          bias=cb2[:, 0:1], scale=1.0 / HD),
                     reads=[pn.b, bW4], writes=[rs.b])
                yield
            s.op("act", lambda e: e.activation(rs.t[:], rs.t[:], AF.Exp, scale=-0.5), reads=[rs.b], writes=[rs.b])
            yield
            s.op("dve", lambda e: e.tensor_tensor(o.t[:], o.t[:], rs.t[:], ALU.mult), reads=[o.b, rs.b], writes=[o.b])
            yield
            s.op("dve", lambda e, SG=SG: e.scalar_tensor_tensor(
                og.t[:].rearrange("p h t -> p (h t)"), o.t[:].rearrange("p h t -> p (h t)"), small[:, 0:1],
                SG.t[:].rearrange("p h t -> p (h t)"), ALU.mult, ALU.mult), reads=[o.b, SG.b, bW4], writes=[og.b])
            mms = []
            for g in range(4):
                for obk in range(2):
                    b = b0 + obk
                    offs = [off for off in range(-5, 6) if (g, b, off) in k.opm_index]
                    for j, off in enumerate(offs):
                        mms.append((ppd.t[:, g, obk * 128:(obk + 1) * 128],
                                    PP.t[:, b + off - blo, g * 128:(g + 1) * 128],
                                    opm[:, k.opm_index[(g, b, off)], :], j == 0, j == len(offs) - 1))
            yield
            s.mm(mms, reads=[PP.b, bW4], writes=[ppd.b])
            yield
            s.op("act", lambda e: e.activation(pd.t[:], ppd.t[:], AF.Copy), reads=[ppd.b], writes=[pd.b])
            yield
            s.mm([(ppd.t[:, g, :], poolw[:, g, :], pd.t[:, g, :], True, True) for g in range(4)],
                 reads=[pd.b, bW4], writes=[ppd.b])
            yield
            s.op("dve", lambda e: e.tensor_tensor(yp1.t[:], ppd.t[:], bc_last(small[:, 1:5], NT), ALU.mult),
                 reads=[ppd.b, bW4], writes=[yp1.b])

        def main(ti):
            tx0 = ti * NT
            X, GT = x1T[ti % 2], gts[ti % 2]
            og, yp1 = ogs[ti % 2], yp1s[ti % 2]
            for dc in range(8):
                pg = pgp[dc % 2]
                mms = [(pg.t[:, 0, :], Wg[:, h, dc * 128:(dc + 1) * 128], og.t[:, h, :], h == 0, h == 7)
                       for h in range(8)]
                mms += [(pg.t[:, 1, :], Wp[:, g, dc * 128:(dc + 1) * 128], yp1.t[:, g, :], g == 0, g == 3)
                        for g in range(4)]
                s.mm(mms, reads=[og.b, yp1.b] + wts, writes=[pg.b])
                a1, a2 = t1[dc % 2], t2[dc % 2]
                s.op("dve", lambda e, pg=pg, a1=a1, GT=GT, dc=dc: e.tensor_tensor(
                    a1.t[:], pg.t[:, 0, :], GT.t[:, 8 + dc, :], ALU.mult), reads=[pg.b, GT.b], writes=[a1.b])
                s.op("dve", lambda e, pg=pg, a2=a2, GT=GT, dc=dc: e.tensor_tensor(
                    a2.t[:], pg.t[:, 1, :], GT.t[:, dc, :], ALU.mult), reads=[pg.b, GT.b], writes=[a2.b])
                s.op("pool", lambda e, a1=a1, a2=a2, dc=dc: e.tensor_tensor(msT.t[:, dc, :], a1.t[:], a2.t[:], ALU.add),
                     reads=[a1.b, a2.b], writes=[msT.b])
                yield
            for dc in range(8):
                pm = pmx[dc % 2]
                s.mm([(pm.t[:, 0:NT], Wmo[:, kc, dc * 128:(dc + 1) * 128], msT.t[:, kc, :], kc == 0, kc == 7)
                      for kc in range(8)], reads=[msT.b] + wts, writes=[pm.b])
                s.op("dve", lambda e, pm=pm, X=X, dc=dc: e.scalar_tensor_tensor(
                    X.t[:, dc, :], pm.t[:, 0:NT], tab[:, 0, 5, dc:dc + 1], X.t[:, dc, :], ALU.mult, ALU.add),
                    reads=[pm.b, X.b, C["btab"]], writes=[X.b])
                yield
            s.dma("sp", k.X2T[:, tx0:tx0 + NT].rearrange("(kc p) t -> p kc t", p=128), X.t[:], reads=[X.b])

        def run_all(gs):
            gs = list(gs)
            while gs:
                for g in list(gs):
                    try:
                        next(g)
                    except StopIteration:
                        gs.remove(g)

        ntl = SEQ // NT
        run_all([prologue(0)])
        for ti in range(ntl):
            gs = [main(ti)]
            if ti + 1 < ntl:
                gs.append(prologue(ti + 1))
            run_all(gs)
        s.barrier()


def pass3b_scan2(k, C):
    s = k.s
    NTOK = k.NTOK
    NCH = NTOK // CH
    NCC = CTX // CH
    with ExitStack() as cx:
        def sbt(shape, dt, p="t"):
            return TB(k.sb(cx, shape, dt, p))

        msk = sbt([128, 3, 8, 64], F32, "msk")
        s.dma("sp", msk.t[:], k.masks2, writes=[msk.b])
        idr = sbt([128, 8, 64], F32, "idr")
        s.dma("sp", idr.t[:], k.identrep2, writes=[idr.b])
        S32 = [sbt([128, 8, 128], F32, "S32") for _ in range(2)]
        Sb = [sbt([128, 8, 128], BF16, "Sb") for _ in range(2)]
        for d in range(2):
            s.op("dve", lambda e, d=d: e.memset(S32[d].t[:], 0.0), writes=[S32[d].b])
            s.op("pool", lambda e, d=d: e.memset(Sb[d].t[:], 0.0), writes=[Sb[d].b])

        def mk_loaded():
            return dict(
                qT=[sbt([128, 8, 64], BF16, "qT") for _ in range(2)], kT=[sbt([128, 8, 64], BF16, "kT") for _ in range(2)],
                EC=[sbt([128, 8, 64], F32, "EC") for _ in range(2)], BL=[sbt([128, 8], F32, "BL") for _ in range(2)],
                ktok=sbt([128, 8, 128], BF16, "ktok"), vtok=sbt([128, 8, 128], BF16, "vtok"),
                axt=sbt([128, 5, 8], F32, "axt"), R1=sbt([128, 8, 64], F32, "R1"), R2=sbt([128, 8, 64], F32, "R2"))

        def mk_comp():
            return dict(
                EA=sbt([128, 8, 64], F32, "EA"), EM=sbt([128, 8, 64], F32, "EM"), EL=sbt([128, 8, 64], F32, "EL"),
                aqk=sbt([128, 8, 64], BF16, "aqk"),
                Mp=[sbt([128, 8, 64], F32, "Mp") for _ in range(2)],
                Lp=[sbt([128, 8, 64], F32, "Lp") for _ in range(2)],
                P=[sbt([128, 8, 64], F32, "P") for _ in range(2)],
                TT=sbt([128, 8, 64], BF16, "TT"),
                bv=sbt([128, 8, 128], BF16, "bv"), kb=sbt([128, 8, 128], BF16, "kb"), kd=sbt([128, 8, 128], BF16, "kd"),
                u=sbt([128, 8, 128], F32, "u"), vn=sbt([128, 8, 128], BF16, "vn"),
                wT=[sbt([128, 8, 64], BF16, "wT") for _ in range(2)], qd=[sbt([128, 8, 64], BF16, "qd") for _ in range(2)],
                o=[sbt([128, 8, 64], F32, "o") for _ in range(2)])

        LS = [mk_loaded() for _ in range(3)]
        CS = [mk_comp() for _ in range(2)]
        pAB = TB(k.ps(cx, [128, 1024], F32))
        pC = TB(k.ps(cx, [128, 512], F32))
        pWS = TB(k.ps(cx, [128, 1024], F32))
        pO = TB(k.ps(cx, [128, 512], F32))
        pS2 = TB(k.ps(cx, [128, 1024], F32))

        def v64(ap):
            return ap.rearrange("p (h n) -> p h n", h=8)

        pa3 = v64(pAB.t[:, 0:512])
        pb3 = v64(pAB.t[:, 512:1024])
        pc3 = v64(pC.t[:, :])
        pu3 = v64(pAB.t[:, :])
        pws3 = v64(pWS.t[:, :])
        po3 = v64(pO.t[:, :])
        ps3 = v64(pS2.t[:, :])
        HF = [slice(0, 64), slice(64, 128)]
        order = [list(range(NCH)), list(range(NCC - 1, -1, -1)) + list(range(NCH - 1, NCC - 1, -1))]

        def loads(st):
            L = LS[st % 3]
            for d in range(2):
                ch = order[d][st]
                tok0 = ch * CH
                d8 = slice(d * 8, (d + 1) * 8)
                hf = HF[d]
                s.dma("sp", L["qT"][d].t[:], k.QT[ch], writes=[L["qT"][d].b])
                s.dma("sp", L["kT"][d].t[:], k.KT[ch], writes=[L["kT"][d].b])
                s.dma("sp", L["ktok"].t[hf], k.KTOK[tok0:tok0 + CH], writes=[L["ktok"].b])
                s.dma("sp", L["vtok"].t[hf], k.VTOK[tok0:tok0 + CH], writes=[L["vtok"].b])
                s.dma("sp", L["axt"].t[hf], k.AUXT[tok0:tok0 + CH, :, d8], writes=[L["axt"].b])
                s.dma("sp", L["R1"].t[hf], k.AUXR[0, :, d, tok0:tok0 + CH].partition_broadcast(64), writes=[L["R1"].b])
                s.dma("sp", L["R2"].t[hf], k.AUXR[1, :, d, tok0:tok0 + CH].partition_broadcast(64), writes=[L["R2"].b])
                s.dma("sp", L["EC"][d].t[:], k.AUXR[2, :, d, tok0:tok0 + CH].partition_broadcast(128),
                      writes=[L["EC"][d].b])
                s.dma("sp", L["BL"][d].t[:], k.BLT[tok0:tok0 + 1, d8].partition_broadcast(128), writes=[L["BL"][d].b])

        def prep(st):
            L = LS[st % 3]
            T = CS[st % 2]
            axt = L["axt"].t
            c_b = bc_last(axt[:, 0, :], 64)
            cb_b = bc_last(axt[:, 1, :], 64)
            kT, qT = L["kT"], L["qT"]
            s.mm([(pa3[HF[d], h, :], kT[d].t[:, h, :], kT[d].t[:, h, :], True, True) for h in range(8) for d in range(2)],
                 reads=[kT[0].b, kT[1].b], writes=[pAB.b])
            s.mm([(pb3[HF[d], h, :], kT[d].t[:, h, :], qT[d].t[:, h, :], True, True) for h in range(8) for d in range(2)],
                 reads=[kT[0].b, kT[1].b, qT[0].b, qT[1].b], writes=[pAB.b])
            EA, EM, EL = T["EA"], T["EM"], T["EL"]
            s.op("dve", lambda e: e.tensor_tensor(EA.t[:], L["R1"].t[:], c_b, ALU.subtract),
                 reads=[L["R1"].b, L["axt"].b], writes=[EA.b])
            s.op("dve", lambda e: e.tensor_tensor(EM.t[:], L["R2"].t[:], c_b, ALU.subtract),
                 reads=[L["R2"].b, L["axt"].b], writes=[EM.b])
            s.op("dve", lambda e: e.scalar_tensor_tensor(EL.t[:], L["R1"].t[:], -1.0, cb_b, ALU.mult, ALU.add),
                 reads=[L["R1"].b, L["axt"].b], writes=[EL.b])
            yield
            for (E, mi) in ((EA, 0), (EM, 1), (EL, 2)):
                s.op("dve", lambda e, E=E, mi=mi: e.scalar_tensor_tensor(E.t[:], E.t[:], 0.0, msk.t[:, mi, :, :],
                                                                       ALU.min, ALU.add),
                     reads=[E.b, msk.b], writes=[E.b])
            yield
            for E in (EA, EM, EL):
                s.op("act", lambda e, E=E: e.activation(E.t[:], E.t[:], AF.Exp), reads=[E.b], writes=[E.b])
            yield
            aqk = T["aqk"]
            Mp, Lp, P = T["Mp"], T["Lp"], T["P"]
            s.op("dve", lambda e: e.tensor_tensor(Mp[0].t[:], pa3, EM.t[:], ALU.mult), reads=[pAB.b, EM.b], writes=[Mp[0].b])
            s.op("dve", lambda e: e.tensor_tensor(Lp[0].t[:], pa3, EL.t[:], ALU.mult), reads=[pAB.b, EL.b], writes=[Lp[0].b])
            s.op("dve", lambda e: e.tensor_tensor(aqk.t[:], pb3, EA.t[:], ALU.mult), reads=[pAB.b, EA.b], writes=[aqk.b])
            s.op("dve", lambda e: e.tensor_tensor(P[0].t[:], idr.t[:], Mp[0].t[:], ALU.subtract),
                 reads=[idr.b, Mp[0].b], writes=[P[0].b])
            bv, kb, kd, qd = T["bv"], T["kb"], T["kd"], T["qd"]
            s.op("pool", lambda e: e.tensor_tensor(bv.t[:], L["vtok"].t[:], bc_last(axt[:, 2, :], 128), ALU.mult),
                 reads=[L["vtok"].b, L["axt"].b], writes=[bv.b])
            s.op("pool", lambda e: e.tensor_tensor(kb.t[:], L["ktok"].t[:], bc_last(axt[:, 3, :], 128), ALU.mult),
                 reads=[L["ktok"].b, L["axt"].b], writes=[kb.b])
            yield
            cur = 0
            pcx = 0
            for lvl in range(1, 6):
                nxt = 1 - cur
                Lc, Mc, Ln, Mn = Lp[cur], Mp[cur], Lp[nxt], Mp[nxt]
                s.mm([(pb3[HF[d], h, :], Mc.t[HF[d], h, :], Lc.t[HF[d], h, :], True, True)
                      for h in range(8) for d in range(2)], reads=[Mc.b, Lc.b], writes=[pAB.b])
                if lvl < 5:
                    s.mm([(pa3[HF[d], h, :], Lc.t[HF[d], h, :], Mc.t[HF[d], h, :], True, True)
                          for h in range(8) for d in range(2)], reads=[Mc.b, Lc.b], writes=[pAB.b])
                yield
                s.op("act", lambda e, Ln=Ln: e.activation(Ln.t[:], pb3, AF.Copy), reads=[pAB.b], writes=[Ln.b])
                if lvl < 5:
                    s.op("act", lambda e, Mn=Mn: e.activation(Mn.t[:], pa3, AF.Copy), reads=[pAB.b], writes=[Mn.b])
                if lvl == 1:
                    s.op("pool", lambda e: e.tensor_tensor(kd.t[:], L["ktok"].t[:], bc_last(axt[:, 4, :], 128), ALU.mult),
                         reads=[L["ktok"].b, L["axt"].b], writes=[kd.b])
                if lvl == 2:
                    for d in range(2):
                        s.op("pool", lambda e, d=d: e.tensor_tensor(qd[d].t[:], qT[d].t[:], L["EC"][d].t[:], ALU.mult),
                             reads=[qT[d].b, L["EC"][d].b], writes=[qd[d].b])
                yield
                Pc, Pn = P[pcx], P[1 - pcx]
                s.mm([(pc3[HF[d], h, :], Ln.t[HF[d], h, :], Pc.t[HF[d], h, :], True, True)
                      for h in range(8) for d in range(2)], reads=[Ln.b, Pc.b], writes=[pC.b])
                yield
                if lvl < 5:
                    s.op("dve", lambda e, Pc=Pc, Pn=Pn: e.tensor_tensor(Pn.t[:], Pc.t[:], pc3, ALU.add),
                         reads=[Pc.b, pC.b], writes=[Pn.b])
                else:
                    s.op("dve", lambda e, Pc=Pc: e.tensor_tensor(T["TT"].t[:], Pc.t[:], pc3, ALU.add),
                         reads=[Pc.b, pC.b], writes=[T["TT"].b])
                cur = nxt
                pcx = 1 - pcx
            yield
            TT = T["TT"]
            s.mm([(pu3[HF[d], h, :], TT.t[HF[d], h, :], bv.t[HF[d], h, :], True, True)
                  for h in range(8) for d in range(2)], reads=[TT.b, bv.b], writes=[pAB.b])
            yield
            s.op("act", lambda e: e.activation(T["u"].t[:], pu3, AF.Copy), reads=[pAB.b], writes=[T["u"].b])
            for d in range(2):
                s.mm([(pc3[:, h, :], kb.t[HF[d], h, :], TT.t[HF[d], h, :], True, True) for h in range(8)],
                     reads=[TT.b, kb.b], writes=[pC.b])
                yield
                s.op("act", lambda e, d=d: e.activation(T["wT"][d].t[:], pc3, AF.Copy), reads=[pC.b], writes=[T["wT"][d].b])
                yield

        def seq(st):
            L = LS[st % 3]
            T = CS[st % 2]
            is_x = order[0][st] >= NCC
            wT, qd, vn, u, kd, aqk = T["wT"], T["qd"], T["vn"], T["u"], T["kd"], T["aqk"]
            s.mm([(pws3[HF[d], h, :], wT[d].t[:, h, :], Sb[d].t[:, h, :], True, True) for h in range(8) for d in range(2)],
                 reads=[wT[0].b, wT[1].b, Sb[0].b, Sb[1].b], writes=[pWS.b])
            yield
            s.op("dve", lambda e: e.tensor_tensor(vn.t[:], u.t[:], pws3, ALU.subtract), reads=[u.b, pWS.b], writes=[vn.b])
            yield
            for d in range(2):
                ch = order[d][st]
                if is_x:
                    mms = []
                    for h in range(8):
                        mms.append((po3[:, h, :], Sb[d].t[:, h, :], qd[d].t[:, h, :], True, False))
                        mms.append((po3[:, h, :], vn.t[HF[d], h, :], aqk.t[HF[d], h, :], False, True))
                    s.mm(mms, reads=[Sb[d].b, qd[d].b, vn.b, aqk.b], writes=[pO.b])
                s.mm([(ps3[:, h, :], kd.t[HF[d], h, :], vn.t[HF[d], h, :], True, True) for h in range(8)],
                     reads=[kd.b, vn.b], writes=[pS2.b])
                s.op("pool", lambda e, d=d: e.tensor_tensor(S32[d].t[:], S32[d].t[:], bc_last(L["BL"][d].t[:, :], 128),
                                                            ALU.mult), reads=[S32[d].b, L["BL"][d].b], writes=[S32[d].b])
                yield
                if is_x:
                    s.op("act", lambda e, d=d: e.activation(T["o"][d].t[:], po3, AF.Copy), reads=[pO.b], writes=[T["o"][d].b])
                    s.dma("sp", k.OT[d, ch - NCC], T["o"][d].t[:], reads=[T["o"][d].b])
                s.op("dve", lambda e, d=d: e.tensor_tensor(S32[d].t[:], S32[d].t[:], ps3, ALU.add),
                     reads=[S32[d].b, pS2.b], writes=[S32[d].b])
                yield
                s.op("act", lambda e, d=d: e.activation(Sb[d].t[:], S32[d].t[:], AF.Copy), reads=[S32[d].b], writes=[Sb[d].b])
                yield

        loads(0)
        for st in range(NCH + 1):
            if st + 1 < NCH:
                loads(st + 1)
            gens = []
            if st < NCH:
                gens.append(prep(st))
            if st >= 1:
                gens.append(seq(st - 1))
            while gens:
                for g in list(gens):
                    try:
                        next(g)
                    except StopIteration:
                        gens.remove(g)
        s.barrier()


def build(SEQ=8192, debug=(), upto=99):
    nc = bass.Bass("TRN2", target_bir_lowering=False)
    es = ExitStack()
    k = K(nc, es, SEQ, debug)
    s = k.s
    NTOK = k.NTOK
    k.xt = k.din("xt", [D, NTOK])
    k.cvec = k.din("cvec", [128, 8, 2])
    k.w_ada = k.din("w_ada", [D, NMOD * D])
    k.b_ada = k.din("b_ada", [128, 72])
    k.nw = k.din("nw", [128, 4, 8])
    k.ffn1_w_in = k.din("ffn1_w_in", [D, 2 * DFF])
    k.ffn1_w_out = k.din("ffn1_w_out", [DFF, D])
    k.ffn2_w_in = k.din("ffn2_w_in", [D, 2 * DFF])
    k.ffn2_w_out = k.din("ffn2_w_out", [DFF, D])
    k.cst = k.din("cst", [128, 2, 128])
    k.w_mix_in = k.din("w_mix_in", [D, MIX_IN])
    k.convw = k.din("convw", [128, 24, 5])
    k.cst2 = k.din("cst2", [128, 3, 128])
    k.adt = k.din("adt", [1, 32])
    k.masks2 = k.din("masks2", [128, 3, 8, 64])
    k.identrep2 = k.din("identrep2", [128, 8, 64])
    mats, k.opm_index = pool_operators(SEQ)
    k.opm_n = mats.shape[0]
    k.opm = k.din("opm", [k.opm_n, 128, 128])
    k.poolw = k.din("poolw", [128, 4, 128])
    k.gnw = k.din("gnw", [128, 1])
    k.pscale = k.din("pscale", [128, 4])
    k.w_gdn_proj = k.din("w_gdn_proj", [D, D])
    k.w_pool_proj = k.din("w_pool_proj", [512, D])
    k.w_mix_out = k.din("w_mix_out", [D, D])
    k.X1T = k.dscr("X1T", [D, NTOK])
    k.PQKV = k.dscr("PQKV", [QKV, NTOK])
    k.ABT = k.dscr("ABT", [NTOK, 32])
    k.SGATE = k.dscr("SGATE", [D, SEQ])
    k.PPT = k.dscr("PPT", [SEQ, 512], BF16)
    k.X2T = k.dscr("X2T", [D, SEQ])
    k.GATES = k.dscr("GATES", [2 * D, SEQ])
    NCH = NTOK // CH
    k.QT = k.dscr("QT", [NCH, 128, 8, CH], BF16)
    k.KT = k.dscr("KT", [NCH, 128, 8, CH], BF16)
    k.KTOK = k.dscr("KTOK", [NTOK, 8, 128], BF16)
    k.VTOK = k.dscr("VTOK", [NTOK, 8, 128], BF16)
    k.AUXT = k.dscr("AUXT", [NTOK, 5, 16])
    k.BLT = k.dscr("BLT", [NTOK, 16])
    k.AUXR = k.dscr("AUXR", [3, 8, 2, NTOK])
    k.OT = k.dscr("OT", [2, SEQ // CH, 128, 8, CH])
    k.outT = nc.dram_tensor("outT", [D, SEQ], F32, kind="ExternalOutput").ap()
    C = {}
    cst = k.sb(es, [128, 2, 128], F32, "cst")
    C["bconst"] = Buf()
    s.dma("sp", cst[:], k.cst, writes=[C["bconst"]])
    C["ones"] = cst[:, 0, :]
    C["ident"] = cst[:, 1, :]
    C["eps"] = k.sb(es, [128, 2], F32, "eps")
    s.op("dve", lambda e: e.memset(C["eps"][:], EPS), writes=[C["bconst"]])
    C["tab"] = k.sb(es, [128, 2, 9, 8], F32, "tab")
    C["fnw"] = k.sb(es, [128, 8], F32, "fnw")
    C["btab"] = Buf()

    x_tiles = [(0, 0, 1)] + [(CTX + i * NT, CTX + i * NT, 0) for i in range(SEQ // NT)]
    pass0_mod(k, C)
    if upto >= 1:
        ffn_pass(k, C, k.xt, k.X1T, k.ffn1_w_in, k.ffn1_w_out, 0, x_tiles)
    if upto >= 2:
        pass2_mixin(k, C, x_tiles)
    if upto >= 3:
        pass3a_gdnprep(k, C, x_tiles)
    if upto >= 4:
        pass3b_scan2(k, C)
    if upto >= 5:
        pass4_merge(k, C)
    if upto >= 6:
        tiles5 = [(i * NT, i * NT, 0) for i in range(SEQ // NT)]
        ffn_pass(k, C, k.X2T, None, k.ffn2_w_in, k.ffn2_w_out, 2, tiles5, final_out=k.outT)
    s.barrier()
    es.close()
    return nc, k


def host_inputs(inp, b, SEQ):
    f = np.float32
    x = np.asarray(inp["x"], f)[b, :SEQ]
    ctx = np.asarray(inp["ctx"], f)[b]
    m = {}
    m["xt"] = np.ascontiguousarray(np.concatenate([ctx.T, x.T], axis=1))
    cv = np.stack([np.asarray(inp["c"], f)[b], np.asarray(inp["c_ctx"], f)], axis=-1)
    m["cvec"] = np.ascontiguousarray(cv.reshape(8, 128, 2).transpose(1, 0, 2))
    m["w_ada"] = np.ascontiguousarray(np.asarray(inp["w_ada"], f)[0])
    m["b_ada"] = np.ascontiguousarray(np.asarray(inp["b_ada"], f)[0].reshape(72, 128).T)
    nws = np.stack([np.asarray(inp["norm1_w"], f)[0], np.asarray(inp["norm2_w"], f)[0],
                    np.asarray(inp["norm3_w"], f)[0], np.asarray(inp["final_norm_w"], f)], axis=0)
    m["nw"] = np.ascontiguousarray(nws.reshape(4, 8, 128).transpose(2, 0, 1))
    m["opm"] = pool_operators(SEQ)[0]
    m["poolw"] = np.ascontiguousarray(np.asarray(inp["pool_w"], f)[0].transpose(1, 0, 2))
    m["gnw"] = np.asarray(inp["gdn_norm_w"], f)[0].reshape(128, 1).copy()
    m["pscale"] = np.ascontiguousarray(np.asarray(inp["pool_scale"], f)[0].reshape(4, 128).T)
    for nm in ["ffn1_w_in", "ffn1_w_out", "ffn2_w_in", "ffn2_w_out", "w_mix_in", "w_gdn_proj", "w_pool_proj",
               "w_mix_out"]:
        m[nm] = np.ascontiguousarray(np.asarray(inp[nm], f)[0])
    cw = np.asarray(inp["conv_w"], f)[0]
    m["convw"] = np.ascontiguousarray(cw.reshape(5, 24, 128).transpose(2, 1, 0))
    m["adt"] = np.concatenate([np.asarray(inp["a_log"], f)[0].reshape(-1),
                               np.asarray(inp["dt_bias"], f)[0].reshape(-1)])[None, :].copy()
    jj = np.arange(128)
    same = (jj[:, None] // 64) == (jj[None, :] // 64)
    c2 = np.zeros((128, 3, 128), f)
    c2[:, 0, :] = same & (jj[:, None] <= jj[None, :])
    c2[:, 1, :] = same & (jj[:, None] >= jj[None, :])
    c2[:, 2, :] = same
    m["cst2"] = c2
    pp = np.arange(64)[:, None]
    ff = np.arange(64)[None, :]
    mk = np.stack([ff >= pp, ff > pp, ff <= pp, ff < pp], 0)
    mk = np.where(mk, 0.0, NEG).astype(f)
    m2 = np.concatenate([mk[[0, 1, 3]].transpose(1, 0, 2), mk[[2, 3, 1]].transpose(1, 0, 2)], axis=0)
    m["masks2"] = np.ascontiguousarray(np.broadcast_to(m2[:, :, None, :], (128, 3, 8, 64)))
    e2 = np.concatenate([np.eye(64, dtype=f), np.eye(64, dtype=f)], axis=0)
    m["identrep2"] = np.ascontiguousarray(np.broadcast_to(e2[:, None, :], (128, 8, 64)))
    cst = np.zeros((128, 2, 128), f)
    cst[:, 0, :] = 1.0
    cst[:, 1, :] = np.eye(128, dtype=f)
    m["cst"] = cst
    return m


def kernel(**inputs):
    SEQ = int(np.asarray(inputs["x"]).shape[1])
    B = int(np.asarray(inputs["x"]).shape[0])
    nc, k = build(SEQ)
    shared = None
    in_maps = []
    for b in range(B):
        m = host_inputs(inputs, b, SEQ)
        if shared is None:
            shared = {n: v for n, v in m.items() if n not in ("xt", "cvec")}
        else:
            for n in shared:
                m[n] = shared[n]
        in_maps.append({n: v for n, v in m.items() if n in k.dram_in})
    res = run_bass_kernel_spmd(nc, in_maps, core_ids=list(range(B)))
    out = np.stack([np.asarray(res.results[b]["outT"]).T for b in range(B)], axis=0)
    return np.ascontiguousarray(out.astype(np.float32))
```
